# Optimizing a Trainium2 kernel written in Bass

```python
import math
import jax, jax.numpy as jnp
from jax import lax
import numpy as np

D_MODEL = 1024
BATCH = 16
SEQ = 2048
DEPTH = 4

N_MEM = 256
EPS = 1e-6
D_FF = 2816
SSD_HEADS = 16
SSD_HEAD_DIM = 64
SSD_INNER = SSD_HEADS * SSD_HEAD_DIM
SSD_GROUPS = 2
SSD_HPG = SSD_HEADS // SSD_GROUPS
SSD_STATE = 128
SSD_CONV = 5
SSD_CHUNK = 128
SSD_CONV_CH = SSD_INNER + 2 * SSD_GROUPS * SSD_STATE
MLA_HEADS = 8
MLA_Q_RANK = 512
MLA_KV_RANK = 256
MLA_NOPE = 64
MLA_ROPE = 32
MLA_V = 64
MLA_Q_BLOCK = 128
ROPE_THETA = 10000.0
IN_SPLITS = (
    SSD_INNER,
    SSD_INNER + SSD_CONV_CH,
    SSD_INNER + SSD_CONV_CH + 2 * SSD_HEADS,
    SSD_INNER + SSD_CONV_CH + 2 * SSD_HEADS + MLA_Q_RANK,
    SSD_INNER + SSD_CONV_CH + 2 * SSD_HEADS + MLA_Q_RANK + MLA_KV_RANK,
)
IN_COLS = IN_SPLITS[-1] + MLA_ROPE
MIX_WIDTH = SSD_INNER + MLA_HEADS * MLA_V
FNET_GROUPS = 4
FNET_GROUP_CH = D_MODEL // FNET_GROUPS
XA_HEADS = 4
XA_HEAD_DIM = D_MODEL // XA_HEADS
N_EVEN = (DEPTH + 1) // 2
N_ODD = DEPTH // 2

kernel_name = "hybrid_ssd_mla_fnet_macaron_encoder"


def rmsnorm(x, w):
    xf = x.astype(jnp.float32)
    y = xf * lax.rsqrt(jnp.mean(xf * xf, axis=-1, keepdims=True) + EPS)
    return (y * w.astype(jnp.float32)).astype(x.dtype)


def swiglu_ffn(x, w_gu, w_down):
    g, u = jnp.split(x @ w_gu, 2, axis=-1)
    return (jax.nn.silu(g) * u) @ w_down


def rope_tables(positions):
    inv = 1.0 / (ROPE_THETA ** (jnp.arange(0, MLA_ROPE, 2, dtype=jnp.float32) / MLA_ROPE))
    ang = positions.astype(jnp.float32)[..., None] * inv
    return jnp.cos(ang), jnp.sin(ang)


def apply_rope(x, cos, sin):
    x1, x2 = jnp.split(x.astype(jnp.float32), 2, axis=-1)
    return jnp.concatenate([x1 * cos - x2 * sin, x1 * sin + x2 * cos], axis=-1).astype(x.dtype)


def centred_depthwise_conv(x, w, bias):
    ch = x.shape[-1]
    y = lax.conv_general_dilated(
        x, w[:, None, :].astype(x.dtype), window_strides=(1,),
        padding=[(SSD_CONV // 2, SSD_CONV // 2)],
        dimension_numbers=("NWC", "WIO", "NWC"), feature_group_count=ch)
    return y + bias.astype(x.dtype)


def ssd_chunked(xdt, la, bm, cm):
    b, l, g, r, p = xdt.shape
    n = bm.shape[-1]
    q = SSD_CHUNK
    nc = l // q
    xdt = xdt.reshape(b, nc, q, g, r, p)
    la = la.reshape(b, nc, q, g, r)
    bm = bm.reshape(b, nc, q, g, n)
    cm = cm.reshape(b, nc, q, g, n)
    cum = jnp.cumsum(la, axis=2)
    lower = jnp.tril(jnp.ones((q, q), dtype=bool))[:, :, None, None]
    seg = cum[:, :, :, None] - cum[:, :, None, :]
    decay = jnp.exp(jnp.where(lower, seg, -jnp.inf))
    cb = jnp.einsum("bzlgn,bzsgn->bzlsg", cm, bm)
    y_diag = jnp.einsum("bzlsgr,bzsgrp->bzlgrp", cb[..., None] * decay, xdt)
    decay_end = jnp.exp(cum[:, :, -1:] - cum)
    states = jnp.einsum("bzsgn,bzsgrp->bzgrpn", bm, xdt * decay_end[..., None])
    chunk_decay = jnp.exp(cum[:, :, -1])

    def carry_step(h, inp):
        s_z, d_z = inp
        return h * d_z[..., None, None] + s_z, h

    h0 = jnp.zeros((b, g, r, p, n), xdt.dtype)
    _, h_in = lax.scan(carry_step, h0, (jnp.moveaxis(states, 1, 0), jnp.moveaxis(chunk_decay, 1, 0)))
    h_in = jnp.moveaxis(h_in, 0, 1)
    y_off = jnp.einsum("bzlgn,bzgrpn->bzlgrp", cm, h_in) * jnp.exp(cum)[..., None]
    return (y_diag + y_off).reshape(b, l, g, r, p)


def block_attention(q, k, v, scale):
    b, s, h, dk = q.shape
    nb = s // MLA_Q_BLOCK
    qb = jnp.moveaxis(q.reshape(b, nb, MLA_Q_BLOCK, h, dk), 1, 0)

    def one_block(qi):
        sc = jnp.einsum("bqhd,bkhd->bhqk", qi, k).astype(jnp.float32) * scale
        pr = jax.nn.softmax(sc, axis=-1).astype(v.dtype)
        return jnp.einsum("bhqk,bkhd->bqhd", pr, v)

    o = lax.map(one_block, qb)
    return jnp.moveaxis(o, 0, 1).reshape(b, s, h, v.shape[-1])


def ssd_mla_mixer(u, cos, sin, w_in, conv_w, conv_b, dt_bias, a_log, ssd_d, ssd_norm,
                  q_norm, w_uq, kv_norm, w_ukv, w_out):
    b, s, _ = u.shape
    z, xbc, dt_raw, c_q, c_kv, k_rope = jnp.split(u @ w_in, IN_SPLITS, axis=-1)
    xbc = jax.nn.silu(centred_depthwise_conv(xbc, conv_w, conv_b))
    xs, bm, cm = jnp.split(xbc, [SSD_INNER, SSD_INNER + SSD_GROUPS * SSD_STATE], axis=-1)
    xs = xs.astype(jnp.float32).reshape(b, s, SSD_GROUPS, SSD_HPG, SSD_HEAD_DIM)
    bm = bm.astype(jnp.float32).reshape(b, s, SSD_GROUPS, SSD_STATE)
    cm = cm.astype(jnp.float32).reshape(b, s, SSD_GROUPS, SSD_STATE)
    dt = jax.nn.softplus(dt_raw.astype(jnp.float32).reshape(b, s, 2, SSD_GROUPS, SSD_HPG)
                         + dt_bias.astype(jnp.float32).reshape(2, SSD_GROUPS, SSD_HPG))
    a = -jnp.exp(a_log.astype(jnp.float32)).reshape(2, SSD_GROUPS, SSD_HPG)
    y_fwd = ssd_chunked(xs * dt[:, :, 0, ..., None], dt[:, :, 0] * a[0], bm, cm)
    flip = lambda t: jnp.flip(t, axis=1)
    y_bwd = flip(ssd_chunked(flip(xs * dt[:, :, 1, ..., None]), flip(dt[:, :, 1] * a[1]), flip(bm), flip(cm)))
    y = y_fwd + y_bwd + xs * ssd_d.astype(jnp.float32).reshape(SSD_GROUPS, SSD_HPG, 1)
    y = y.reshape(b, s, SSD_INNER).astype(u.dtype)
    y_ssd = rmsnorm(y * jax.nn.silu(z), ssd_norm)
    q = (rmsnorm(c_q, q_norm) @ w_uq).reshape(b, s, MLA_HEADS, MLA_NOPE + MLA_ROPE)
    q_nope, q_pe = jnp.split(q, [MLA_NOPE], axis=-1)
    q_pe = apply_rope(q_pe, cos[:, :, None], sin[:, :, None])
    kv = (rmsnorm(c_kv, kv_norm) @ w_ukv).reshape(b, s, MLA_HEADS, MLA_NOPE + MLA_V)
    k_nope, v = jnp.split(kv, [MLA_NOPE], axis=-1)
    k_pe = apply_rope(k_rope, cos, sin)
    k = jnp.concatenate([k_nope, jnp.broadcast_to(k_pe[:, :, None], (b, s, MLA_HEADS, MLA_ROPE))], axis=-1)
    qf = jnp.concatenate([q_nope, q_pe], axis=-1)
    o_mla = block_attention(qf, k, v, (MLA_NOPE + MLA_ROPE) ** -0.5).reshape(b, s, MLA_HEADS * MLA_V)
    return jnp.concatenate([y_ssd, o_mla], axis=-1) @ w_out


def fourier_mixer(u, w_out):
    b, s, d = u.shape
    ug = jnp.moveaxis(u.astype(jnp.float32).reshape(b, s, FNET_GROUPS, FNET_GROUP_CH), 2, 1)
    f = jnp.fft.fft2(ug, norm="ortho").real
    return jnp.moveaxis(f, 1, 2).reshape(b, s, d).astype(u.dtype) @ w_out


def memory_cross_attention(hn, mem_n, wq, wkv, wo):
    b, s, d = hn.shape
    t = mem_n.shape[1]
    q = (hn @ wq).reshape(b, s, XA_HEADS, XA_HEAD_DIM)
    k, v = jnp.split(mem_n @ wkv, 2, axis=-1)
    k = k.reshape(b, t, XA_HEADS, XA_HEAD_DIM)
    v = v.reshape(b, t, XA_HEADS, XA_HEAD_DIM)
    sc = jnp.einsum("bqhd,bkhd->bhqk", q, k).astype(jnp.float32) * (XA_HEAD_DIM ** -0.5)
    pr = jax.nn.softmax(sc, axis=-1).astype(v.dtype)
    o = jnp.einsum("bhqk,bkhd->bqhd", pr, v).reshape(b, s, d)
    return o @ wo


def setup_inputs(seed: int = 0) -> dict:
    key = jax.random.key(seed)
    ks = iter(jax.random.split(key, 32))
    f32 = jnp.float32

    def nrm(shape, fan_in):
        return jax.random.normal(next(ks), shape, f32) * (fan_in ** -0.5)

    def gain(shape):
        return 1.0 + 0.02 * jax.random.normal(next(ks), shape, f32)

    L, E, O = DEPTH, N_EVEN, N_ODD
    x = jax.random.normal(next(ks), (BATCH, SEQ, D_MODEL), f32)
    mem = jax.random.normal(next(ks), (BATCH, N_MEM, D_MODEL), f32)
    positions = jnp.broadcast_to(jnp.arange(SEQ, dtype=jnp.int32)[None], (BATCH, SEQ))
    mem_norm = gain((D_MODEL,))
    final_norm = gain((D_MODEL,))
    ffn1_norm = gain((L, D_MODEL))
    ffn1_w_gu = nrm((L, D_MODEL, 2 * D_FF), D_MODEL)
    ffn1_w_down = nrm((L, D_FF, D_MODEL), D_FF)
    mix_norm = gain((L, D_MODEL))
    xa_norm = gain((L, D_MODEL))
    xa_wq = nrm((L, D_MODEL, D_MODEL), D_MODEL)
    xa_wkv = nrm((L, D_MODEL, 2 * D_MODEL), D_MODEL)
    xa_wo = nrm((L, D_MODEL, D_MODEL), D_MODEL)
    ffn2_norm = gain((L, D_MODEL))
    ffn2_w_gu = nrm((L, D_MODEL, 2 * D_FF), D_MODEL)
    ffn2_w_down = nrm((L, D_FF, D_MODEL), D_FF)
    w_in = nrm((E, D_MODEL, IN_COLS), D_MODEL)
    conv_w = nrm((E, SSD_CONV, SSD_CONV_CH), SSD_CONV)
    conv_b = 0.02 * jax.random.normal(next(ks), (E, SSD_CONV_CH), f32)
    dt0 = jnp.exp(jax.random.uniform(next(ks), (E, 2, SSD_HEADS), f32,
                                     minval=math.log(1e-3), maxval=math.log(1e-1)))
    dt_bias = dt0 + jnp.log(-jnp.expm1(-dt0))
    a_log = jnp.log(jax.random.uniform(next(ks), (E, 2, SSD_HEADS), f32, minval=1.0, maxval=16.0))
    ssd_d = 1.0 + 0.1 * jax.random.normal(next(ks), (E, SSD_HEADS), f32)
    ssd_norm = gain((E, SSD_INNER))
    q_norm = gain((E, MLA_Q_RANK))
    w_uq = nrm((E, MLA_Q_RANK, MLA_HEADS * (MLA_NOPE + MLA_ROPE)), MLA_Q_RANK)
    kv_norm = gain((E, MLA_KV_RANK))
    w_ukv = nrm((E, MLA_KV_RANK, MLA_HEADS * (MLA_NOPE + MLA_V)), MLA_KV_RANK)
    w_out = nrm((E, MIX_WIDTH, D_MODEL), MIX_WIDTH)
    fnet_w_out = nrm((O, D_MODEL, D_MODEL), D_MODEL)
    return {
        "x": x, "mem": mem, "positions": positions, "mem_norm": mem_norm, "final_norm": final_norm,
        "ffn1_norm": ffn1_norm, "ffn1_w_gu": ffn1_w_gu, "ffn1_w_down": ffn1_w_down,
        "mix_norm": mix_norm, "xa_norm": xa_norm, "xa_wq": xa_wq, "xa_wkv": xa_wkv, "xa_wo": xa_wo,
        "ffn2_norm": ffn2_norm, "ffn2_w_gu": ffn2_w_gu, "ffn2_w_down": ffn2_w_down,
        "w_in": w_in, "conv_w": conv_w, "conv_b": conv_b, "dt_bias": dt_bias, "a_log": a_log,
        "ssd_d": ssd_d, "ssd_norm": ssd_norm, "q_norm": q_norm, "w_uq": w_uq, "kv_norm": kv_norm,
        "w_ukv": w_ukv, "w_out": w_out, "fnet_w_out": fnet_w_out,
    }


def reference(x, mem, positions, mem_norm, final_norm,
              ffn1_norm, ffn1_w_gu, ffn1_w_down,
              mix_norm, xa_norm, xa_wq, xa_wkv, xa_wo,
              ffn2_norm, ffn2_w_gu, ffn2_w_down,
              w_in, conv_w, conv_b, dt_bias, a_log, ssd_d, ssd_norm,
              q_norm, w_uq, kv_norm, w_ukv, w_out, fnet_w_out):
    cos, sin = rope_tables(positions)
    mem_n = rmsnorm(mem, mem_norm)
    h = x
    for layer in range(DEPTH):
        h = h + 0.5 * swiglu_ffn(rmsnorm(h, ffn1_norm[layer]), ffn1_w_gu[layer], ffn1_w_down[layer])
        u = rmsnorm(h, mix_norm[layer])
        if layer % 2 == 0:
            e = layer // 2
            h = h + ssd_mla_mixer(u, cos, sin, w_in[e], conv_w[e], conv_b[e], dt_bias[e], a_log[e],
                                  ssd_d[e], ssd_norm[e], q_norm[e], w_uq[e], kv_norm[e], w_ukv[e], w_out[e])
        else:
            h = h + fourier_mixer(u, fnet_w_out[layer // 2])
        h = h + memory_cross_attention(rmsnorm(h, xa_norm[layer]), mem_n, xa_wq[layer], xa_wkv[layer], xa_wo[layer])
        h = h + 0.5 * swiglu_ffn(rmsnorm(h, ffn2_norm[layer]), ffn2_w_gu[layer], ffn2_w_down[layer])
    return rmsnorm(h, final_norm)
```

```python
import math
import numpy as np
import ml_dtypes
import concourse.bass as bass
import concourse.mybir as mybir
from concourse.bass_utils import run_bass_kernel_spmd

F32 = mybir.dt.float32
BF16 = mybir.dt.bfloat16
I32 = mybir.dt.int32
U8 = mybir.dt.uint8
AF = mybir.ActivationFunctionType
ALU = mybir.AluOpType

NCORES = 8
NSEQ = 2
SEQ = 2048
D = 1024
DFF = 2816
NMEM = 256
EPS = 1e-6
DEPTH = 4
DBG = set()
SW2 = 99
EVEN_STOP = 99

ENGS = ("pe", "act", "dve", "pool", "sp")
NDSEM = {"sp": 6, "act": 3, "pool": 6}


class Tok:
    __slots__ = ("w", "rs")

    def __init__(self):
        self.w = None
        self.rs = {}


class Op:
    __slots__ = ("eng", "fn", "idx", "key", "val", "waits", "sig", "sigval", "K", "dma")


class Sched:
    def __init__(self, nc):
        self.nc = nc
        self.ops = {e: [] for e in ENGS}
        self.cnt = {e: 0 for e in ENGS}
        self.K = {e: {} for e in ENGS}
        self.dcnt = {}
        self.drr = {q: 0 for q in NDSEM}
        self.last = {}

    def _add(self, eng, fn, deps, dma):
        op = Op()
        op.eng, op.fn, op.dma, op.sig, op.sigval = eng, fn, dma, False, 0
        self.cnt[eng] += 1
        op.idx = self.cnt[eng]
        if dma:
            slot = self.drr[eng]
            self.drr[eng] = (slot + 1) % NDSEM[eng]
            op.key = (eng, slot)
            self.dcnt[op.key] = self.dcnt.get(op.key, 0) + 1
            op.val = self.dcnt[op.key]
        else:
            op.key = eng
            op.val = op.idx
        K = self.K[eng]
        waits = []
        for d in sorted(deps, key=lambda d: -d.val):
            if eng == "pe" and d.eng == "pe" and not d.dma:
                continue
            if K.get(d.key, 0) >= d.val:
                continue
            waits.append(d)
            d.sig = True
            for k, v in d.K.items():
                if K.get(k, 0) < v:
                    K[k] = v
            if K.get(d.key, 0) < d.val:
                K[d.key] = d.val
        op.waits = waits
        op.K = dict(K)
        self.ops[eng].append(op)
        if fn is not None:
            self.last[op.key] = op
        return op

    def op(self, eng, fn, r=(), w=(), dma=False, nowaw=False):
        deps = set()
        for t in r:
            if t.w is not None:
                deps.add(t.w)
        for t in w:
            if t.w is not None and not (nowaw and t.w.eng == eng and not t.w.dma):
                deps.add(t.w)
            for o in t.rs.values():
                deps.add(o)
        op = self._add(eng, fn, deps, dma)
        for t in r:
            t.rs[op.key] = op
        for t in w:
            t.w = op
            t.rs = {}
        return op

    def dma(self, q, out, in_, r=(), w=()):
        return self.op(q, lambda e: e.dma_start(out=out, in_=in_), r, w, dma=True)

    def barrier(self):
        lasts = set(self.last.values())
        for e in ENGS:
            self._add(e, None, lasts, False)

    def emit(self):
        nc = self.nc
        from contextlib import ExitStack
        with ExitStack() as es:
            esem = {e: es.enter_context(nc.semaphore("s_" + e)) for e in ENGS}
            dsem = {}
            for q, n in NDSEM.items():
                for i in range(n):
                    dsem[(q, i)] = es.enter_context(nc.semaphore("d_%s%d" % (q, i)))
            for e in ENGS:
                c = 0
                for op in self.ops[e]:
                    if op.dma:
                        continue
                    if op.sig and op.fn is not None:
                        c += 1
                    op.sigval = c

            def run(e, eng):
                for op in self.ops[e]:
                    for d in op.waits:
                        if d.dma:
                            eng.wait_ge(dsem[d.key], 16 * d.val)
                        elif d.sigval > 0:
                            eng.wait_ge(esem[d.eng], d.sigval)
                    if op.fn is None:
                        continue
                    ins = op.fn(eng)
                    if op.dma:
                        ins.then_inc(dsem[op.key], 16)
                    elif op.sig:
                        ins.then_inc(esem[e], 1)

            block = es.enter_context(nc.Block())

            @block.tensor
            def _(eng):
                run("pe", eng)

            @block.scalar
            def _(eng):
                run("act", eng)

            @block.vector
            def _(eng):
                run("dve", eng)

            @block.gpsimd
            def _(eng):
                run("pool", eng)

            @block.sync
            def _(eng):
                run("sp", eng)


WEIGHT_SPECS = [
    ("ffn1_w_gu", [4, 1024, 5632]), ("ffn1_w_down", [4, 2816, 1024]),
    ("xa_wq", [4, 1024, 1024]), ("xa_wkv", [4, 1024, 2048]), ("xa_wo", [4, 1024, 1024]),
    ("ffn2_w_gu", [4, 1024, 5632]), ("ffn2_w_down", [4, 2816, 1024]),
    ("w_in", [2, 1024, 3392]), ("w_uq", [2, 512, 768]), ("w_ukv", [2, 256, 1024]),
    ("w_out", [2, 1536, 1024]), ("fnet_w_out", [2, 1024, 1024]),
]
G_FFN1, G_MIX, G_XA, G_FFN2, G_FINAL, G_MEM, G_SSD = 0, 4, 8, 12, 16, 17, 18
NGAIN = 20


class Bump:
    def __init__(self, arena, base, limit):
        self.arena, self.off, self.limit = arena, base, limit

    def alloc(self, free_shape, dtype):
        esz = {F32: 4, BF16: 2, I32: 4, U8: 1}[dtype]
        n = esz
        for s in free_shape:
            n *= s
        off = (self.off + 31) // 32 * 32
        assert off + n <= self.limit, ("arena overflow", off, n, self.limit)
        self.off = off + n
        v = self.arena[:, off:off + n]
        if dtype != U8:
            v = v.bitcast(dtype)
        if len(free_shape) == 2:
            v = v.rearrange("p (a b) -> p a b", b=free_shape[1])
        elif len(free_shape) == 3:
            v = v.rearrange("p (a b c) -> p a b c", b=free_shape[1], c=free_shape[2])
        elif len(free_shape) == 4:
            v = v.rearrange("p (a b c d) -> p a b c d", b=free_shape[1], c=free_shape[2], d=free_shape[3])
        return v


def build_program(plan=None, nseq=NSEQ):
    nc = bass.Bass("TRN2", target_bir_lowering=False)
    dr = {}

    def din(name, shape, dt=F32):
        dr[name] = nc.dram_tensor(name, shape, dt, kind="ExternalInput").ap()
        return dr[name]

    x_d = din("x", [nseq, SEQ, D])
    mem_d = din("mem", [nseq, NMEM, D])
    pos_d = din("positions", [nseq, SEQ], I32)
    for name, shape in WEIGHT_SPECS:
        din(name, shape)
    gains_d = din("gains", [128, NGAIN * 8])
    qkvg_d = din("qkvgains", [128, 2, 6])
    convw_d = din("convw", [128, 2, 12, 5])
    convb_d = din("convb", [128, 2, 12])
    dtb_d = din("dt_bias", [2, 32])
    alog_d = din("a_log", [2, 32])
    ssdd_d = din("ssd_d", [2, 16])
    ident_d = din("ident", [128, 128])
    tri_d = din("tri", [128, 2, 128])
    invf_d = din("invf", [32, 1])
    sel_d = din("sel", [128, 2, 128])
    shift_d = din("shiftm", [128, 128])
    cdft_d = din("cdft", [128, 2, 512], BF16)
    sdft_d = din("sdft", [SEQ, 2, SEQ], BF16)
    out_d = nc.dram_tensor("out", [nseq, SEQ, D], F32, kind="ExternalOutput").ap()
    zs_d = nc.dram_tensor("zs_scr", [SEQ, D], BF16).ap()
    hin_d = nc.dram_tensor("hin_scr", [16, 128, 1024], BF16).ap()
    ot_d = nc.dram_tensor("ot_scr", [128, 4 * SEQ], BF16).ap()

    S = Sched(nc)
    A = nc.alloc_sbuf_tensor

    hT = A("hT", [128, 8, SEQ], F32)
    TH = [[Tok() for _ in range(4)] for _ in range(8)]
    identf = A("identf", [128, 128], F32)
    identb = A("identb", [128, 128], BF16)
    onesb = A("onesb", [128, 128], BF16)
    epst = A("epst", [128, 1], F32)
    gains = A("gains_sb", [128, NGAIN * 8], F32)
    qkvg = A("qkvg_sb", [128, 2, 6], F32)
    convw = A("convw_sb", [128, 2, 12, 5], F32)
    convb = A("convb_sb", [128, 2, 12], F32)
    tri = A("tri_sb", [128, 2, 128], F32)
    invf = A("invf_sb", [128, 1], F32)
    memT = A("memT", [128, 8, NMEM], BF16)
    cos2 = A("cos2", [128, SEQ], BF16)
    sin2 = A("sin2", [128, SEQ], BF16)
    shiftm = A("shiftm_sb", [128, 128], F32)
    TC = Tok()
    TMEM = Tok()
    TROPE = Tok()
    remaining = nc.sbuf_bytes_remaining
    ARENA = (remaining - 256) // 32 * 32
    arena = A("arena", [128, ARENA], U8)
    ps = nc.alloc_psum_tensor("ps", [128, 4096], F32)
    PB = [Tok() for _ in range(8)]

    def bank(i):
        return ps[:, i * 512:(i + 1) * 512]

    def mm(out, lhsT, rhs, start, stop, r, w):
        S.op("pe", lambda e: e.matmul(out, lhsT, rhs, start=start, stop=stop), r=r, w=w)

    def tr(out, in_, idn, r, w):
        S.op("pe", lambda e: e.transpose(out, in_, idn), r=r, w=w)

    def act(out, in_, func, r, w, **kw):
        S.op("act", lambda e: e.activation(out=out, in_=in_, func=func, **kw), r=r, w=w)

    def dve(name, *args, r, w, eng="dve", nowaw=False, **kw):
        S.op(eng, lambda e: getattr(e, name)(*args, **kw), r=r, w=w, nowaw=nowaw)

    def tsl(tt):
        return slice(tt * 512, (tt + 1) * 512)

    def gcol(gi, c):
        return gains[:, gi * 8 + c:gi * 8 + c + 1]

    S.dma("sp", identf[:], ident_d, w=[TC])
    S.dma("sp", gains[:], gains_d, w=[TC])
    S.dma("sp", qkvg[:], qkvg_d, w=[TC])
    S.dma("sp", convw[:], convw_d, w=[TC])
    S.dma("sp", convb[:], convb_d, w=[TC])
    S.dma("sp", tri[:], tri_d, w=[TC])
    S.dma("sp", invf[64:96, :], invf_d, w=[TC])
    S.dma("sp", shiftm[:], shift_d, w=[TC])
    dve("tensor_copy", identb[:], identf[:], r=[TC], w=[TC])
    dve("memset", onesb[:], 1.0, r=[], w=[TC])
    dve("memset", epst[:], EPS, r=[], w=[TC])

    class Ring:
        def __init__(self, b, n, nbytes):
            self.slots = [b.alloc((nbytes,), U8) for _ in range(n)]
            self.toks = [Tok() for _ in range(n)]
            self.i = 0

        def next(self):
            k = self.i % len(self.slots)
            self.i += 1
            return self.slots[k], self.toks[k]

    rr = {"n": 6}

    def nbank(group):
        lo, n = group
        k = rr.get(group, 0)
        rr[group] = (k + 1) % n
        return lo + k

    def rmsnorm_fm(b, src, src_toks, nch, dim, gain_fn, out, out_toks, ntt=4, banks=(6, 2), scr_sq=None):
        sq = [b.alloc((nch, 512), BF16) for _ in range(ntt)]
        tsq = [Tok() for _ in range(ntt)]
        rs = [b.alloc((512,), F32) for _ in range(ntt)]
        trs = [Tok() for _ in range(ntt)]
        bks = []
        for tt in range(ntt):
            act(sq[tt][:], src[:, :, tsl(tt)], AF.Square, r=src_toks(tt), w=[tsq[tt]])
        for tt in range(ntt):
            bk = nbank(banks)
            for c in range(nch):
                mm(bank(bk), onesb[:], sq[tt][:, c, :], c == 0, c == nch - 1, r=[tsq[tt], TC], w=[PB[bk]])
            act(rs[tt][:], bank(bk), AF.Ln, r=[PB[bk], TC], w=[trs[tt]], scale=1.0 / dim, bias=epst[:])
        for tt in range(ntt):
            act(rs[tt][:], rs[tt][:], AF.Exp, r=[trs[tt]], w=[trs[tt]], scale=-0.5)
            for c in range(nch):
                eng = "dve"
                dve("scalar_tensor_tensor", out[:, c, tsl(tt)], src[:, c, tsl(tt)], gain_fn(c), rs[tt][:], ALU.mult, ALU.mult,
                    r=src_toks(tt) + [trs[tt], TC], w=[out_toks[tt]], eng=eng, nowaw=True)

    def h_toks(tt):
        return [TH[c][tt] for c in range(8)]

    def proj_accum(b, wsrc, nkc, actT, act_toks, scale, ring, ybanks=(4, 2), act_view=None):
        for dcp in range(4):
            slot, tk = ring.next()
            wd = slot[:, 0:nkc * 256 * 2].bitcast(BF16).rearrange("p (j n) -> p j n", j=nkc)
            S.dma("pool", wd, wsrc[:, :, dcp * 256:(dcp + 1) * 256], w=[tk])
            for dd in range(2):
                dc = dcp * 2 + dd
                for tt in range(4):
                    by = nbank(ybanks)
                    for j in range(nkc):
                        av = act_view(j, tt) if act_view is not None else actT[:, j, tsl(tt)]
                        at = act_toks(j, tt)
                        mm(bank(by), wd[:, j, dd * 128:(dd + 1) * 128], av, j == 0, j == nkc - 1,
                           r=[tk] + (at if isinstance(at, list) else [at]), w=[PB[by]])
                    dve("scalar_tensor_tensor", hT[:, dc, tsl(tt)], bank(by), float(scale), hT[:, dc, tsl(tt)], ALU.mult, ALU.add,
                        r=[PB[by], TH[dc][tt]], w=[TH[dc][tt]])

    def load_x(s):
        S.barrier()
        b = Bump(arena, 0, ARENA)
        xt = [b.alloc((D,), F32) for _ in range(2)]
        txt = [Tok() for _ in range(2)]
        for t in range(16):
            i = t % 2
            S.dma("sp", xt[i][:], x_d[s, t * 128:(t + 1) * 128, :], w=[txt[i]])
            for cg in range(2):
                bk = nbank((0, 4))
                for ci in range(4):
                    c = cg * 4 + ci
                    tr(bank(bk)[:, ci * 128:(ci + 1) * 128], xt[i][:, c * 128:(c + 1) * 128], identf[:], r=[txt[i], TC], w=[PB[bk]])
                dst = hT[:, cg * 4:(cg + 1) * 4, t * 128:(t + 1) * 128]
                src = bank(bk).rearrange("p (c n) -> p c n", c=4)
                toks = [TH[c][t // 4] for c in range(cg * 4, cg * 4 + 4)]
                if cg == 0:
                    act(dst, src, AF.Copy, r=[PB[bk]], w=toks)
                else:
                    dve("tensor_copy", dst, src, r=[PB[bk]], w=toks)

    def prep_mem(s):
        S.barrier()
        b = Bump(arena, 0, ARENA)
        for t in range(2):
            mt = b.alloc((D,), F32)
            m2 = b.alloc((D,), F32)
            junk = b.alloc((D,), F32)
            ss = b.alloc((1,), F32)
            tk = Tok()
            S.dma("sp", mt[:], mem_d[s, t * 128:(t + 1) * 128, :], w=[tk])
            act(junk[:], mt[:], AF.Square, r=[tk], w=[tk], accum_out=ss[:])
            act(ss[:], ss[:], AF.Sqrt, r=[tk, TC], w=[tk], scale=1.0 / D, bias=epst[:])
            dve("reciprocal", ss[:], ss[:], r=[tk], w=[tk])
            act(m2[:], mt[:], AF.Copy, r=[tk], w=[tk], scale=ss[:])
            for cg in range(2):
                bk = nbank((0, 4))
                for ci in range(4):
                    c = cg * 4 + ci
                    tr(bank(bk)[:, ci * 128:(ci + 1) * 128], m2[:, c * 128:(c + 1) * 128], identf[:], r=[tk, TC], w=[PB[bk]])
                for ci in range(4):
                    c = cg * 4 + ci
                    act(memT[:, c, t * 128:(t + 1) * 128], bank(bk)[:, ci * 128:(ci + 1) * 128], AF.Copy, r=[PB[bk], TC], w=[TMEM],
                        scale=gcol(G_MEM, c))

    def rope_tables(s):
        S.barrier()
        b = Bump(arena, 0, ARENA)
        pi_ = b.alloc((SEQ,), I32)
        pf = b.alloc((SEQ,), F32)
        tk = Tok()
        S.dma("sp", pi_[64:96, :], pos_d[s:s + 1, :].to_broadcast([32, SEQ]), w=[tk])
        dve("tensor_copy", pf[64:96, :], pi_[64:96, :], r=[tk], w=[tk])
        ang = b.alloc((SEQ,), F32)
        t_ = b.alloc((SEQ,), F32)
        ki = b.alloc((SEQ,), I32)
        kf = b.alloc((SEQ,), F32)
        r_ = b.alloc((SEQ,), F32)
        m_ = b.alloc((SEQ,), F32)
        P32 = slice(64, 96)
        dve("tensor_scalar", ang[P32, :], pf[P32, :], invf[P32, 0:1], 0.0, ALU.mult, ALU.add, r=[tk, TC], w=[tk])
        for dst, off in ((sin2, 0.0), (cos2, math.pi / 2)):
            dve("tensor_scalar", t_[P32, :], ang[P32, :], off, 1.0 / (2 * math.pi), ALU.add, ALU.mult, r=[tk], w=[tk])
            dve("tensor_copy", ki[P32, :], t_[P32, :], r=[tk], w=[tk])
            dve("tensor_copy", kf[P32, :], ki[P32, :], r=[tk], w=[tk])
            dve("tensor_scalar", r_[P32, :], ang[P32, :], off, None, ALU.add, r=[tk], w=[tk])
            dve("scalar_tensor_tensor", r_[P32, :], kf[P32, :], -2 * math.pi, r_[P32, :], ALU.mult, ALU.add, r=[tk], w=[tk])
            dve("tensor_scalar", m_[P32, :], r_[P32, :], math.pi, None, ALU.is_gt, r=[tk], w=[tk])
            dve("scalar_tensor_tensor", r_[P32, :], m_[P32, :], -2 * math.pi, r_[P32, :], ALU.mult, ALU.add, r=[tk], w=[tk])
            dve("tensor_scalar", m_[P32, :], r_[P32, :], -math.pi, None, ALU.is_lt, r=[tk], w=[tk])
            dve("scalar_tensor_tensor", r_[P32, :], m_[P32, :], 2 * math.pi, r_[P32, :], ALU.mult, ALU.add, r=[tk], w=[tk])
            act(dst[P32, :], r_[P32, :], AF.Sin, r=[tk], w=[TROPE])

    def ffn(gu, dn, gi):
        S.barrier()
        b = Bump(arena, 0, ARENA)
        xn = b.alloc((8, SEQ), BF16)
        txn = [Tok() for _ in range(4)]
        rmsnorm_fm(Bump(arena, b.off, ARENA), hT, h_toks, 8, D, lambda c: gcol(gi, c), xn, txn)
        S.barrier()
        actb = b.alloc((11, SEQ), BF16)
        ta = [[Tok() for _ in range(4)] for _ in range(11)]
        sg = [b.alloc((512,), F32) for _ in range(2)]
        tsg = [Tok() for _ in range(2)]
        ring = Ring(b, 6, 6144)
        gsrc = gu.rearrange("(kc p) (two n) -> p two kc n", p=128, two=2)
        dsrc = dn.rearrange("(j p) n -> p j n", p=128)
        k = 0
        for half in range(2):
            for jj in range(11):
                j = half * 11 + jj
                slot, tk = ring.next()
                wv = slot[:, 0:4096].bitcast(BF16).rearrange("p (two kc n) -> p two kc n", two=2, kc=8)
                S.dma("pool", wv, gsrc[:, :, :, j * 128:(j + 1) * 128], w=[tk])
                for tt in range(4):
                    bg = nbank((0, 2))
                    bu = nbank((2, 2))
                    for kc in range(8):
                        mm(bank(bg), wv[:, 0, kc, :], xn[:, kc, tsl(tt)], kc == 0, kc == 7, r=[tk, txn[tt]], w=[PB[bg]])
                    for kc in range(8):
                        mm(bank(bu), wv[:, 1, kc, :], xn[:, kc, tsl(tt)], kc == 0, kc == 7, r=[tk, txn[tt]], w=[PB[bu]])
                    i = k % 2
                    k += 1
                    act(sg[i][:], bank(bg), AF.Silu, r=[PB[bg]], w=[tsg[i]])
                    dve("tensor_tensor", actb[:, jj, tsl(tt)], bank(bu), sg[i][:], ALU.mult, r=[PB[bu], tsg[i]], w=[ta[jj][tt]])
            proj_accum(b, dsrc[:, half * 11:(half + 1) * 11, :], 11, actb, lambda j, tt: ta[j][tt], 0.5, ring)

    def xattn(layer):
        S.barrier()
        b = Bump(arena, 0, ARENA)
        xn = b.alloc((8, SEQ), BF16)
        txn = [Tok() for _ in range(4)]
        rmsnorm_fm(Bump(arena, b.off, ARENA), hT, h_toks, 8, D, lambda c: gcol(G_XA + layer, c), xn, txn)
        S.barrier()
        QT = b.alloc((8, SEQ), BF16)
        tq = [[Tok() for _ in range(4)] for _ in range(8)]
        OT = xn
        tot = [[Tok() for _ in range(4)] for _ in range(8)]
        KT = b.alloc((8, NMEM), BF16)
        tkt = Tok()
        Vt = b.alloc((2, D), BF16)
        tv = Tok()
        PT = [b.alloc((512,), BF16) for _ in range(4)]
        tpt = [Tok() for _ in range(4)]
        rden = [b.alloc((512,), F32) for _ in range(2)]
        trd = [Tok() for _ in range(2)]
        ring = Ring(b, 3, 8192)
        wkv = dr["xa_wkv"][layer].rearrange("(kc p) n -> p kc n", p=128)
        wq = dr["xa_wq"][layer].rearrange("(kc p) n -> p kc n", p=128)
        wo = dr["xa_wo"][layer].rearrange("(kc p) n -> p kc n", p=128)
        for ocp in range(4):
            slot, tk = ring.next()
            wv = slot[:, 0:4096].bitcast(BF16).rearrange("p (kc n) -> p kc n", kc=8)
            S.dma("pool", wv, wkv[:, :, ocp * 256:(ocp + 1) * 256], w=[tk])
            for oo in range(2):
                oc = ocp * 2 + oo
                bk = nbank((6, 2))
                for kc in range(8):
                    mm(bank(bk)[:, 0:NMEM], wv[:, kc, oo * 128:(oo + 1) * 128], memT[:, kc, :], kc == 0, kc == 7, r=[tk, TMEM], w=[PB[bk]])
                act(KT[:, oc, :], bank(bk)[:, 0:NMEM], AF.Copy, r=[PB[bk]], w=[tkt])
        for ct in range(2):
            slot, tk = ring.next()
            wv = slot[:, 0:8192].bitcast(BF16).rearrange("p (kc n) -> p kc n", kc=8)
            S.dma("pool", wv, wkv[:, :, D + ct * 512:D + (ct + 1) * 512], w=[tk])
            for kt in range(2):
                bk = nbank((6, 2))
                for kc in range(8):
                    mm(bank(bk), memT[:, kc, kt * 128:(kt + 1) * 128], wv[:, kc, :], kc == 0, kc == 7, r=[tk, TMEM], w=[PB[bk]])
                dve("tensor_copy", Vt[:, kt, ct * 512:(ct + 1) * 512], bank(bk), r=[PB[bk]], w=[tv])
        for ocp in range(4):
            slot, tk = ring.next()
            wv = slot[:, 0:4096].bitcast(BF16).rearrange("p (kc n) -> p kc n", kc=8)
            S.dma("pool", wv, wq[:, :, ocp * 256:(ocp + 1) * 256], w=[tk])
            for oo in range(2):
                oc = ocp * 2 + oo
                for tt in range(4):
                    bk = nbank((6, 2))
                    for kc in range(8):
                        mm(bank(bk), wv[:, kc, oo * 128:(oo + 1) * 128], xn[:, kc, tsl(tt)], kc == 0, kc == 7, r=[tk, txn[tt]], w=[PB[bk]])
                    act(QT[:, oc, tsl(tt)], bank(bk), AF.Copy, r=[PB[bk]], w=[tq[oc][tt]])
        S.barrier()
        scale = 256.0 ** -0.5
        ip = 0
        ir = 0
        for h in range(4):
            for tt in range(4):
                pts = []
                for kc in range(2):
                    bk = nbank((0, 2))
                    for dc in range(2):
                        mm(bank(bk), KT[:, h * 2 + dc, kc * 128:(kc + 1) * 128], QT[:, h * 2 + dc, tsl(tt)], dc == 0, dc == 1,
                           r=[tkt, tq[h * 2 + dc][tt]], w=[PB[bk]])
                    i = ip % 4
                    ip += 1
                    act(PT[i][:], bank(bk), AF.Exp, r=[PB[bk]], w=[tpt[i]], scale=scale)
                    pts.append(i)
                bd = nbank((2, 2))
                for kc in range(2):
                    mm(bank(bd), onesb[:], PT[pts[kc]][:], kc == 0, kc == 1, r=[tpt[pts[kc]], TC], w=[PB[bd]])
                j = ir % 2
                ir += 1
                dve("reciprocal", rden[j][:], bank(bd), r=[PB[bd]], w=[trd[j]])
                for dvc in range(2):
                    bo = nbank((4, 2))
                    for kc in range(2):
                        mm(bank(bo), Vt[:, kc, h * 256 + dvc * 128:h * 256 + (dvc + 1) * 128], PT[pts[kc]][:], kc == 0, kc == 1,
                           r=[tv, tpt[pts[kc]]], w=[PB[bo]])
                    dve("tensor_tensor", OT[:, h * 2 + dvc, tsl(tt)], bank(bo), rden[j][:], ALU.mult, r=[PB[bo], trd[j]], w=[tot[h * 2 + dvc][tt]])
        proj_accum(b, wo, 8, OT, lambda j, tt: tot[j][tt], 1.0, ring, ybanks=(6, 2))

    def fnet(layer):
        S.barrier()
        b = Bump(arena, 0, ARENA)
        xn = b.alloc((8, SEQ), BF16)
        txn = [Tok() for _ in range(4)]
        rmsnorm_fm(Bump(arena, b.off, ARENA), hT, h_toks, 8, D, lambda c: gcol(G_MIX + layer, c), xn, txn)
        S.barrier()
        Y = b.alloc((16, 4, 512), BF16)
        ty = [Tok() for _ in range(16)]
        cd = b.alloc((2, 512), BF16)
        tcd = Tok()
        ring = Ring(b, 6, 2048)
        ring2 = Ring(b, 2, 4096)
        S.dma("sp", cd[:], cdft_d, w=[tcd])
        k = 0
        for sc in range(16):
            for g in range(4):
                bk = nbank((0, 4))
                for cc in range(2):
                    mm(bank(bk), xn[:, g * 2 + cc, sc * 128:(sc + 1) * 128], cd[:, cc, :], cc == 0, cc == 1, r=[txn[sc // 4], tcd], w=[PB[bk]])
                if k % 2 == 0:
                    act(Y[:, sc, g, :], bank(bk), AF.Copy, r=[PB[bk]], w=[ty[sc]])
                else:
                    dve("tensor_copy", Y[:, sc, g, :], bank(bk), r=[PB[bk]], w=[ty[sc]])
                k += 1
        nrm = 1.0 / math.sqrt(SEQ * 256.0)
        for kt in range(4):
            for sc in range(16):
                slot, tk = ring.next()
                dv = slot[:, 0:2048].bitcast(BF16).rearrange("p (a n) -> p a n", a=2)
                S.dma("sp", dv, sdft_d[sc * 128:(sc + 1) * 128, :, kt * 512:(kt + 1) * 512], w=[tk])
                for o in range(8):
                    g, jc = o // 2, o % 2
                    mm(bank(o), Y[:, sc, g, jc * 128:(jc + 1) * 128], dv[:, 0, :], sc == 0, False, r=[ty[sc], tk], w=[PB[o]])
                    mm(bank(o), Y[:, sc, g, 256 + jc * 128:256 + (jc + 1) * 128], dv[:, 1, :], False, sc == 15, r=[ty[sc], tk], w=[PB[o]])
            for o in range(8):
                if o % 2 == 0:
                    act(xn[:, o, tsl(kt)], bank(o), AF.Copy, r=[PB[o]], w=[txn[kt]], scale=nrm)
                else:
                    dve("tensor_scalar", xn[:, o, tsl(kt)], bank(o), nrm, None, ALU.mult, r=[PB[o]], w=[txn[kt]])
        wsrc = dr["fnet_w_out"][layer // 2].rearrange("(kc p) n -> p kc n", p=128)
        proj_accum(b, wsrc, 8, xn, lambda j, tt: txn[tt], 1.0, ring2, ybanks=(0, 4))

    def final(s):
        S.barrier()
        b = Bump(arena, 0, ARENA)
        sq = [b.alloc((8, 512), BF16) for _ in range(2)]
        tsq = [Tok() for _ in range(2)]
        rs = [b.alloc((512,), F32) for _ in range(2)]
        trs = [Tok() for _ in range(2)]
        xf = [b.alloc((8, 512), F32) for _ in range(2)]
        txf = [Tok() for _ in range(2)]
        yt = [b.alloc((D,), F32) for _ in range(2)]
        tyt = [Tok() for _ in range(2)]
        k = 0
        for tt in range(4):
            i = tt % 2
            act(sq[i][:], hT[:, :, tsl(tt)], AF.Square, r=h_toks(tt), w=[tsq[i]])
            bk = nbank((6, 2))
            for c in range(8):
                mm(bank(bk), onesb[:], sq[i][:, c, :], c == 0, c == 7, r=[tsq[i], TC], w=[PB[bk]])
            act(rs[i][:], bank(bk), AF.Sqrt, r=[PB[bk], TC], w=[trs[i]], scale=1.0 / D, bias=epst[:])
            dve("reciprocal", rs[i][:], rs[i][:], r=[trs[i]], w=[trs[i]])
            for c in range(8):
                dve("scalar_tensor_tensor", xf[i][:, c, :], hT[:, c, tsl(tt)], gcol(G_FINAL, c), rs[i][:], ALU.mult, ALU.mult,
                    r=h_toks(tt) + [trs[i], TC], w=[txf[i]])
            for t4 in range(4):
                t = tt * 4 + t4
                j = k % 2
                k += 1
                for cg in range(2):
                    bk2 = nbank((0, 4))
                    for ci in range(4):
                        c = cg * 4 + ci
                        tr(bank(bk2)[:, ci * 128:(ci + 1) * 128], xf[i][:, c, t4 * 128:(t4 + 1) * 128], identf[:], r=[txf[i], TC], w=[PB[bk2]])
                    if cg == 0:
                        act(yt[j][:, 0:512], bank(bk2), AF.Copy, r=[PB[bk2]], w=[tyt[j]])
                    else:
                        dve("tensor_copy", yt[j][:, 512:1024], bank(bk2), r=[PB[bk2]], w=[tyt[j]])
                S.dma("sp", out_d[s, t * 128:(t + 1) * 128, :], yt[j][:], r=[tyt[j]])

    def norm_tile(nt, src_tile, src_toks, nch, dim, gain_fn, out_view, out_toks, banks=(6, 2)):
        sq, tsq, rs, trs = nt
        i = rr.get("nt", 0)
        rr["nt"] = (i + 1) % 2
        act(sq[i][:, 0:nch, :], src_tile, AF.Square, r=src_toks, w=[tsq[i]])
        bk = nbank(banks)
        for c in range(nch):
            mm(bank(bk), onesb[:], sq[i][:, c, :], c == 0, c == nch - 1, r=[tsq[i], TC], w=[PB[bk]])
        act(rs[i][:], bank(bk), AF.Sqrt, r=[PB[bk], TC], w=[trs[i]], scale=1.0 / dim, bias=epst[:])
        dve("reciprocal", rs[i][:], rs[i][:], r=[trs[i]], w=[trs[i]])
        for c in range(nch):
            dve("scalar_tensor_tensor", out_view[:, c, :], src_tile[:, c, :], gain_fn(c), rs[i][:], ALU.mult, ALU.mult,
                r=src_toks + [trs[i], TC], w=out_toks)

    def even_mixer(e, layer):
        K16, K48, K76, K96, K104 = 16384, 49152, 77824, 98304, 98304 + 16384
        win = dr["w_in"][e].rearrange("(kc p) n -> p kc n", p=128)
        wuq_d = dr["w_uq"][e].rearrange("(kc p) n -> p kc n", p=128)
        wukv_d = dr["w_ukv"][e].rearrange("(kc p) n -> p kc n", p=128)
        wout_d = dr["w_out"][e].rearrange("(kc p) n -> p kc n", p=128)
        S.barrier()
        b0 = Bump(arena, 0, K16)
        OT = b0.alloc((4, SEQ), BF16)
        tot = [[Tok() for _ in range(4)] for _ in range(4)]
        bx = Bump(arena, K16, K48)
        xn = bx.alloc((8, SEQ), BF16)
        txn = [Tok() for _ in range(4)]
        rmsnorm_fm(Bump(arena, K48, ARENA), hT, h_toks, 8, D, lambda c: gcol(G_MIX + layer, c), xn, txn)
        S.barrier()
        ba = Bump(arena, K48, K76)
        cqn = ba.alloc((4, SEQ), BF16)
        tcq = [Tok() for _ in range(4)]
        ckvn = ba.alloc((2, SEQ), BF16)
        tckv = [Tok() for _ in range(4)]
        kpe = ba.alloc((SEQ,), BF16)
        tkpe = [Tok() for _ in range(4)]
        bt = Bump(arena, K76, ARENA)
        wcq = bt.alloc((8, 512), BF16)
        wckv = bt.alloc((8, 256), BF16)
        wkr = bt.alloc((8, 32), BF16)
        wkrr = bt.alloc((8, 32), BF16)
        tw = Tok()
        S.dma("pool", wcq, win[:, :, 2592:3104], w=[tw])
        S.dma("pool", wckv, win[:, :, 3104:3360], w=[tw])
        S.dma("pool", wkr, win[:, :, 3360:3392], w=[tw])
        dve("tensor_scalar", wkrr[:, :, 0:16], wkr[:, :, 16:32], -1.0, None, ALU.mult, r=[tw], w=[tw])
        dve("tensor_copy", wkrr[:, :, 16:32], wkr[:, :, 0:16], r=[tw], w=[tw])
        nt = ([bt.alloc((4, 512), BF16) for _ in range(2)], [Tok() for _ in range(2)],
              [bt.alloc((512,), F32) for _ in range(2)], [Tok() for _ in range(2)])
        cqr = [bt.alloc((4, 512), BF16) for _ in range(2)]
        tcqr = [Tok() for _ in range(2)]
        ckr = [bt.alloc((2, 512), BF16) for _ in range(2)]
        tckr = [Tok() for _ in range(2)]
        t1 = [bt.alloc((512,), F32) for _ in range(2)]
        t2 = [bt.alloc((512,), F32) for _ in range(2)]
        tt12 = [Tok() for _ in range(2)]
        for tt in range(4):
            i = tt % 2
            for oc in range(4):
                bk = nbank((0, 4))
                for kc in range(8):
                    mm(bank(bk), wcq[:, kc, oc * 128:(oc + 1) * 128], xn[:, kc, tsl(tt)], kc == 0, kc == 7, r=[tw, txn[tt]], w=[PB[bk]])
                act(cqr[i][:, oc, :], bank(bk), AF.Copy, r=[PB[bk]], w=[tcqr[i]])
            norm_tile(nt, cqr[i][:], [tcqr[i]], 4, 512, lambda c: qkvg[:, e, c:c + 1], cqn[:, :, tsl(tt)], [tcq[tt]])
            for oc in range(2):
                bk = nbank((0, 4))
                for kc in range(8):
                    mm(bank(bk), wckv[:, kc, oc * 128:(oc + 1) * 128], xn[:, kc, tsl(tt)], kc == 0, kc == 7, r=[tw, txn[tt]], w=[PB[bk]])
                act(ckr[i][:, oc, :], bank(bk), AF.Copy, r=[PB[bk]], w=[tckr[i]])
            norm_tile(nt, ckr[i][:], [tckr[i]], 2, 256, lambda c: qkvg[:, e, 4 + c:5 + c], ckvn[:, :, tsl(tt)], [tckv[tt]])
            ba_, bb_ = nbank((0, 4)), nbank((0, 4))
            for kc in range(8):
                mm(bank(ba_)[64:96, :], wkr[:, kc, :], xn[:, kc, tsl(tt)], kc == 0, kc == 7, r=[tw, txn[tt]], w=[PB[ba_]])
            for kc in range(8):
                mm(bank(bb_)[64:96, :], wkrr[:, kc, :], xn[:, kc, tsl(tt)], kc == 0, kc == 7, r=[tw, txn[tt]], w=[PB[bb_]])
            dve("tensor_tensor", t1[i][64:96, :], bank(ba_)[64:96, :], cos2[64:96, tsl(tt)], ALU.mult, r=[PB[ba_], TROPE], w=[tt12[i]])
            dve("tensor_tensor", t2[i][64:96, :], bank(bb_)[64:96, :], sin2[64:96, tsl(tt)], ALU.mult, r=[PB[bb_], TROPE], w=[tt12[i]])
            dve("tensor_tensor", kpe[64:96, tsl(tt)], t1[i][64:96, :], t2[i][64:96, :], ALU.add, r=[tt12[i]], w=[tkpe[tt]])
        S.barrier()
        if EVEN_STOP < 1:
            return
        bb = Bump(arena, K16, K48)
        V = bb.alloc((16, 8, 64), BF16)
        tv = Tok()
        wukv = bb.alloc((2, 1024), BF16)
        wuq = bb.alloc((4, 768), BF16)
        wuqr = bb.alloc((4, 8, 32), BF16)
        twb = Tok()
        S.dma("pool", wukv, wukv_d, w=[twb])
        S.dma("pool", wuq, wuq_d, w=[twb])
        for h in range(8):
            dve("tensor_scalar", wuqr[:, :, h, 0:16], wuq[:, :, h * 96 + 80:h * 96 + 96], -1.0, None, ALU.mult, r=[twb], w=[twb])
            dve("tensor_copy", wuqr[:, :, h, 16:32], wuq[:, :, h * 96 + 64:h * 96 + 80], r=[twb], w=[twb])
        bh = Bump(arena, K76, ARENA)
        Qh = [bh.alloc((SEQ,), BF16) for _ in range(2)]
        Kh = [bh.alloc((SEQ,), BF16) for _ in range(2)]
        tqk = [[Tok() for _ in range(4)] for _ in range(2)]
        Va = [bh.alloc((16, 128), BF16) for _ in range(2)]
        tva = [Tok() for _ in range(2)]
        PT = [bh.alloc((512,), BF16) for _ in range(4)]
        tpt = [Tok() for _ in range(4)]
        rden = [bh.alloc((512,), F32) for _ in range(2)]
        trd = [Tok() for _ in range(2)]
        rsh = [bh.alloc((512,), F32) for _ in range(2)]
        trs_ = [Tok() for _ in range(2)]
        u1 = [bh.alloc((512,), F32) for _ in range(2)]
        u2 = [bh.alloc((512,), F32) for _ in range(2)]
        tu = [Tok() for _ in range(2)]
        for sc in range(16):
            for ct in range(2):
                bk = nbank((6, 2))
                for kc in range(2):
                    mm(bank(bk), ckvn[:, kc, sc * 128:(sc + 1) * 128], wukv[:, kc, ct * 512:(ct + 1) * 512], kc == 0, kc == 1,
                       r=[twb, tckv[sc // 4]], w=[PB[bk]])
                src = bank(bk).rearrange("p (h c) -> p h c", h=4)[:, :, 64:128]
                if ct == 0:
                    act(V[:, sc, 0:4, :], src, AF.Copy, r=[PB[bk]], w=[tv])
                else:
                    dve("tensor_copy", V[:, sc, 4:8, :], src, r=[PB[bk]], w=[tv])
        dve("memset", Va[0][:, :, 64:128], 1.0, r=[], w=[tva[0]])
        dve("memset", Va[1][:, :, 0:64], 1.0, r=[], w=[tva[1]])
        for hb in range(2):
            dve("tensor_copy", Kh[hb][64:96, :], kpe[64:96, :], r=tkpe, w=tqk[hb])
        scale = 96.0 ** -0.5
        cnt = {"ip": 0, "ir": 0, "iu": 0}

        def head_proj(h):
            hb = h % 2
            po = hb * 64
            dve("tensor_copy", Va[hb][:, :, po:po + 64], V[:, :, h, :], r=[tv], w=[tva[hb]])
            for tt in range(4):
                bk = nbank((6, 2))
                for kc in range(4):
                    mm(bank(bk)[0:64, :], wuq[:, kc, h * 96:h * 96 + 64], cqn[:, kc, tsl(tt)], kc == 0, kc == 3, r=[twb, tcq[tt]], w=[PB[bk]])
                dve("tensor_copy", Qh[hb][0:64, tsl(tt)], bank(bk)[0:64, :], r=[PB[bk]], w=[tqk[hb][tt]])
                bk = nbank((6, 2))
                for kc in range(2):
                    mm(bank(bk)[0:64, :], wukv[:, kc, h * 128:h * 128 + 64], ckvn[:, kc, tsl(tt)], kc == 0, kc == 1, r=[twb, tckv[tt]], w=[PB[bk]])
                dve("tensor_copy", Kh[hb][0:64, tsl(tt)], bank(bk)[0:64, :], r=[PB[bk]], w=[tqk[hb][tt]])
                ba_, bb_ = nbank((6, 2)), nbank((6, 2))
                for kc in range(4):
                    mm(bank(ba_)[64:96, :], wuq[:, kc, h * 96 + 64:h * 96 + 96], cqn[:, kc, tsl(tt)], kc == 0, kc == 3, r=[twb, tcq[tt]], w=[PB[ba_]])
                for kc in range(4):
                    mm(bank(bb_)[64:96, :], wuqr[:, kc, h, :], cqn[:, kc, tsl(tt)], kc == 0, kc == 3, r=[twb, tcq[tt]], w=[PB[bb_]])
                i = cnt["iu"] % 2
                cnt["iu"] += 1
                dve("tensor_tensor", u1[i][64:96, :], bank(ba_)[64:96, :], cos2[64:96, tsl(tt)], ALU.mult, r=[PB[ba_], TROPE], w=[tu[i]])
                dve("tensor_tensor", u2[i][64:96, :], bank(bb_)[64:96, :], sin2[64:96, tsl(tt)], ALU.mult, r=[PB[bb_], TROPE], w=[tu[i]])
                dve("tensor_tensor", Qh[hb][64:96, tsl(tt)], u1[i][64:96, :], u2[i][64:96, :], ALU.add, r=[tu[i]], w=[tqk[hb][tt]])
                yield tt

        def s_issue(h, qt, kc):
            hb = h % 2
            bs = nbank((0, 4))
            mm(bank(bs), Kh[hb][0:96, kc * 128:(kc + 1) * 128], Qh[hb][0:96, tsl(qt)], True, True,
               r=[tqk[hb][kc // 4], tqk[hb][qt]], w=[PB[bs]])
            return bs

        for _ in head_proj(0):
            pass
        pending = []
        for h in range(8):
            hb = h % 2
            po = hb * 64
            dpo = 64 - po
            gen = head_proj(h + 1) if h < 7 else iter(())
            units = [(qt, kc) for qt in range(4) for kc in range(16)]
            sb = {}
            LA = 3
            for n_ in range(LA):
                sb[n_] = s_issue(h, *units[n_])
            bo = None
            for n_, (qt, kc) in enumerate(units):
                if kc == 0:
                    bo = nbank((4, 2))
                bs = sb.pop(n_)
                i = cnt["ip"] % 4
                cnt["ip"] += 1
                act(PT[i][:], bank(bs), AF.Exp, r=[PB[bs]], w=[tpt[i]], scale=scale)
                mm(bank(bo), Va[hb][:, kc, :], PT[i][:], kc == 0, kc == 15, r=[tva[hb], tpt[i]], w=[PB[bo]])
                if n_ + LA < len(units):
                    sb[n_ + LA] = s_issue(h, *units[n_ + LA])
                if kc == 15:
                    j = cnt["ir"] % 2
                    cnt["ir"] += 1
                    dve("reciprocal", rden[j][dpo:dpo + 64, :], bank(bo)[dpo:dpo + 64, :], r=[PB[bo]], w=[trd[j]])

                    def epi(bo=bo, j=j, po=po, dpo=dpo, h=h, qt=qt):
                        br = nbank((6, 2))
                        mm(bank(br), shiftm[dpo:dpo + 64, :], rden[j][dpo:dpo + 64, :], True, True, r=[TC, trd[j]], w=[PB[br]])
                        dve("tensor_copy", rsh[j][po:po + 64, :], bank(br)[po:po + 64, :], r=[PB[br]], w=[trs_[j]])
                        dve("tensor_tensor", OT[po:po + 64, h // 2, tsl(qt)], bank(bo)[po:po + 64, :], rsh[j][po:po + 64, :], ALU.mult,
                            r=[PB[bo], trs_[j]], w=[tot[h // 2][qt]])
                    pending.append(epi)
                if kc == 8 and pending:
                    pending.pop(0)()
                if kc == 3:
                    next(gen, None)
        while pending:
            pending.pop(0)()
        tots = Tok()
        S.dma("sp", ot_d, OT[:].rearrange("p a b -> p (a b)"), r=[t for row in tot for t in row], w=[tots])
        S.barrier()
        if EVEN_STOP < 2:
            return
        rmsnorm_fm(Bump(arena, K48, ARENA), hT, h_toks, 8, D, lambda c: gcol(G_MIX + layer, c), xn, txn)
        S.barrier()
        bc = Bump(arena, K48, K96)
        xbc = bc.alloc((12, SEQ), BF16)
        txb = [Tok() for _ in range(16)]
        bt = Bump(arena, K96, ARENA)
        raw = [bt.alloc((SEQ + 4,), F32) for _ in range(2)]
        traw = [Tok() for _ in range(2)]
        acc = bt.alloc((SEQ,), F32)
        tacc = Tok()
        ring = Ring(bt, 3, 2048)
        for i in range(2):
            dve("memset", raw[i][:, 0:2], 0.0, r=[], w=[traw[i]])
            dve("memset", raw[i][:, SEQ + 2:SEQ + 4], 0.0, r=[], w=[traw[i]])
        for oc in range(12):
            i = oc % 2
            slot, tk = ring.next()
            wv = slot[:, 0:2048].bitcast(BF16).rearrange("p (kc n) -> p kc n", kc=8)
            S.dma("pool", wv, win[:, :, 1024 + oc * 128:1024 + (oc + 1) * 128], w=[tk])
            for tt in range(4):
                bk = nbank((0, 4))
                for kc in range(8):
                    mm(bank(bk), wv[:, kc, :], xn[:, kc, tsl(tt)], kc == 0, kc == 7, r=[tk, txn[tt]], w=[PB[bk]])
                act(raw[i][:, 2 + tt * 512:2 + (tt + 1) * 512], bank(bk), AF.Copy, r=[PB[bk]], w=[traw[i]])
            dve("tensor_scalar", acc[:], raw[i][:, 0:SEQ], convw[:, e, oc, 0:1], 0.0, ALU.mult, ALU.add, r=[traw[i], TC], w=[tacc])
            for t in range(1, 5):
                dve("scalar_tensor_tensor", acc[:], raw[i][:, t:t + SEQ], convw[:, e, oc, t:t + 1], acc[:], ALU.mult, ALU.add,
                    r=[traw[i], TC, tacc], w=[tacc])
            act(xbc[:, oc, :], acc[:], AF.Silu, r=[tacc, TC], w=txb, bias=convb[:, e, oc:oc + 1])
        S.barrier()
        if EVEN_STOP < 3:
            return
        bs_ = Bump(arena, K96, K104)
        dt = bs_.alloc((16, 32), F32)
        la = bs_.alloc((16, 32), F32)
        cum = bs_.alloc((16, 32), F32)
        ecum = bs_.alloc((16, 32), F32)
        dtb = bs_.alloc((32,), F32)
        eal = bs_.alloc((32,), F32)
        dsk = bs_.alloc((16,), F32)
        tsm = Tok()
        bt = Bump(arena, K104, ARENA)
        wz = Bump(arena, 0, K16).alloc((8, 1024), BF16)
        wdt = bt.alloc((8, 32), BF16)
        twz = Tok()
        zt = [bt.alloc((1024,), BF16) for _ in range(2)]
        tzt = [Tok() for _ in range(2)]
        S.dma("pool", wz, win[:, :, 0:1024], w=[twz])
        S.dma("pool", wdt, win[:, :, 2560:2592], w=[twz])
        S.dma("sp", dtb[:], dtb_d[e:e + 1, :].to_broadcast([128, 32]), w=[tsm])
        S.dma("sp", eal[:], alog_d[e:e + 1, :].to_broadcast([128, 32]), w=[tsm])
        S.dma("sp", dsk[:], ssdd_d[e:e + 1, :].to_broadcast([128, 16]), w=[tsm])
        tzs = [Tok() for _ in range(16)]
        for sc in range(16):
            i = sc % 2
            for ct in range(2):
                bk = nbank((0, 4))
                for kc in range(8):
                    mm(bank(bk), xn[:, kc, sc * 128:(sc + 1) * 128], wz[:, kc, ct * 512:(ct + 1) * 512], kc == 0, kc == 7,
                       r=[twz, txn[sc // 4]], w=[PB[bk]])
                act(zt[i][:, ct * 512:(ct + 1) * 512], bank(bk), AF.Silu, r=[PB[bk]], w=[tzt[i]])
            S.dma("sp", zs_d[sc * 128:(sc + 1) * 128, :], zt[i][:], r=[tzt[i]], w=[tzs[sc]])
            bk = nbank((4, 2))
            for kc in range(8):
                mm(bank(bk)[:, 0:32], xn[:, kc, sc * 128:(sc + 1) * 128], wdt[:, kc, :], kc == 0, kc == 7, r=[twz, txn[sc // 4]], w=[PB[bk]])
            dve("tensor_tensor", dt[:, sc, :], bank(bk)[:, 0:32], dtb[:], ALU.add, r=[PB[bk], tsm], w=[tsm])
        act(dt[:], dt[:], AF.Exp, r=[tsm], w=[tsm])
        act(dt[:], dt[:], AF.Ln, r=[tsm], w=[tsm], bias=1.0)
        act(eal[:], eal[:], AF.Exp, r=[tsm], w=[tsm])
        dve("scalar_tensor_tensor", la[:], dt[:], -1.0, eal[:].unsqueeze(1).to_broadcast([128, 16, 32]), ALU.mult, ALU.mult, r=[tsm], w=[tsm])
        for d_ in range(2):
            bk = nbank((4, 2))
            mm(bank(bk)[:, 0:256].rearrange("p (a b) -> p a b", b=16), tri[:, d_, :], la[:, :, d_ * 16:(d_ + 1) * 16], True, True, r=[tsm, TC], w=[PB[bk]])
            dve("tensor_copy", cum[:, :, d_ * 16:(d_ + 1) * 16], bank(bk)[:, 0:256].rearrange("p (a b) -> p a b", b=16), r=[PB[bk]], w=[tsm])
        act(ecum[:], cum[:], AF.Exp, r=[tsm], w=[tsm])
        negc = la
        totr = bs_.alloc((16, 32), F32)
        dtw = bs_.alloc((16, 32), F32)
        etot = bs_.alloc((16, 32), F32)
        selt = bs_.alloc((2, 128), F32)
        S.dma("sp", selt, sel_d, w=[tsm])
        for d_ in range(2):
            bk = nbank((4, 2))
            mm(bank(bk)[:, 0:256].rearrange("p (a b) -> p a b", b=16), selt[:, d_, :], cum[:, :, d_ * 16:(d_ + 1) * 16], True, True, r=[tsm], w=[PB[bk]])
            dve("tensor_copy", totr[:, :, d_ * 16:(d_ + 1) * 16], bank(bk)[:, 0:256].rearrange("p (a b) -> p a b", b=16), r=[PB[bk]], w=[tsm])
        dve("tensor_tensor", dtw[:], totr[:], cum[:], ALU.subtract, r=[tsm], w=[tsm])
        act(dtw[:], dtw[:], AF.Exp, r=[tsm], w=[tsm])
        dve("tensor_tensor", dtw[:], dtw[:], dt[:], ALU.mult, r=[tsm], w=[tsm])
        act(etot[:], totr[:], AF.Exp, r=[tsm], w=[tsm])
        dve("tensor_scalar", negc[:], cum[:], -1.0, None, ALU.mult, r=[tsm], w=[tsm])
        S.barrier()
        if EVEN_STOP < 4:
            return
        bt1 = Bump(arena, 0, K48)
        bt = Bump(arena, K104, ARENA)
        Dm = bt1.alloc((16, 128), BF16)
        tdm = Tok()
        dve("tensor_tensor", Dm[:], identb[:].unsqueeze(1).to_broadcast([128, 16, 128]), dsk[:].unsqueeze(2).to_broadcast([128, 16, 128]), ALU.mult,
            r=[TC, tsm], w=[tdm])
        XT1 = bt1.alloc((1024,), BF16)
        txt1 = Tok()
        BT1 = bt1.alloc((256,), BF16)
        tbt1 = Tok()
        xdtF = bt1.alloc((1024,), BF16)
        xdtB = bt1.alloc((1024,), BF16)
        xw = bt1.alloc((1024,), BF16)
        txdF, txdB, txw = Tok(), Tok(), Tok()
        CBm = bt1.alloc((2, 2, 128), F32)
        tcb = [Tok(), Tok()]
        Eb = [bt1.alloc((4, 128), F32) for _ in range(2)]
        teb = [Tok() for _ in range(2)]
        MT = [bt1.alloc((16, 128), BF16) for _ in range(2)]
        tmt = [[Tok() for _ in range(4)] for _ in range(2)]
        Hf = bt1.alloc((1024,), F32)
        Hb = bt1.alloc((1024,), BF16)
        thf, thb = Tok(), Tok()
        hinl = bt1.alloc((1024,), BF16)
        thin = Tok()
        t1 = bt1.alloc((1024,), F32)
        yv = bt1.alloc((1024,), F32)
        tt1, tyv = Tok(), Tok()
        zl = bt.alloc((1024,), BF16)
        tzl = Tok()
        ynb = bt.alloc((1024,), BF16)
        tynb = Tok()
        junk = xw
        ssq = bt.alloc((1,), F32)
        tss = Tok()
        thd = [Tok() for _ in range(16)]
        pb7 = bank(7).bitcast(BF16)
        pb6 = bank(6).bitcast(BF16)
        h3 = lambda v: v.rearrange("p (h c) -> p h c", c=64)

        def make_xt(sc):
            csl = slice(sc * 128, (sc + 1) * 128)
            for c in range(8):
                tr(pb7[:, c * 128:(c + 1) * 128], xbc[:, c, csl], identb[:], r=[txb[sc], TC], w=[PB[7]])
            dve("tensor_copy", XT1[:], pb7[:, :], r=[PB[7]], w=[txt1])
            for g in range(2):
                tr(pb6[:, g * 128:(g + 1) * 128], xbc[:, 8 + g, csl], identb[:], r=[txb[sc], TC], w=[PB[6]])
            dve("tensor_copy", BT1[:], pb6[:, 0:256], r=[PB[6]], w=[tbt1])

        def state_update(sc, d_, first, split=False):
            S.op("pool", lambda e: e.tensor_tensor(h3(xw[:]), h3(XT1[:]), dtw[:, sc, d_ * 16:(d_ + 1) * 16].unsqueeze(2).to_broadcast([128, 16, 64]), ALU.mult),
                 r=[txt1, tsm], w=[txw])
            bks = []
            for g in range(2):
                bk = (2 + g) if split else nbank((4, 2))
                bks.append(bk)
                mm(bank(bk), BT1[:, g * 128:(g + 1) * 128], xw[:, g * 512:(g + 1) * 512], True, True, r=[tbt1, txw], w=[PB[bk]])
            if not first:
                S.op("pool", lambda e: e.tensor_tensor(h3(Hf[:]), h3(Hf[:]), etot[:, sc, d_ * 16:(d_ + 1) * 16].unsqueeze(2).to_broadcast([128, 16, 64]), ALU.mult),
                     r=[thf, tsm], w=[thf])
            def fin():
                for g in range(2):
                    gs = slice(g * 512, (g + 1) * 512)
                    if first:
                        dve("tensor_copy", Hf[:, gs], bank(bks[g]), r=[PB[bks[g]]], w=[thf])
                    else:
                        dve("tensor_tensor", Hf[:, gs], Hf[:, gs], bank(bks[g]), ALU.add, r=[thf, PB[bks[g]]], w=[thf])
                act(Hb[:], Hf[:], AF.Copy, r=[thf], w=[thb])
            if split:
                return fin
            fin()

        for sc in range(15):
            make_xt(sc)
            state_update(sc, 0, sc == 0)
            S.dma("sp", hin_d[sc + 1], Hb[:], r=[thb], w=[thd[sc + 1]])
        if EVEN_STOP == 41:
            return
        t1s = [t1, bt.alloc((1024,), F32)]
        t1Bs = [bt1.alloc((1024,), F32), bt.alloc((1024,), F32)]
        zls = [zl, bt.alloc((1024,), BF16)]
        tt1s, tt1bs, tzls = [Tok(), Tok()], [Tok(), Tok()], [Tok(), Tok()]
        lnt = bt.alloc((1,), F32)

        def front_a(sc):
            csl = slice(sc * 128, (sc + 1) * 128)
            hasF, hasB = sc >= 1, sc <= 14
            par = sc % 2
            t1, t1B, zl, tt1, tt1b, tzl = t1s[par], t1Bs[par], zls[par], tt1s[par], tt1bs[par], tzls[par]
            if hasF:
                S.dma("sp", hinl[:], hin_d[sc], r=[thd[sc]], w=[thin])
            S.dma("sp", zl[:], zs_d[csl, :], r=[tzs[sc]], w=[tzl])
            bkc = nbank((4, 2))
            for g in range(2):
                mm(bank(bkc)[:, g * 128:(g + 1) * 128], xbc[:, 8 + g, csl], xbc[:, 10 + g, csl], True, True, r=[txb[sc]], w=[PB[bkc]])
            for d_ in range(2):
                dve("tensor_tensor", CBm[:, d_], bank(bkc)[:, 0:256].rearrange("p (g l) -> p g l", g=2),
                    tri[:, d_, :].unsqueeze(1).to_broadcast([128, 2, 128]), ALU.mult, r=[PB[bkc], TC], w=[tcb[d_]])
            ie = [0]

            def seg_group(d_, hb):
                g = hb // 2
                bk = nbank((4, 2))
                for i in range(4):
                    h = hb * 4 + i
                    tr(bank(bk)[:, i * 128:(i + 1) * 128], cum[:, sc, d_ * 16 + h:d_ * 16 + h + 1].to_broadcast([128, 128]), identf[:],
                       r=[tsm, TC], w=[PB[bk]])
                k = ie[0] % 2
                ie[0] += 1
                for i in range(4):
                    h = hb * 4 + i
                    S.op("act", lambda e, k=k, i=i, bk=bk, h=h: e.activation(out=Eb[k][:, i, :], in_=bank(bk)[:, i * 128:(i + 1) * 128], func=AF.Exp,
                                                                     bias=negc[:, sc, d_ * 16 + h:d_ * 16 + h + 1]),
                         r=[PB[bk], tsm], w=[teb[k]], nowaw=True)
                dve("scalar_tensor_tensor", MT[d_][:, hb * 4:(hb + 1) * 4, :], Eb[k][:], 1.0, CBm[:, d_, g, :].unsqueeze(1).to_broadcast([128, 4, 128]),
                    ALU.min, ALU.mult, r=[teb[k], tcb[d_]], w=[tmt[d_][hb]])

            groups = [(d_, hb) for d_ in range(2) for hb in range(4)]
            seg_group(*groups[0])
            seg_group(*groups[1])
            make_xt(sc)
            S.op("pool", lambda e: e.tensor_tensor(h3(xdtF[:]), h3(XT1[:]), dt[:, sc, 0:16].unsqueeze(2).to_broadcast([128, 16, 64]), ALU.mult),
                 r=[txt1, tsm], w=[txdF])
            S.op("pool", lambda e: e.tensor_tensor(h3(xdtB[:]), h3(XT1[:]), dt[:, sc, 16:32].unsqueeze(2).to_broadcast([128, 16, 64]), ALU.mult),
                 r=[txt1, tsm], w=[txdB])
            seg_group(*groups[2])
            seg_group(*groups[3])
            for g in range(2):
                gs = slice(g * 512, (g + 1) * 512)
                if hasF:
                    mm(bank(2), xbc[:, 10 + g, csl], hinl[:, gs], True, True, r=[txb[sc], thin], w=[PB[2]])
                    dve("tensor_tensor", h3(t1[:, gs]), h3(bank(2)), ecum[:, sc, g * 8:(g + 1) * 8].unsqueeze(2).to_broadcast([128, 8, 64]), ALU.mult,
                        r=[PB[2], tsm], w=[tt1], nowaw=True)
                if hasB:
                    mm(bank(3), xbc[:, 10 + g, csl], Hb[:, gs], True, True, r=[txb[sc], thb], w=[PB[3]])
                    dve("tensor_tensor", h3(t1B[:, gs]), h3(bank(3)), ecum[:, sc, 16 + g * 8:16 + (g + 1) * 8].unsqueeze(2).to_broadcast([128, 8, 64]), ALU.mult,
                        r=[PB[3], tsm], w=[tt1b], nowaw=True)
            seg_group(*groups[4])
            fin = None
            if sc >= 1:
                fin = state_update(sc, 1, sc == 15, split=True)
            seg_group(*groups[5])
            seg_group(*groups[6])
            seg_group(*groups[7])
            if fin is not None:
                fin()

        def front_b(sc):
            for h in range(16):
                hs = slice(h * 64, (h + 1) * 64)
                mm(ps[:, 0:1024][:, hs], MT[0][:, h, :], xdtF[:, hs], True, False, r=[tmt[0][h // 4], txdF], w=[PB[h // 8]])
                mm(ps[:, 0:1024][:, hs], MT[1][:, h, :], xdtB[:, hs], False, False, r=[tmt[1][h // 4], txdB], w=[PB[h // 8]])
                mm(ps[:, 0:1024][:, hs], Dm[:, h, :], XT1[:, hs], False, True, r=[tdm, txt1], w=[PB[h // 8]])

        def back(sc):
            csl = slice(sc * 128, (sc + 1) * 128)
            hasF, hasB = sc >= 1, sc <= 14
            par = sc % 2
            t1, t1B, zl, tt1, tt1b, tzl = t1s[par], t1Bs[par], zls[par], tt1s[par], tt1bs[par], tzls[par]
            for g in range(2):
                gs = slice(g * 512, (g + 1) * 512)
                if hasF:
                    dve("tensor_tensor", yv[:, gs], t1[:, gs], bank(g), ALU.add, r=[tt1, PB[g]], w=[tyv], nowaw=True)
                else:
                    dve("tensor_copy", yv[:, gs], bank(g), r=[PB[g]], w=[tyv], nowaw=True)
            if hasB:
                S.op("pool", lambda e: e.tensor_tensor(yv[:], yv[:], t1B[:], ALU.add), r=[tt1b, tyv], w=[tyv])
            S.op("pool", lambda e: e.tensor_tensor(yv[:], yv[:], zl[:], ALU.mult), r=[tyv, tzl], w=[tyv])
            act(junk[:], yv[:], AF.Square, r=[tyv], w=[tss, txw], accum_out=ssq[:])
            act(lnt[:], ssq[:], AF.Ln, r=[tss, TC], w=[tss], scale=1.0 / 1024, bias=epst[:])
            act(ssq[:], lnt[:], AF.Exp, r=[tss], w=[tss], scale=-0.5)
            dve("tensor_scalar", ynb[:], yv[:], ssq[:, 0:1], 0.0, ALU.mult, ALU.add, r=[tyv, tss], w=[tynb])
            for c in range(8):
                tr(pb7[:, c * 128:(c + 1) * 128], ynb[:, c * 128:(c + 1) * 128], identb[:], r=[tynb, TC], w=[PB[7]])
            dve("tensor_tensor", xbc[:, 0:8, csl], pb7[:, :].rearrange("p (c l) -> p c l", c=8),
                gains[:, (G_SSD + e) * 8:(G_SSD + e) * 8 + 8].unsqueeze(2).to_broadcast([128, 8, 128]), ALU.mult,
                r=[PB[7], TC], w=[txb[sc]])

        front_a(15)
        front_b(15)
        for sc in range(14, -1, -1):
            front_a(sc)
            back(sc + 1)
            front_b(sc)
        back(0)
        S.barrier()
        if EVEN_STOP < 5:
            return
        OT = Bump(arena, 0, K16).alloc((4, SEQ), BF16)
        tot = [[Tok() for _ in range(4)] for _ in range(4)]
        S.dma("sp", OT[:].rearrange("p a b -> p (a b)"), ot_d, r=[tots], w=[t for row in tot for t in row])
        ring = Ring(Bump(arena, K96, ARENA), 3, 6144)
        proj_accum(None, wout_d, 12, None, lambda j, tt: (txb[tt * 4:tt * 4 + 4] if j < 8 else [tot[j - 8][tt]]), 1.0, ring, ybanks=(0, 4),
                   act_view=lambda j, tt: (xbc[:, j, tsl(tt)] if j < 8 else OT[:, j - 8, tsl(tt)]))

    if plan is None:
        plan = []
        for layer in range(DEPTH):
            plan += [("ffn1", layer), ("mix", layer), ("xa", layer), ("ffn2", layer)]
    for s in range(nseq):
        load_x(s)
        if any(p[0] == "xa" for p in plan):
            prep_mem(s)
        if any(p[0] == "mix" and p[1] % 2 == 0 for p in plan):
            rope_tables(s)
        for kind, layer in plan:
            if kind == "ffn1":
                ffn(dr["ffn1_w_gu"][layer], dr["ffn1_w_down"][layer], G_FFN1 + layer)
            elif kind == "ffn2":
                ffn(dr["ffn2_w_gu"][layer], dr["ffn2_w_down"][layer], G_FFN2 + layer)
            elif kind == "xa":
                xattn(layer)
            elif kind == "mix":
                if layer % 2 == 1:
                    fnet(layer)
                else:
                    even_mixer(layer // 2, layer)
        final(s)
    S.barrier()
    S.emit()
    return nc


def _fm(v):
    return np.ascontiguousarray(np.asarray(v, np.float32).reshape(-1, 128).T)


def host_consts(inp):
    c = {}
    g = np.zeros((128, NGAIN * 8), np.float32)

    def put(idx, v):
        g[:, idx * 8:(idx + 1) * 8] = _fm(v)

    for l in range(4):
        put(G_FFN1 + l, inp["ffn1_norm"][l])
        put(G_MIX + l, inp["mix_norm"][l])
        put(G_XA + l, inp["xa_norm"][l])
        put(G_FFN2 + l, inp["ffn2_norm"][l])
    put(G_FINAL, inp["final_norm"])
    put(G_MEM, inp["mem_norm"])
    for e in range(2):
        put(G_SSD + e, inp["ssd_norm"][e])
    c["gains"] = g
    q = np.zeros((128, 2, 6), np.float32)
    for e in range(2):
        q[:, e, 0:4] = _fm(inp["q_norm"][e])
        q[:, e, 4:6] = _fm(inp["kv_norm"][e])
    c["qkvgains"] = q
    cw = np.asarray(inp["conv_w"], np.float32)
    c["convw"] = np.ascontiguousarray(cw.reshape(2, 5, 12, 128).transpose(3, 0, 2, 1))
    cb = np.asarray(inp["conv_b"], np.float32)
    c["convb"] = np.ascontiguousarray(cb.reshape(2, 12, 128).transpose(2, 0, 1))
    c["dt_bias"] = np.ascontiguousarray(np.asarray(inp["dt_bias"], np.float32).reshape(2, 32))
    c["a_log"] = np.ascontiguousarray(np.asarray(inp["a_log"], np.float32).reshape(2, 32))
    c["ssd_d"] = np.ascontiguousarray(np.asarray(inp["ssd_d"], np.float32))
    c["ident"] = np.eye(128, dtype=np.float32)
    sel = np.zeros((128, 2, 128), np.float32)
    sel[127, 0, :] = 1.0
    sel[0, 1, :] = 1.0
    c["sel"] = sel
    s_ = np.arange(128)
    tri = np.zeros((128, 2, 128), np.float32)
    tri[:, 0, :] = (s_[:, None] <= s_[None, :])
    tri[:, 1, :] = (s_[:, None] >= s_[None, :])
    c["tri"] = tri
    inv = 1.0 / (10000.0 ** (np.arange(0, 32, 2, dtype=np.float32) / 32.0))
    c["invf"] = np.concatenate([inv, inv]).astype(np.float32).reshape(32, 1)
    c["shiftm"] = np.ascontiguousarray(np.roll(np.eye(128, dtype=np.float32), 64, axis=1))
    j = np.arange(256)
    angc = 2 * np.pi * ((j[:, None] * j[None, :]) % 256) / 256.0
    cd = np.concatenate([np.cos(angc), np.sin(angc)], axis=1)
    c["cdft"] = np.ascontiguousarray(cd.reshape(2, 128, 512).transpose(1, 0, 2)).astype(ml_dtypes.bfloat16)
    k = np.arange(SEQ)
    angs = 2 * np.pi * ((k[:, None] * k[None, :]) % SEQ) / float(SEQ)
    sd = np.stack([np.cos(angs), -np.sin(angs)], axis=1)
    c["sdft"] = np.ascontiguousarray(sd).astype(ml_dtypes.bfloat16)
    return c


_CACHE = {}


def kernel(**inputs):
    inp = {k: np.asarray(v) for k, v in inputs.items()}
    if "nc" not in _CACHE:
        _CACHE["nc"] = build_program()
    nc = _CACHE["nc"]
    consts = host_consts(inp)
    shared = {name: np.ascontiguousarray(inp[name], dtype=np.float32) for name, _ in WEIGHT_SPECS}
    shared.update(consts)
    in_maps = []
    for c in range(NCORES):
        m = dict(shared)
        m["x"] = np.ascontiguousarray(inp["x"][c * NSEQ:(c + 1) * NSEQ], dtype=np.float32)
        m["mem"] = np.ascontiguousarray(inp["mem"][c * NSEQ:(c + 1) * NSEQ], dtype=np.float32)
        m["positions"] = np.ascontiguousarray(inp["positions"][c * NSEQ:(c + 1) * NSEQ], dtype=np.int32)
        in_maps.append(m)
    res = run_bass_kernel_spmd(nc, in_maps, core_ids=list(range(NCORES)))
    return np.concatenate([np.asarray(r["out"], np.float32) for r in res.results], axis=0)
```

```python
import math
import numpy as np
import ml_dtypes
import concourse.bass as bass
import concourse.mybir as mybir
from concourse.bass_utils import run_bass_kernel_spmd

F32 = mybir.dt.float32
BF16 = mybir.dt.bfloat16
I32 = mybir.dt.int32
U8 = mybir.dt.uint8
AF = mybir.ActivationFunctionType
ALU = mybir.AluOpType

NCORES = 8
NSEQ = 2
SEQ = 2048
D = 1024
DFF = 2816
NMEM = 256
EPS = 1e-6
DEPTH = 4
DBG = set()
SW2 = 99
EVEN_STOP = 99

ENGS = ("pe", "act", "dve", "pool", "sp")
NDSEM = {"sp": 6, "act": 3, "pool": 6}


class Tok:
    __slots__ = ("w", "rs")

    def __init__(self):
        self.w = None
        self.rs = {}


class Op:
    __slots__ = ("eng", "fn", "idx", "key", "val", "waits", "sig", "sigval", "K", "dma")


class Sched:
    def __init__(self, nc):
        self.nc = nc
        self.ops = {e: [] for e in ENGS}
        self.cnt = {e: 0 for e in ENGS}
        self.K = {e: {} for e in ENGS}
        self.dcnt = {}
        self.drr = {q: 0 for q in NDSEM}
        self.last = {}

    def _add(self, eng, fn, deps, dma):
        op = Op()
        op.eng, op.fn, op.dma, op.sig, op.sigval = eng, fn, dma, False, 0
        self.cnt[eng] += 1
        op.idx = self.cnt[eng]
        if dma:
            slot = self.drr[eng]
            self.drr[eng] = (slot + 1) % NDSEM[eng]
            op.key = (eng, slot)
            self.dcnt[op.key] = self.dcnt.get(op.key, 0) + 1
            op.val = self.dcnt[op.key]
        else:
            op.key = eng
            op.val = op.idx
        K = self.K[eng]
        waits = []
        for d in sorted(deps, key=lambda d: -d.val):
            if eng == "pe" and d.eng == "pe" and not d.dma:
                continue
            if K.get(d.key, 0) >= d.val:
                continue
            waits.append(d)
            d.sig = True
            for k, v in d.K.items():
                if K.get(k, 0) < v:
                    K[k] = v
            if K.get(d.key, 0) < d.val:
                K[d.key] = d.val
        op.waits = waits
        op.K = dict(K)
        self.ops[eng].append(op)
        if fn is not None:
            self.last[op.key] = op
        return op

    def op(self, eng, fn, r=(), w=(), dma=False, nowaw=False):
        deps = set()
        for t in r:
            if t.w is not None:
                deps.add(t.w)
        for t in w:
            if t.w is not None and not (nowaw and t.w.eng == eng and not t.w.dma):
                deps.add(t.w)
            for o in t.rs.values():
                deps.add(o)
        op = self._add(eng, fn, deps, dma)
        for t in r:
            t.rs[op.key] = op
        for t in w:
            t.w = op
            t.rs = {}
        return op

    def dma(self, q, out, in_, r=(), w=()):
        return self.op(q, lambda e: e.dma_start(out=out, in_=in_), r, w, dma=True)

    def barrier(self):
        lasts = set(self.last.values())
        for e in ENGS:
            self._add(e, None, lasts, False)

    def emit(self):
        nc = self.nc
        from contextlib import ExitStack
        with ExitStack() as es:
            esem = {e: es.enter_context(nc.semaphore("s_" + e)) for e in ENGS}
            dsem = {}
            for q, n in NDSEM.items():
                for i in range(n):
                    dsem[(q, i)] = es.enter_context(nc.semaphore("d_%s%d" % (q, i)))
            for e in ENGS:
                c = 0
                for op in self.ops[e]:
                    if op.dma:
                        continue
                    if op.sig and op.fn is not None:
                        c += 1
                    op.sigval = c

            def run(e, eng):
                for op in self.ops[e]:
                    for d in op.waits:
                        if d.dma:
                            eng.wait_ge(dsem[d.key], 16 * d.val)
                        elif d.sigval > 0:
                            eng.wait_ge(esem[d.eng], d.sigval)
                    if op.fn is None:
                        continue
                    ins = op.fn(eng)
                    if op.dma:
                        ins.then_inc(dsem[op.key], 16)
                    elif op.sig:
                        ins.then_inc(esem[e], 1)

            block = es.enter_context(nc.Block())

            @block.tensor
            def _(eng):
                run("pe", eng)

            @block.scalar
            def _(eng):
                run("act", eng)

            @block.vector
            def _(eng):
                run("dve", eng)

            @block.gpsimd
            def _(eng):
                run("pool", eng)

            @block.sync
            def _(eng):
                run("sp", eng)


WEIGHT_SPECS = [
    ("ffn1_w_gu", [4, 1024, 5632]), ("ffn1_w_down", [4, 2816, 1024]),
    ("xa_wq", [4, 1024, 1024]), ("xa_wkv", [4, 1024, 2048]), ("xa_wo", [4, 1024, 1024]),
    ("ffn2_w_gu", [4, 1024, 5632]), ("ffn2_w_down", [4, 2816, 1024]),
    ("w_in", [2, 1024, 3392]), ("w_uq", [2, 512, 768]), ("w_ukv", [2, 256, 1024]),
    ("w_out", [2, 1536, 1024]), ("fnet_w_out", [2, 1024, 1024]),
]
G_FFN1, G_MIX, G_XA, G_FFN2, G_FINAL, G_MEM, G_SSD = 0, 4, 8, 12, 16, 17, 18
NGAIN = 20


class Bump:
    def __init__(self, arena, base, limit):
        self.arena, self.off, self.limit = arena, base, limit

    def alloc(self, free_shape, dtype):
        esz = {F32: 4, BF16: 2, I32: 4, U8: 1}[dtype]
        n = esz
        for s in free_shape:
            n *= s
        off = (self.off + 31) // 32 * 32
        assert off + n <= self.limit, ("arena overflow", off, n, self.limit)
        self.off = off + n
        v = self.arena[:, off:off + n]
        if dtype != U8:
            v = v.bitcast(dtype)
        if len(free_shape) == 2:
            v = v.rearrange("p (a b) -> p a b", b=free_shape[1])
        elif len(free_shape) == 3:
            v = v.rearrange("p (a b c) -> p a b c", b=free_shape[1], c=free_shape[2])
        elif len(free_shape) == 4:
            v = v.rearrange("p (a b c d) -> p a b c d", b=free_shape[1], c=free_shape[2], d=free_shape[3])
        return v


def build_program(plan=None, nseq=NSEQ):
    nc = bass.Bass("TRN2", target_bir_lowering=False)
    dr = {}

    def din(name, shape, dt=F32):
        dr[name] = nc.dram_tensor(name, shape, dt, kind="ExternalInput").ap()
        return dr[name]

    x_d = din("x", [nseq, SEQ, D])
    mem_d = din("mem", [nseq, NMEM, D])
    pos_d = din("positions", [nseq, SEQ], I32)
    for name, shape in WEIGHT_SPECS:
        din(name, shape)
    gains_d = din("gains", [128, NGAIN * 8])
    qkvg_d = din("qkvgains", [128, 2, 6])
    convw_d = din("convw", [128, 2, 12, 5])
    convb_d = din("convb", [128, 2, 12])
    dtb_d = din("dt_bias", [2, 32])
    alog_d = din("a_log", [2, 32])
    ssdd_d = din("ssd_d", [2, 16])
    ident_d = din("ident", [128, 128])
    tri_d = din("tri", [128, 2, 128])
    invf_d = din("invf", [32, 1])
    sel_d = din("sel", [128, 2, 128])
    shift_d = din("shiftm", [128, 128])
    cdft_d = din("cdft", [128, 2, 512], BF16)
    sdft_d = din("sdft", [SEQ, 2, SEQ], BF16)
    out_d = nc.dram_tensor("out", [nseq, SEQ, D], F32, kind="ExternalOutput").ap()
    zs_d = nc.dram_tensor("zs_scr", [SEQ, D], BF16).ap()
    hin_d = nc.dram_tensor("hin_scr", [16, 128, 1024], BF16).ap()
    ot_d = nc.dram_tensor("ot_scr", [128, 4 * SEQ], BF16).ap()

    S = Sched(nc)
    A = nc.alloc_sbuf_tensor

    hT = A("hT", [128, 8, SEQ], F32)
    TH = [[Tok() for _ in range(4)] for _ in range(8)]
    identf = A("identf", [128, 128], F32)
    identb = A("identb", [128, 128], BF16)
    onesb = A("onesb", [128, 128], BF16)
    epst = A("epst", [128, 1], F32)
    gains = A("gains_sb", [128, NGAIN * 8], F32)
    qkvg = A("qkvg_sb", [128, 2, 6], F32)
    convw = A("convw_sb", [128, 2, 12, 5], F32)
    convb = A("convb_sb", [128, 2, 12], F32)
    tri = A("tri_sb", [128, 2, 128], F32)
    invf = A("invf_sb", [128, 1], F32)
    memT = A("memT", [128, 8, NMEM], BF16)
    cos2 = A("cos2", [128, SEQ], BF16)
    sin2 = A("sin2", [128, SEQ], BF16)
    shiftm = A("shiftm_sb", [128, 128], F32)
    TC = Tok()
    TMEM = Tok()
    TROPE = Tok()
    remaining = nc.sbuf_bytes_remaining
    ARENA = (remaining - 256) // 32 * 32
    arena = A("arena", [128, ARENA], U8)
    ps = nc.alloc_psum_tensor("ps", [128, 4096], F32)
    PB = [Tok() for _ in range(8)]

    def bank(i):
        return ps[:, i * 512:(i + 1) * 512]

    def mm(out, lhsT, rhs, start, stop, r, w):
        S.op("pe", lambda e: e.matmul(out, lhsT, rhs, start=start, stop=stop), r=r, w=w)

    def tr(out, in_, idn, r, w):
        S.op("pe", lambda e: e.transpose(out, in_, idn), r=r, w=w)

    def act(out, in_, func, r, w, **kw):
        S.op("act", lambda e: e.activation(out=out, in_=in_, func=func, **kw), r=r, w=w)

    def dve(name, *args, r, w, eng="dve", nowaw=False, **kw):
        S.op(eng, lambda e: getattr(e, name)(*args, **kw), r=r, w=w, nowaw=nowaw)

    def tsl(tt):
        return slice(tt * 512, (tt + 1) * 512)

    def gcol(gi, c):
        return gains[:, gi * 8 + c:gi * 8 + c + 1]

    S.dma("sp", identf[:], ident_d, w=[TC])
    S.dma("sp", gains[:], gains_d, w=[TC])
    S.dma("sp", qkvg[:], qkvg_d, w=[TC])
    S.dma("sp", convw[:], convw_d, w=[TC])
    S.dma("sp", convb[:], convb_d, w=[TC])
    S.dma("sp", tri[:], tri_d, w=[TC])
    S.dma("sp", invf[64:96, :], invf_d, w=[TC])
    S.dma("sp", shiftm[:], shift_d, w=[TC])
    dve("tensor_copy", identb[:], identf[:], r=[TC], w=[TC])
    dve("memset", onesb[:], 1.0, r=[], w=[TC])
    dve("memset", epst[:], EPS, r=[], w=[TC])

    class Ring:
        def __init__(self, b, n, nbytes):
            self.slots = [b.alloc((nbytes,), U8) for _ in range(n)]
            self.toks = [Tok() for _ in range(n)]
            self.i = 0

        def next(self):
            k = self.i % len(self.slots)
            self.i += 1
            return self.slots[k], self.toks[k]

    rr = {"n": 6}

    def nbank(group):
        lo, n = group
        k = rr.get(group, 0)
        rr[group] = (k + 1) % n
        return lo + k

    def rmsnorm_fm(b, src, src_toks, nch, dim, gain_fn, out, out_toks, ntt=4, banks=(6, 2), scr_sq=None):
        sq = [b.alloc((nch, 512), BF16) for _ in range(2)]
        tsq = [Tok() for _ in range(2)]
        rs = [b.alloc((512,), F32) for _ in range(2)]
        trs = [Tok() for _ in range(2)]
        act(sq[0][:], src[:, :, tsl(0)], AF.Square, r=src_toks(0), w=[tsq[0]])
        for tt in range(ntt):
            i = tt % 2
            if tt + 1 < ntt:
                act(sq[1 - i][:], src[:, :, tsl(tt + 1)], AF.Square, r=src_toks(tt + 1), w=[tsq[1 - i]])
            bk = nbank(banks)
            for c in range(nch):
                mm(bank(bk), onesb[:], sq[i][:, c, :], c == 0, c == nch - 1, r=[tsq[i], TC], w=[PB[bk]])
            act(rs[i][:], bank(bk), AF.Ln, r=[PB[bk], TC], w=[trs[i]], scale=1.0 / dim, bias=epst[:])
            act(rs[i][:], rs[i][:], AF.Exp, r=[trs[i]], w=[trs[i]], scale=-0.5)
            for c in range(nch):
                dve("scalar_tensor_tensor", out[:, c, tsl(tt)], src[:, c, tsl(tt)], gain_fn(c), rs[i][:], ALU.mult, ALU.mult,
                    r=src_toks(tt) + [trs[i], TC], w=[out_toks[tt]], nowaw=True)

    def h_toks(tt):
        return [TH[c][tt] for c in range(8)]

    def proj_accum(b, wsrc, nkc, actT, act_toks, scale, ring, ybanks=(4, 2), act_view=None):
        for dcp in range(4):
            slot, tk = ring.next()
            wd = slot[:, 0:nkc * 256 * 2].bitcast(BF16).rearrange("p (j n) -> p j n", j=nkc)
            S.dma("pool", wd, wsrc[:, :, dcp * 256:(dcp + 1) * 256], w=[tk])
            for dd in range(2):
                dc = dcp * 2 + dd
                for tt in range(4):
                    by = nbank(ybanks)
                    for j in range(nkc):
                        av = act_view(j, tt) if act_view is not None else actT[:, j, tsl(tt)]
                        at = act_toks(j, tt)
                        mm(bank(by), wd[:, j, dd * 128:(dd + 1) * 128], av, j == 0, j == nkc - 1,
                           r=[tk] + (at if isinstance(at, list) else [at]), w=[PB[by]])
                    dve("scalar_tensor_tensor", hT[:, dc, tsl(tt)], bank(by), float(scale), hT[:, dc, tsl(tt)], ALU.mult, ALU.add,
                        r=[PB[by], TH[dc][tt]], w=[TH[dc][tt]])

    def load_x(s):
        S.barrier()
        b = Bump(arena, 0, ARENA)
        xt = [b.alloc((D,), F32) for _ in range(2)]
        txt = [Tok() for _ in range(2)]
        for t in range(16):
            i = t % 2
            S.dma("sp", xt[i][:], x_d[s, t * 128:(t + 1) * 128, :], w=[txt[i]])
            for cg in range(2):
                bk = nbank((0, 4))
                for ci in range(4):
                    c = cg * 4 + ci
                    tr(bank(bk)[:, ci * 128:(ci + 1) * 128], xt[i][:, c * 128:(c + 1) * 128], identf[:], r=[txt[i], TC], w=[PB[bk]])
                dst = hT[:, cg * 4:(cg + 1) * 4, t * 128:(t + 1) * 128]
                src = bank(bk).rearrange("p (c n) -> p c n", c=4)
                toks = [TH[c][t // 4] for c in range(cg * 4, cg * 4 + 4)]
                if cg == 0:
                    act(dst, src, AF.Copy, r=[PB[bk]], w=toks)
                else:
                    dve("tensor_copy", dst, src, r=[PB[bk]], w=toks)

    def prep_mem(s):
        S.barrier()
        b = Bump(arena, 0, ARENA)
        for t in range(2):
            mt = b.alloc((D,), F32)
            m2 = b.alloc((D,), F32)
            junk = b.alloc((D,), F32)
            ss = b.alloc((1,), F32)
            tk = Tok()
            S.dma("sp", mt[:], mem_d[s, t * 128:(t + 1) * 128, :], w=[tk])
            act(junk[:], mt[:], AF.Square, r=[tk], w=[tk], accum_out=ss[:])
            act(ss[:], ss[:], AF.Sqrt, r=[tk, TC], w=[tk], scale=1.0 / D, bias=epst[:])
            dve("reciprocal", ss[:], ss[:], r=[tk], w=[tk])
            act(m2[:], mt[:], AF.Copy, r=[tk], w=[tk], scale=ss[:])
            for cg in range(2):
                bk = nbank((0, 4))
                for ci in range(4):
                    c = cg * 4 + ci
                    tr(bank(bk)[:, ci * 128:(ci + 1) * 128], m2[:, c * 128:(c + 1) * 128], identf[:], r=[tk, TC], w=[PB[bk]])
                for ci in range(4):
                    c = cg * 4 + ci
                    act(memT[:, c, t * 128:(t + 1) * 128], bank(bk)[:, ci * 128:(ci + 1) * 128], AF.Copy, r=[PB[bk], TC], w=[TMEM],
                        scale=gcol(G_MEM, c))

    def rope_tables(s):
        S.barrier()
        b = Bump(arena, 0, ARENA)
        pi_ = b.alloc((SEQ,), I32)
        pf = b.alloc((SEQ,), F32)
        tk = Tok()
        S.dma("sp", pi_[64:96, :], pos_d[s:s + 1, :].to_broadcast([32, SEQ]), w=[tk])
        dve("tensor_copy", pf[64:96, :], pi_[64:96, :], r=[tk], w=[tk])
        ang = b.alloc((SEQ,), F32)
        t_ = b.alloc((SEQ,), F32)
        ki = b.alloc((SEQ,), I32)
        kf = b.alloc((SEQ,), F32)
        r_ = b.alloc((SEQ,), F32)
        m_ = b.alloc((SEQ,), F32)
        P32 = slice(64, 96)
        dve("tensor_scalar", ang[P32, :], pf[P32, :], invf[P32, 0:1], 0.0, ALU.mult, ALU.add, r=[tk, TC], w=[tk])
        for dst, off in ((sin2, 0.0), (cos2, math.pi / 2)):
            dve("tensor_scalar", t_[P32, :], ang[P32, :], off, 1.0 / (2 * math.pi), ALU.add, ALU.mult, r=[tk], w=[tk])
            dve("tensor_copy", ki[P32, :], t_[P32, :], r=[tk], w=[tk])
            dve("tensor_copy", kf[P32, :], ki[P32, :], r=[tk], w=[tk])
            dve("tensor_scalar", r_[P32, :], ang[P32, :], off, None, ALU.add, r=[tk], w=[tk])
            dve("scalar_tensor_tensor", r_[P32, :], kf[P32, :], -2 * math.pi, r_[P32, :], ALU.mult, ALU.add, r=[tk], w=[tk])
            dve("tensor_scalar", m_[P32, :], r_[P32, :], math.pi, None, ALU.is_gt, r=[tk], w=[tk])
            dve("scalar_tensor_tensor", r_[P32, :], m_[P32, :], -2 * math.pi, r_[P32, :], ALU.mult, ALU.add, r=[tk], w=[tk])
            dve("tensor_scalar", m_[P32, :], r_[P32, :], -math.pi, None, ALU.is_lt, r=[tk], w=[tk])
            dve("scalar_tensor_tensor", r_[P32, :], m_[P32, :], 2 * math.pi, r_[P32, :], ALU.mult, ALU.add, r=[tk], w=[tk])
            act(dst[P32, :], r_[P32, :], AF.Sin, r=[tk], w=[TROPE])

    def ffn(gu, dn, gi):
        S.barrier()
        b = Bump(arena, 0, ARENA)
        xn = b.alloc((8, SEQ), BF16)
        txn = [Tok() for _ in range(4)]
        rmsnorm_fm(Bump(arena, b.off, ARENA), hT, h_toks, 8, D, lambda c: gcol(gi, c), xn, txn)
        S.barrier()
        actb = b.alloc((11, SEQ), BF16)
        ta = [[Tok() for _ in range(4)] for _ in range(11)]
        sg = [b.alloc((512,), F32) for _ in range(2)]
        tsg = [Tok() for _ in range(2)]
        ring = Ring(b, 6, 6144)
        gsrc = gu.rearrange("(kc p) (two n) -> p two kc n", p=128, two=2)
        dsrc = dn.rearrange("(j p) n -> p j n", p=128)
        k = 0
        for half in range(2):
            for jj in range(11):
                j = half * 11 + jj
                slot, tk = ring.next()
                wv = slot[:, 0:4096].bitcast(BF16).rearrange("p (two kc n) -> p two kc n", two=2, kc=8)
                S.dma("pool", wv, gsrc[:, :, :, j * 128:(j + 1) * 128], w=[tk])
                for tt in range(4):
                    bg = nbank((0, 2))
                    bu = nbank((2, 2))
                    for kc in range(8):
                        mm(bank(bg), wv[:, 0, kc, :], xn[:, kc, tsl(tt)], kc == 0, kc == 7, r=[tk, txn[tt]], w=[PB[bg]])
                    for kc in range(8):
                        mm(bank(bu), wv[:, 1, kc, :], xn[:, kc, tsl(tt)], kc == 0, kc == 7, r=[tk, txn[tt]], w=[PB[bu]])
                    i = k % 2
                    k += 1
                    act(sg[i][:], bank(bg), AF.Silu, r=[PB[bg]], w=[tsg[i]])
                    dve("tensor_tensor", actb[:, jj, tsl(tt)], bank(bu), sg[i][:], ALU.mult, r=[PB[bu], tsg[i]], w=[ta[jj][tt]])
            proj_accum(b, dsrc[:, half * 11:(half + 1) * 11, :], 11, actb, lambda j, tt: ta[j][tt], 0.5, ring)

    def xattn(layer):
        S.barrier()
        b = Bump(arena, 0, ARENA)
        xn = b.alloc((8, SEQ), BF16)
        txn = [Tok() for _ in range(4)]
        rmsnorm_fm(Bump(arena, b.off, ARENA), hT, h_toks, 8, D, lambda c: gcol(G_XA + layer, c), xn, txn)
        S.barrier()
        QT = b.alloc((8, SEQ), BF16)
        tq = [[Tok() for _ in range(4)] for _ in range(8)]
        OT = xn
        tot = [[Tok() for _ in range(4)] for _ in range(8)]
        KT = b.alloc((8, NMEM), BF16)
        tkt = Tok()
        Vt = b.alloc((2, D), BF16)
        tv = Tok()
        PT = [b.alloc((512,), BF16) for _ in range(4)]
        tpt = [Tok() for _ in range(4)]
        rden = [b.alloc((512,), F32) for _ in range(2)]
        trd = [Tok() for _ in range(2)]
        ring = Ring(b, 3, 8192)
        wkv = dr["xa_wkv"][layer].rearrange("(kc p) n -> p kc n", p=128)
        wq = dr["xa_wq"][layer].rearrange("(kc p) n -> p kc n", p=128)
        wo = dr["xa_wo"][layer].rearrange("(kc p) n -> p kc n", p=128)
        for ocp in range(4):
            slot, tk = ring.next()
            wv = slot[:, 0:4096].bitcast(BF16).rearrange("p (kc n) -> p kc n", kc=8)
            S.dma("pool", wv, wkv[:, :, ocp * 256:(ocp + 1) * 256], w=[tk])
            for oo in range(2):
                oc = ocp * 2 + oo
                bk = nbank((6, 2))
                for kc in range(8):
                    mm(bank(bk)[:, 0:NMEM], wv[:, kc, oo * 128:(oo + 1) * 128], memT[:, kc, :], kc == 0, kc == 7, r=[tk, TMEM], w=[PB[bk]])
                act(KT[:, oc, :], bank(bk)[:, 0:NMEM], AF.Copy, r=[PB[bk]], w=[tkt])
        for ct in range(2):
            slot, tk = ring.next()
            wv = slot[:, 0:8192].bitcast(BF16).rearrange("p (kc n) -> p kc n", kc=8)
            S.dma("pool", wv, wkv[:, :, D + ct * 512:D + (ct + 1) * 512], w=[tk])
            for kt in range(2):
                bk = nbank((6, 2))
                for kc in range(8):
                    mm(bank(bk), memT[:, kc, kt * 128:(kt + 1) * 128], wv[:, kc, :], kc == 0, kc == 7, r=[tk, TMEM], w=[PB[bk]])
                dve("tensor_copy", Vt[:, kt, ct * 512:(ct + 1) * 512], bank(bk), r=[PB[bk]], w=[tv])
        for ocp in range(4):
            slot, tk = ring.next()
            wv = slot[:, 0:4096].bitcast(BF16).rearrange("p (kc n) -> p kc n", kc=8)
            S.dma("pool", wv, wq[:, :, ocp * 256:(ocp + 1) * 256], w=[tk])
            for oo in range(2):
                oc = ocp * 2 + oo
                for tt in range(4):
                    bk = nbank((6, 2))
                    for kc in range(8):
                        mm(bank(bk), wv[:, kc, oo * 128:(oo + 1) * 128], xn[:, kc, tsl(tt)], kc == 0, kc == 7, r=[tk, txn[tt]], w=[PB[bk]])
                    act(QT[:, oc, tsl(tt)], bank(bk), AF.Copy, r=[PB[bk]], w=[tq[oc][tt]])
        S.barrier()
        scale = 256.0 ** -0.5
        ip = 0
        ir = 0
        units = [(h, tt) for h in range(4) for tt in range(4)]

        def s_issue(h, tt):
            bks = []
            for kc in range(2):
                bk = nbank((0, 4))
                for dc in range(2):
                    mm(bank(bk), KT[:, h * 2 + dc, kc * 128:(kc + 1) * 128], QT[:, h * 2 + dc, tsl(tt)], dc == 0, dc == 1,
                       r=[tkt, tq[h * 2 + dc][tt]], w=[PB[bk]])
                bks.append(bk)
            return bks

        pend = s_issue(*units[0])
        for n_, (h, tt) in enumerate(units):
            bks = pend
            pts = []
            for kc in range(2):
                i = ip % 4
                ip += 1
                act(PT[i][:], bank(bks[kc]), AF.Exp, r=[PB[bks[kc]]], w=[tpt[i]], scale=scale)
                pts.append(i)
            if n_ + 1 < len(units):
                pend = s_issue(*units[n_ + 1])
            bd = nbank((6, 2))
            for kc in range(2):
                mm(bank(bd), onesb[:], PT[pts[kc]][:], kc == 0, kc == 1, r=[tpt[pts[kc]], TC], w=[PB[bd]])
            j = ir % 2
            ir += 1
            act(rden[j][:], bank(bd), AF.Ln, r=[PB[bd]], w=[trd[j]])
            act(rden[j][:], rden[j][:], AF.Exp, r=[trd[j]], w=[trd[j]], scale=-1.0)
            for dvc in range(2):
                bo = nbank((4, 2))
                for kc in range(2):
                    mm(bank(bo), Vt[:, kc, h * 256 + dvc * 128:h * 256 + (dvc + 1) * 128], PT[pts[kc]][:], kc == 0, kc == 1,
                       r=[tv, tpt[pts[kc]]], w=[PB[bo]])
                dve("tensor_tensor", OT[:, h * 2 + dvc, tsl(tt)], bank(bo), rden[j][:], ALU.mult, r=[PB[bo], trd[j]], w=[tot[h * 2 + dvc][tt]])
        proj_accum(b, wo, 8, OT, lambda j, tt: tot[j][tt], 1.0, ring, ybanks=(6, 2))

    def fnet(layer):
        S.barrier()
        b = Bump(arena, 0, ARENA)
        xn = b.alloc((8, SEQ), BF16)
        txn = [Tok() for _ in range(4)]
        rmsnorm_fm(Bump(arena, b.off, ARENA), hT, h_toks, 8, D, lambda c: gcol(G_MIX + layer, c), xn, txn)
        S.barrier()
        Y = b.alloc((16, 4, 512), BF16)
        ty = [Tok() for _ in range(16)]
        cd = b.alloc((2, 512), BF16)
        tcd = Tok()
        ring = Ring(b, 6, 2048)
        ring2 = Ring(b, 2, 4096)
        S.dma("sp", cd[:], cdft_d, w=[tcd])
        k = 0
        for sc in range(16):
            for g in range(4):
                bk = nbank((0, 4))
                for cc in range(2):
                    mm(bank(bk), xn[:, g * 2 + cc, sc * 128:(sc + 1) * 128], cd[:, cc, :], cc == 0, cc == 1, r=[txn[sc // 4], tcd], w=[PB[bk]])
                if k % 2 == 0:
                    act(Y[:, sc, g, :], bank(bk), AF.Copy, r=[PB[bk]], w=[ty[sc]])
                else:
                    dve("tensor_copy", Y[:, sc, g, :], bank(bk), r=[PB[bk]], w=[ty[sc]])
                k += 1
        nrm = 1.0 / math.sqrt(SEQ * 256.0)
        for kt in range(4):
            for sc in range(16):
                slot, tk = ring.next()
                dv = slot[:, 0:2048].bitcast(BF16).rearrange("p (a n) -> p a n", a=2)
                S.dma("sp", dv, sdft_d[sc * 128:(sc + 1) * 128, :, kt * 512:(kt + 1) * 512], w=[tk])
                for o in range(8):
                    g, jc = o // 2, o % 2
                    mm(bank(o), Y[:, sc, g, jc * 128:(jc + 1) * 128], dv[:, 0, :], sc == 0, False, r=[ty[sc], tk], w=[PB[o]])
                    mm(bank(o), Y[:, sc, g, 256 + jc * 128:256 + (jc + 1) * 128], dv[:, 1, :], False, sc == 15, r=[ty[sc], tk], w=[PB[o]])
            for o in range(8):
                if o % 2 == 0:
                    act(xn[:, o, tsl(kt)], bank(o), AF.Copy, r=[PB[o]], w=[txn[kt]], scale=nrm)
                else:
                    dve("tensor_scalar", xn[:, o, tsl(kt)], bank(o), nrm, None, ALU.mult, r=[PB[o]], w=[txn[kt]])
        wsrc = dr["fnet_w_out"][layer // 2].rearrange("(kc p) n -> p kc n", p=128)
        proj_accum(b, wsrc, 8, xn, lambda j, tt: txn[tt], 1.0, ring2, ybanks=(0, 4))

    def final(s):
        S.barrier()
        b = Bump(arena, 0, ARENA)
        sq = [b.alloc((8, 512), BF16) for _ in range(2)]
        tsq = [Tok() for _ in range(2)]
        rs = [b.alloc((512,), F32) for _ in range(2)]
        trs = [Tok() for _ in range(2)]
        xf = [b.alloc((8, 512), F32) for _ in range(2)]
        txf = [Tok() for _ in range(2)]
        yt = [b.alloc((D,), F32) for _ in range(2)]
        tyt = [Tok() for _ in range(2)]
        k = 0
        for tt in range(4):
            i = tt % 2
            act(sq[i][:], hT[:, :, tsl(tt)], AF.Square, r=h_toks(tt), w=[tsq[i]])
            bk = nbank((6, 2))
            for c in range(8):
                mm(bank(bk), onesb[:], sq[i][:, c, :], c == 0, c == 7, r=[tsq[i], TC], w=[PB[bk]])
            act(rs[i][:], bank(bk), AF.Sqrt, r=[PB[bk], TC], w=[trs[i]], scale=1.0 / D, bias=epst[:])
            dve("reciprocal", rs[i][:], rs[i][:], r=[trs[i]], w=[trs[i]])
            for c in range(8):
                dve("scalar_tensor_tensor", xf[i][:, c, :], hT[:, c, tsl(tt)], gcol(G_FINAL, c), rs[i][:], ALU.mult, ALU.mult,
                    r=h_toks(tt) + [trs[i], TC], w=[txf[i]])
            for t4 in range(4):
                t = tt * 4 + t4
                j = k % 2
                k += 1
                for cg in range(2):
                    bk2 = nbank((0, 4))
                    for ci in range(4):
                        c = cg * 4 + ci
                        tr(bank(bk2)[:, ci * 128:(ci + 1) * 128], xf[i][:, c, t4 * 128:(t4 + 1) * 128], identf[:], r=[txf[i], TC], w=[PB[bk2]])
                    if cg == 0:
                        act(yt[j][:, 0:512], bank(bk2), AF.Copy, r=[PB[bk2]], w=[tyt[j]])
                    else:
                        dve("tensor_copy", yt[j][:, 512:1024], bank(bk2), r=[PB[bk2]], w=[tyt[j]])
                S.dma("sp", out_d[s, t * 128:(t + 1) * 128, :], yt[j][:], r=[tyt[j]])

    def norm_tile(nt, src_tile, src_toks, nch, dim, gain_fn, out_view, out_toks, banks=(6, 2)):
        sq, tsq, rs, trs = nt
        i = rr.get("nt", 0)
        rr["nt"] = (i + 1) % 2
        act(sq[i][:, 0:nch, :], src_tile, AF.Square, r=src_toks, w=[tsq[i]])
        bk = nbank(banks)
        for c in range(nch):
            mm(bank(bk), onesb[:], sq[i][:, c, :], c == 0, c == nch - 1, r=[tsq[i], TC], w=[PB[bk]])
        act(rs[i][:], bank(bk), AF.Ln, r=[PB[bk], TC], w=[trs[i]], scale=1.0 / dim, bias=epst[:])
        act(rs[i][:], rs[i][:], AF.Exp, r=[trs[i]], w=[trs[i]], scale=-0.5)
        for c in range(nch):
            dve("scalar_tensor_tensor", out_view[:, c, :], src_tile[:, c, :], gain_fn(c), rs[i][:], ALU.mult, ALU.mult,
                r=src_toks + [trs[i], TC], w=out_toks, nowaw=True)

    def even_mixer(e, layer):
        K16, K48, K76, K96, K104 = 16384, 49152, 77824, 98304, 98304 + 16384
        win = dr["w_in"][e].rearrange("(kc p) n -> p kc n", p=128)
        wuq_d = dr["w_uq"][e].rearrange("(kc p) n -> p kc n", p=128)
        wukv_d = dr["w_ukv"][e].rearrange("(kc p) n -> p kc n", p=128)
        wout_d = dr["w_out"][e].rearrange("(kc p) n -> p kc n", p=128)
        S.barrier()
        b0 = Bump(arena, 0, K16)
        OT = b0.alloc((4, SEQ), BF16)
        tot = [[Tok() for _ in range(4)] for _ in range(4)]
        bx = Bump(arena, K16, K48)
        xn = bx.alloc((8, SEQ), BF16)
        txn = [Tok() for _ in range(4)]
        rmsnorm_fm(Bump(arena, K48, ARENA), hT, h_toks, 8, D, lambda c: gcol(G_MIX + layer, c), xn, txn)
        S.barrier()
        ba = Bump(arena, K48, K76)
        cqn = ba.alloc((4, SEQ), BF16)
        tcq = [Tok() for _ in range(4)]
        ckvn = ba.alloc((2, SEQ), BF16)
        tckv = [Tok() for _ in range(4)]
        kpe = ba.alloc((SEQ,), BF16)
        tkpe = [Tok() for _ in range(4)]
        bt = Bump(arena, K76, ARENA)
        wcq = bt.alloc((8, 512), BF16)
        wckv = bt.alloc((8, 256), BF16)
        wkr = bt.alloc((8, 32), BF16)
        wkrr = bt.alloc((8, 32), BF16)
        tw = Tok()
        S.dma("pool", wcq, win[:, :, 2592:3104], w=[tw])
        S.dma("pool", wckv, win[:, :, 3104:3360], w=[tw])
        S.dma("pool", wkr, win[:, :, 3360:3392], w=[tw])
        dve("tensor_scalar", wkrr[:, :, 0:16], wkr[:, :, 16:32], -1.0, None, ALU.mult, r=[tw], w=[tw])
        dve("tensor_copy", wkrr[:, :, 16:32], wkr[:, :, 0:16], r=[tw], w=[tw])
        nt = ([bt.alloc((4, 512), BF16) for _ in range(2)], [Tok() for _ in range(2)],
              [bt.alloc((512,), F32) for _ in range(2)], [Tok() for _ in range(2)])
        cqr = [bt.alloc((4, 512), BF16) for _ in range(2)]
        tcqr = [Tok() for _ in range(2)]
        ckr = [bt.alloc((2, 512), BF16) for _ in range(2)]
        tckr = [Tok() for _ in range(2)]
        t1 = [bt.alloc((512,), F32) for _ in range(2)]
        t2 = [bt.alloc((512,), F32) for _ in range(2)]
        tt12 = [Tok() for _ in range(2)]
        for tt in range(4):
            i = tt % 2
            for oc in range(4):
                bk = nbank((0, 4))
                for kc in range(8):
                    mm(bank(bk), wcq[:, kc, oc * 128:(oc + 1) * 128], xn[:, kc, tsl(tt)], kc == 0, kc == 7, r=[tw, txn[tt]], w=[PB[bk]])
                act(cqr[i][:, oc, :], bank(bk), AF.Copy, r=[PB[bk]], w=[tcqr[i]])
            norm_tile(nt, cqr[i][:], [tcqr[i]], 4, 512, lambda c: qkvg[:, e, c:c + 1], cqn[:, :, tsl(tt)], [tcq[tt]])
            for oc in range(2):
                bk = nbank((0, 4))
                for kc in range(8):
                    mm(bank(bk), wckv[:, kc, oc * 128:(oc + 1) * 128], xn[:, kc, tsl(tt)], kc == 0, kc == 7, r=[tw, txn[tt]], w=[PB[bk]])
                act(ckr[i][:, oc, :], bank(bk), AF.Copy, r=[PB[bk]], w=[tckr[i]])
            norm_tile(nt, ckr[i][:], [tckr[i]], 2, 256, lambda c: qkvg[:, e, 4 + c:5 + c], ckvn[:, :, tsl(tt)], [tckv[tt]])
            ba_, bb_ = nbank((0, 4)), nbank((0, 4))
            for kc in range(8):
                mm(bank(ba_)[64:96, :], wkr[:, kc, :], xn[:, kc, tsl(tt)], kc == 0, kc == 7, r=[tw, txn[tt]], w=[PB[ba_]])
            for kc in range(8):
                mm(bank(bb_)[64:96, :], wkrr[:, kc, :], xn[:, kc, tsl(tt)], kc == 0, kc == 7, r=[tw, txn[tt]], w=[PB[bb_]])
            dve("tensor_tensor", t1[i][64:96, :], bank(ba_)[64:96, :], cos2[64:96, tsl(tt)], ALU.mult, r=[PB[ba_], TROPE], w=[tt12[i]])
            dve("tensor_tensor", t2[i][64:96, :], bank(bb_)[64:96, :], sin2[64:96, tsl(tt)], ALU.mult, r=[PB[bb_], TROPE], w=[tt12[i]])
            dve("tensor_tensor", kpe[64:96, tsl(tt)], t1[i][64:96, :], t2[i][64:96, :], ALU.add, r=[tt12[i]], w=[tkpe[tt]])
        S.barrier()
        if EVEN_STOP < 1:
            return
        bb = Bump(arena, K16, K48)
        V = bb.alloc((16, 8, 64), BF16)
        tv = Tok()
        wukv = bb.alloc((2, 1024), BF16)
        wuq = bb.alloc((4, 768), BF16)
        wuqr = bb.alloc((4, 8, 32), BF16)
        twb = Tok()
        S.dma("pool", wukv, wukv_d, w=[twb])
        S.dma("pool", wuq, wuq_d, w=[twb])
        for h in range(8):
            dve("tensor_scalar", wuqr[:, :, h, 0:16], wuq[:, :, h * 96 + 80:h * 96 + 96], -1.0, None, ALU.mult, r=[twb], w=[twb])
            dve("tensor_copy", wuqr[:, :, h, 16:32], wuq[:, :, h * 96 + 64:h * 96 + 80], r=[twb], w=[twb])
        bh = Bump(arena, K76, ARENA)
        Qh = [bh.alloc((SEQ,), BF16) for _ in range(2)]
        Kh = [bh.alloc((SEQ,), BF16) for _ in range(2)]
        tqk = [[Tok() for _ in range(4)] for _ in range(2)]
        Va = [bh.alloc((16, 128), BF16) for _ in range(2)]
        tva = [Tok() for _ in range(2)]
        PT = [bh.alloc((512,), BF16) for _ in range(4)]
        tpt = [Tok() for _ in range(4)]
        rden = [bh.alloc((512,), F32) for _ in range(2)]
        trd = [Tok() for _ in range(2)]
        rsh = [bh.alloc((512,), F32) for _ in range(2)]
        trs_ = [Tok() for _ in range(2)]
        u1 = [bh.alloc((512,), F32) for _ in range(2)]
        u2 = [bh.alloc((512,), F32) for _ in range(2)]
        tu = [Tok() for _ in range(2)]
        for sc in range(16):
            for ct in range(2):
                bk = nbank((6, 2))
                for kc in range(2):
                    mm(bank(bk), ckvn[:, kc, sc * 128:(sc + 1) * 128], wukv[:, kc, ct * 512:(ct + 1) * 512], kc == 0, kc == 1,
                       r=[twb, tckv[sc // 4]], w=[PB[bk]])
                src = bank(bk).rearrange("p (h c) -> p h c", h=4)[:, :, 64:128]
                if ct == 0:
                    act(V[:, sc, 0:4, :], src, AF.Copy, r=[PB[bk]], w=[tv])
                else:
                    dve("tensor_copy", V[:, sc, 4:8, :], src, r=[PB[bk]], w=[tv])
        dve("memset", Va[0][:, :, 64:128], 1.0, r=[], w=[tva[0]])
        dve("memset", Va[1][:, :, 0:64], 1.0, r=[], w=[tva[1]])
        for hb in range(2):
            dve("tensor_copy", Kh[hb][64:96, :], kpe[64:96, :], r=tkpe, w=tqk[hb])
        scale = 96.0 ** -0.5
        cnt = {"ip": 0, "ir": 0, "iu": 0}

        def head_proj(h):
            hb = h % 2
            po = hb * 64
            dve("tensor_copy", Va[hb][:, :, po:po + 64], V[:, :, h, :], r=[tv], w=[tva[hb]])
            for tt in range(4):
                bk = nbank((6, 2))
                for kc in range(4):
                    mm(bank(bk)[0:64, :], wuq[:, kc, h * 96:h * 96 + 64], cqn[:, kc, tsl(tt)], kc == 0, kc == 3, r=[twb, tcq[tt]], w=[PB[bk]])
                dve("tensor_copy", Qh[hb][0:64, tsl(tt)], bank(bk)[0:64, :], r=[PB[bk]], w=[tqk[hb][tt]])
                bk = nbank((6, 2))
                for kc in range(2):
                    mm(bank(bk)[0:64, :], wukv[:, kc, h * 128:h * 128 + 64], ckvn[:, kc, tsl(tt)], kc == 0, kc == 1, r=[twb, tckv[tt]], w=[PB[bk]])
                dve("tensor_copy", Kh[hb][0:64, tsl(tt)], bank(bk)[0:64, :], r=[PB[bk]], w=[tqk[hb][tt]])
                ba_, bb_ = nbank((6, 2)), nbank((6, 2))
                for kc in range(4):
                    mm(bank(ba_)[64:96, :], wuq[:, kc, h * 96 + 64:h * 96 + 96], cqn[:, kc, tsl(tt)], kc == 0, kc == 3, r=[twb, tcq[tt]], w=[PB[ba_]])
                for kc in range(4):
                    mm(bank(bb_)[64:96, :], wuqr[:, kc, h, :], cqn[:, kc, tsl(tt)], kc == 0, kc == 3, r=[twb, tcq[tt]], w=[PB[bb_]])
                i = cnt["iu"] % 2
                cnt["iu"] += 1
                dve("tensor_tensor", u1[i][64:96, :], bank(ba_)[64:96, :], cos2[64:96, tsl(tt)], ALU.mult, r=[PB[ba_], TROPE], w=[tu[i]])
                dve("tensor_tensor", u2[i][64:96, :], bank(bb_)[64:96, :], sin2[64:96, tsl(tt)], ALU.mult, r=[PB[bb_], TROPE], w=[tu[i]])
                dve("tensor_tensor", Qh[hb][64:96, tsl(tt)], u1[i][64:96, :], u2[i][64:96, :], ALU.add, r=[tu[i]], w=[tqk[hb][tt]])
                yield tt

        def s_issue(h, qt, kc):
            hb = h % 2
            bs = nbank((0, 4))
            mm(bank(bs), Kh[hb][0:96, kc * 128:(kc + 1) * 128], Qh[hb][0:96, tsl(qt)], True, True,
               r=[tqk[hb][kc // 4], tqk[hb][qt]], w=[PB[bs]])
            return bs

        for _ in head_proj(0):
            pass
        pending = []
        for h in range(8):
            hb = h % 2
            po = hb * 64
            dpo = 64 - po
            gen = head_proj(h + 1) if h < 7 else iter(())
            units = [(qt, kc) for qt in range(4) for kc in range(16)]
            sb = {}
            LA = 3
            for n_ in range(LA):
                sb[n_] = s_issue(h, *units[n_])
            bo = None
            for n_, (qt, kc) in enumerate(units):
                if kc == 0:
                    bo = nbank((4, 2))
                bs = sb.pop(n_)
                i = cnt["ip"] % 4
                cnt["ip"] += 1
                act(PT[i][:], bank(bs), AF.Exp, r=[PB[bs]], w=[tpt[i]], scale=scale)
                mm(bank(bo), Va[hb][:, kc, :], PT[i][:], kc == 0, kc == 15, r=[tva[hb], tpt[i]], w=[PB[bo]])
                if n_ + LA < len(units):
                    sb[n_ + LA] = s_issue(h, *units[n_ + LA])
                if kc == 15:
                    j = cnt["ir"] % 2
                    cnt["ir"] += 1
                    dve("reciprocal", rden[j][dpo:dpo + 64, :], bank(bo)[dpo:dpo + 64, :], r=[PB[bo]], w=[trd[j]])

                    def epi(bo=bo, j=j, po=po, dpo=dpo, h=h, qt=qt):
                        br = nbank((6, 2))
                        mm(bank(br), shiftm[dpo:dpo + 64, :], rden[j][dpo:dpo + 64, :], True, True, r=[TC, trd[j]], w=[PB[br]])
                        dve("tensor_copy", rsh[j][po:po + 64, :], bank(br)[po:po + 64, :], r=[PB[br]], w=[trs_[j]])
                        dve("tensor_tensor", OT[po:po + 64, h // 2, tsl(qt)], bank(bo)[po:po + 64, :], rsh[j][po:po + 64, :], ALU.mult,
                            r=[PB[bo], trs_[j]], w=[tot[h // 2][qt]])
                    pending.append(epi)
                if kc == 8 and pending:
                    pending.pop(0)()
                if kc == 3:
                    next(gen, None)
        while pending:
            pending.pop(0)()
        tots = Tok()
        S.dma("sp", ot_d, OT[:].rearrange("p a b -> p (a b)"), r=[t for row in tot for t in row], w=[tots])
        S.barrier()
        if EVEN_STOP < 2:
            return
        rmsnorm_fm(Bump(arena, K48, ARENA), hT, h_toks, 8, D, lambda c: gcol(G_MIX + layer, c), xn, txn)
        S.barrier()
        bc = Bump(arena, K48, K96)
        xbc = bc.alloc((12, SEQ), BF16)
        txb = [Tok() for _ in range(16)]
        bt = Bump(arena, K96, ARENA)
        raw = [bt.alloc((SEQ + 4,), F32) for _ in range(2)]
        traw = [Tok() for _ in range(2)]
        acc = bt.alloc((SEQ,), F32)
        tacc = Tok()
        ring = Ring(bt, 3, 2048)
        for i in range(2):
            dve("memset", raw[i][:, 0:2], 0.0, r=[], w=[traw[i]])
            dve("memset", raw[i][:, SEQ + 2:SEQ + 4], 0.0, r=[], w=[traw[i]])
        for oc in range(12):
            i = oc % 2
            slot, tk = ring.next()
            wv = slot[:, 0:2048].bitcast(BF16).rearrange("p (kc n) -> p kc n", kc=8)
            S.dma("pool", wv, win[:, :, 1024 + oc * 128:1024 + (oc + 1) * 128], w=[tk])
            for tt in range(4):
                bk = nbank((0, 4))
                for kc in range(8):
                    mm(bank(bk), wv[:, kc, :], xn[:, kc, tsl(tt)], kc == 0, kc == 7, r=[tk, txn[tt]], w=[PB[bk]])
                act(raw[i][:, 2 + tt * 512:2 + (tt + 1) * 512], bank(bk), AF.Copy, r=[PB[bk]], w=[traw[i]])
            dve("tensor_scalar", acc[:], raw[i][:, 0:SEQ], convw[:, e, oc, 0:1], 0.0, ALU.mult, ALU.add, r=[traw[i], TC], w=[tacc])
            for t in range(1, 5):
                dve("scalar_tensor_tensor", acc[:], raw[i][:, t:t + SEQ], convw[:, e, oc, t:t + 1], acc[:], ALU.mult, ALU.add,
                    r=[traw[i], TC, tacc], w=[tacc])
            act(xbc[:, oc, :], acc[:], AF.Silu, r=[tacc, TC], w=txb, bias=convb[:, e, oc:oc + 1])
        S.barrier()
        if EVEN_STOP < 3:
            return
        bs_ = Bump(arena, K96, K104)
        dt = bs_.alloc((16, 32), F32)
        la = bs_.alloc((16, 32), F32)
        cum = bs_.alloc((16, 32), F32)
        ecum = bs_.alloc((16, 32), F32)
        dtb = bs_.alloc((32,), F32)
        eal = bs_.alloc((32,), F32)
        dsk = bs_.alloc((16,), F32)
        tsm = Tok()
        bt = Bump(arena, K104, ARENA)
        wz = Bump(arena, 0, K16).alloc((8, 1024), BF16)
        wdt = bt.alloc((8, 32), BF16)
        twz = Tok()
        zt = [bt.alloc((1024,), BF16) for _ in range(2)]
        tzt = [Tok() for _ in range(2)]
        S.dma("pool", wz, win[:, :, 0:1024], w=[twz])
        S.dma("pool", wdt, win[:, :, 2560:2592], w=[twz])
        S.dma("sp", dtb[:], dtb_d[e:e + 1, :].to_broadcast([128, 32]), w=[tsm])
        S.dma("sp", eal[:], alog_d[e:e + 1, :].to_broadcast([128, 32]), w=[tsm])
        S.dma("sp", dsk[:], ssdd_d[e:e + 1, :].to_broadcast([128, 16]), w=[tsm])
        tzs = [Tok() for _ in range(16)]
        for sc in range(16):
            i = sc % 2
            for ct in range(2):
                bk = nbank((0, 4))
                for kc in range(8):
                    mm(bank(bk), xn[:, kc, sc * 128:(sc + 1) * 128], wz[:, kc, ct * 512:(ct + 1) * 512], kc == 0, kc == 7,
                       r=[twz, txn[sc // 4]], w=[PB[bk]])
                act(zt[i][:, ct * 512:(ct + 1) * 512], bank(bk), AF.Silu, r=[PB[bk]], w=[tzt[i]])
            S.dma("sp", zs_d[sc * 128:(sc + 1) * 128, :], zt[i][:], r=[tzt[i]], w=[tzs[sc]])
            bk = nbank((4, 2))
            for kc in range(8):
                mm(bank(bk)[:, 0:32], xn[:, kc, sc * 128:(sc + 1) * 128], wdt[:, kc, :], kc == 0, kc == 7, r=[twz, txn[sc // 4]], w=[PB[bk]])
            dve("tensor_tensor", dt[:, sc, :], bank(bk)[:, 0:32], dtb[:], ALU.add, r=[PB[bk], tsm], w=[tsm])
        act(dt[:], dt[:], AF.Exp, r=[tsm], w=[tsm])
        act(dt[:], dt[:], AF.Ln, r=[tsm], w=[tsm], bias=1.0)
        act(eal[:], eal[:], AF.Exp, r=[tsm], w=[tsm])
        dve("scalar_tensor_tensor", la[:], dt[:], -1.0, eal[:].unsqueeze(1).to_broadcast([128, 16, 32]), ALU.mult, ALU.mult, r=[tsm], w=[tsm])
        for d_ in range(2):
            bk = nbank((4, 2))
            mm(bank(bk)[:, 0:256].rearrange("p (a b) -> p a b", b=16), tri[:, d_, :], la[:, :, d_ * 16:(d_ + 1) * 16], True, True, r=[tsm, TC], w=[PB[bk]])
            dve("tensor_copy", cum[:, :, d_ * 16:(d_ + 1) * 16], bank(bk)[:, 0:256].rearrange("p (a b) -> p a b", b=16), r=[PB[bk]], w=[tsm])
        act(ecum[:], cum[:], AF.Exp, r=[tsm], w=[tsm])
        negc = la
        totr = bs_.alloc((16, 32), F32)
        dtw = bs_.alloc((16, 32), F32)
        etot = bs_.alloc((16, 32), F32)
        selt = bs_.alloc((2, 128), F32)
        S.dma("sp", selt, sel_d, w=[tsm])
        for d_ in range(2):
            bk = nbank((4, 2))
            mm(bank(bk)[:, 0:256].rearrange("p (a b) -> p a b", b=16), selt[:, d_, :], cum[:, :, d_ * 16:(d_ + 1) * 16], True, True, r=[tsm], w=[PB[bk]])
            dve("tensor_copy", totr[:, :, d_ * 16:(d_ + 1) * 16], bank(bk)[:, 0:256].rearrange("p (a b) -> p a b", b=16), r=[PB[bk]], w=[tsm])
        dve("tensor_tensor", dtw[:], totr[:], cum[:], ALU.subtract, r=[tsm], w=[tsm])
        act(dtw[:], dtw[:], AF.Exp, r=[tsm], w=[tsm])
        dve("tensor_tensor", dtw[:], dtw[:], dt[:], ALU.mult, r=[tsm], w=[tsm])
        act(etot[:], totr[:], AF.Exp, r=[tsm], w=[tsm])
        dve("tensor_scalar", negc[:], cum[:], -1.0, None, ALU.mult, r=[tsm], w=[tsm])
        S.barrier()
        if EVEN_STOP < 4:
            return
        bt1 = Bump(arena, 0, K48)
        bt = Bump(arena, K104, ARENA)
        Dm = bt1.alloc((16, 128), BF16)
        tdm = Tok()
        dve("tensor_tensor", Dm[:], identb[:].unsqueeze(1).to_broadcast([128, 16, 128]), dsk[:].unsqueeze(2).to_broadcast([128, 16, 128]), ALU.mult,
            r=[TC, tsm], w=[tdm])
        XT1 = bt1.alloc((1024,), BF16)
        txt1 = Tok()
        BT1 = bt1.alloc((256,), BF16)
        tbt1 = Tok()
        xdtF = bt1.alloc((1024,), BF16)
        xdtB = bt1.alloc((1024,), BF16)
        xw = bt1.alloc((1024,), BF16)
        txdF, txdB, txw = Tok(), Tok(), Tok()
        CBm = bt1.alloc((2, 2, 128), F32)
        tcb = [Tok(), Tok()]
        Eb = [bt1.alloc((4, 128), F32) for _ in range(2)]
        teb = [Tok() for _ in range(2)]
        MT = [bt1.alloc((16, 128), BF16) for _ in range(2)]
        tmt = [[Tok() for _ in range(4)] for _ in range(2)]
        Hf = bt1.alloc((1024,), F32)
        Hb = bt1.alloc((1024,), BF16)
        thf, thb = Tok(), Tok()
        hinl = bt1.alloc((1024,), BF16)
        thin = Tok()
        t1 = bt1.alloc((1024,), F32)
        yv = bt1.alloc((1024,), F32)
        tt1, tyv = Tok(), Tok()
        zl = bt.alloc((1024,), BF16)
        tzl = Tok()
        ynb = bt.alloc((1024,), BF16)
        tynb = Tok()
        junk = xw
        ssq = bt.alloc((1,), F32)
        tss = Tok()
        thd = [Tok() for _ in range(16)]
        pb7 = bank(7).bitcast(BF16)
        pb6 = bank(6).bitcast(BF16)
        h3 = lambda v: v.rearrange("p (h c) -> p h c", c=64)

        def make_xt(sc):
            csl = slice(sc * 128, (sc + 1) * 128)
            for c in range(8):
                tr(pb7[:, c * 128:(c + 1) * 128], xbc[:, c, csl], identb[:], r=[txb[sc], TC], w=[PB[7]])
            dve("tensor_copy", XT1[:], pb7[:, :], r=[PB[7]], w=[txt1])
            for g in range(2):
                tr(pb6[:, g * 128:(g + 1) * 128], xbc[:, 8 + g, csl], identb[:], r=[txb[sc], TC], w=[PB[6]])
            dve("tensor_copy", BT1[:], pb6[:, 0:256], r=[PB[6]], w=[tbt1])

        def state_update(sc, d_, first, split=False):
            S.op("pool", lambda e: e.tensor_tensor(h3(xw[:]), h3(XT1[:]), dtw[:, sc, d_ * 16:(d_ + 1) * 16].unsqueeze(2).to_broadcast([128, 16, 64]), ALU.mult),
                 r=[txt1, tsm], w=[txw])
            bks = []
            for g in range(2):
                bk = (2 + g) if split else nbank((4, 2))
                bks.append(bk)
                mm(bank(bk), BT1[:, g * 128:(g + 1) * 128], xw[:, g * 512:(g + 1) * 512], True, True, r=[tbt1, txw], w=[PB[bk]])
            if not first:
                S.op("pool", lambda e: e.tensor_tensor(h3(Hf[:]), h3(Hf[:]), etot[:, sc, d_ * 16:(d_ + 1) * 16].unsqueeze(2).to_broadcast([128, 16, 64]), ALU.mult),
                     r=[thf, tsm], w=[thf])
            def fin():
                for g in range(2):
                    gs = slice(g * 512, (g + 1) * 512)
                    if first:
                        dve("tensor_copy", Hf[:, gs], bank(bks[g]), r=[PB[bks[g]]], w=[thf])
                    else:
                        dve("tensor_tensor", Hf[:, gs], Hf[:, gs], bank(bks[g]), ALU.add, r=[thf, PB[bks[g]]], w=[thf])
                act(Hb[:], Hf[:], AF.Copy, r=[thf], w=[thb])
            if split:
                return fin
            fin()

        for sc in range(15):
            make_xt(sc)
            state_update(sc, 0, sc == 0)
            S.dma("sp", hin_d[sc + 1], Hb[:], r=[thb], w=[thd[sc + 1]])
        if EVEN_STOP == 41:
            return
        t1s = [t1, bt.alloc((1024,), F32)]
        t1Bs = [bt1.alloc((1024,), F32), bt.alloc((1024,), F32)]
        zls = [zl, bt.alloc((1024,), BF16)]
        tt1s, tt1bs, tzls = [Tok(), Tok()], [Tok(), Tok()], [Tok(), Tok()]
        lnt = bt.alloc((1,), F32)

        def front_a(sc):
            csl = slice(sc * 128, (sc + 1) * 128)
            hasF, hasB = sc >= 1, sc <= 14
            par = sc % 2
            t1, t1B, zl, tt1, tt1b, tzl = t1s[par], t1Bs[par], zls[par], tt1s[par], tt1bs[par], tzls[par]
            if hasF:
                S.dma("sp", hinl[:], hin_d[sc], r=[thd[sc]], w=[thin])
            S.dma("sp", zl[:], zs_d[csl, :], r=[tzs[sc]], w=[tzl])
            bkc = nbank((4, 2))
            for g in range(2):
                mm(bank(bkc)[:, g * 128:(g + 1) * 128], xbc[:, 8 + g, csl], xbc[:, 10 + g, csl], True, True, r=[txb[sc]], w=[PB[bkc]])
            for d_ in range(2):
                dve("tensor_tensor", CBm[:, d_], bank(bkc)[:, 0:256].rearrange("p (g l) -> p g l", g=2),
                    tri[:, d_, :].unsqueeze(1).to_broadcast([128, 2, 128]), ALU.mult, r=[PB[bkc], TC], w=[tcb[d_]])
            ie = [0]

            def seg_group(d_, hb):
                g = hb // 2
                bk = nbank((4, 2))
                for i in range(4):
                    h = hb * 4 + i
                    tr(bank(bk)[:, i * 128:(i + 1) * 128], cum[:, sc, d_ * 16 + h:d_ * 16 + h + 1].to_broadcast([128, 128]), identf[:],
                       r=[tsm, TC], w=[PB[bk]])
                k = ie[0] % 2
                ie[0] += 1
                for i in range(4):
                    h = hb * 4 + i
                    S.op("act", lambda e, k=k, i=i, bk=bk, h=h: e.activation(out=Eb[k][:, i, :], in_=bank(bk)[:, i * 128:(i + 1) * 128], func=AF.Exp,
                                                                     bias=negc[:, sc, d_ * 16 + h:d_ * 16 + h + 1]),
                         r=[PB[bk], tsm], w=[teb[k]], nowaw=True)
                dve("scalar_tensor_tensor", MT[d_][:, hb * 4:(hb + 1) * 4, :], Eb[k][:], 1.0, CBm[:, d_, g, :].unsqueeze(1).to_broadcast([128, 4, 128]),
                    ALU.min, ALU.mult, r=[teb[k], tcb[d_]], w=[tmt[d_][hb]])

            groups = [(d_, hb) for d_ in range(2) for hb in range(4)]
            seg_group(*groups[0])
            seg_group(*groups[1])
            make_xt(sc)
            S.op("pool", lambda e: e.tensor_tensor(h3(xdtF[:]), h3(XT1[:]), dt[:, sc, 0:16].unsqueeze(2).to_broadcast([128, 16, 64]), ALU.mult),
                 r=[txt1, tsm], w=[txdF])
            S.op("pool", lambda e: e.tensor_tensor(h3(xdtB[:]), h3(XT1[:]), dt[:, sc, 16:32].unsqueeze(2).to_broadcast([128, 16, 64]), ALU.mult),
                 r=[txt1, tsm], w=[txdB])
            seg_group(*groups[2])
            seg_group(*groups[3])
            for g in range(2):
                gs = slice(g * 512, (g + 1) * 512)
                if hasF:
                    mm(bank(2), xbc[:, 10 + g, csl], hinl[:, gs], True, True, r=[txb[sc], thin], w=[PB[2]])
                    dve("tensor_tensor", h3(t1[:, gs]), h3(bank(2)), ecum[:, sc, g * 8:(g + 1) * 8].unsqueeze(2).to_broadcast([128, 8, 64]), ALU.mult,
                        r=[PB[2], tsm], w=[tt1], nowaw=True)
                if hasB:
                    mm(bank(3), xbc[:, 10 + g, csl], Hb[:, gs], True, True, r=[txb[sc], thb], w=[PB[3]])
                    dve("tensor_tensor", h3(t1B[:, gs]), h3(bank(3)), ecum[:, sc, 16 + g * 8:16 + (g + 1) * 8].unsqueeze(2).to_broadcast([128, 8, 64]), ALU.mult,
                        r=[PB[3], tsm], w=[tt1b], nowaw=True)
            seg_group(*groups[4])
            fin = None
            if sc >= 1:
                fin = state_update(sc, 1, sc == 15, split=True)
            seg_group(*groups[5])
            seg_group(*groups[6])
            seg_group(*groups[7])
            if fin is not None:
                fin()

        def front_b(sc):
            for h in range(16):
                hs = slice(h * 64, (h + 1) * 64)
                mm(ps[:, 0:1024][:, hs], MT[0][:, h, :], xdtF[:, hs], True, False, r=[tmt[0][h // 4], txdF], w=[PB[h // 8]])
                mm(ps[:, 0:1024][:, hs], MT[1][:, h, :], xdtB[:, hs], False, False, r=[tmt[1][h // 4], txdB], w=[PB[h // 8]])
                mm(ps[:, 0:1024][:, hs], Dm[:, h, :], XT1[:, hs], False, True, r=[tdm, txt1], w=[PB[h // 8]])

        def back(sc):
            csl = slice(sc * 128, (sc + 1) * 128)
            hasF, hasB = sc >= 1, sc <= 14
            par = sc % 2
            t1, t1B, zl, tt1, tt1b, tzl = t1s[par], t1Bs[par], zls[par], tt1s[par], tt1bs[par], tzls[par]
            for g in range(2):
                gs = slice(g * 512, (g + 1) * 512)
                if hasF:
                    dve("tensor_tensor", yv[:, gs], t1[:, gs], bank(g), ALU.add, r=[tt1, PB[g]], w=[tyv], nowaw=True)
                else:
                    dve("tensor_copy", yv[:, gs], bank(g), r=[PB[g]], w=[tyv], nowaw=True)
            if hasB:
                S.op("pool", lambda e: e.tensor_tensor(yv[:], yv[:], t1B[:], ALU.add), r=[tt1b, tyv], w=[tyv])
            S.op("pool", lambda e: e.tensor_tensor(yv[:], yv[:], zl[:], ALU.mult), r=[tyv, tzl], w=[tyv])
            act(junk[:], yv[:], AF.Square, r=[tyv], w=[tss, txw], accum_out=ssq[:])
            act(lnt[:], ssq[:], AF.Ln, r=[tss, TC], w=[tss], scale=1.0 / 1024, bias=epst[:])
            act(ssq[:], lnt[:], AF.Exp, r=[tss], w=[tss], scale=-0.5)
            dve("tensor_scalar", ynb[:], yv[:], ssq[:, 0:1], 0.0, ALU.mult, ALU.add, r=[tyv, tss], w=[tynb])
            for c in range(8):
                tr(pb7[:, c * 128:(c + 1) * 128], ynb[:, c * 128:(c + 1) * 128], identb[:], r=[tynb, TC], w=[PB[7]])
            dve("tensor_tensor", xbc[:, 0:8, csl], pb7[:, :].rearrange("p (c l) -> p c l", c=8),
                gains[:, (G_SSD + e) * 8:(G_SSD + e) * 8 + 8].unsqueeze(2).to_broadcast([128, 8, 128]), ALU.mult,
                r=[PB[7], TC], w=[txb[sc]])

        front_a(15)
        front_b(15)
        for sc in range(14, -1, -1):
            front_a(sc)
            back(sc + 1)
            front_b(sc)
        back(0)
        S.barrier()
        if EVEN_STOP < 5:
            return
        OT = Bump(arena, 0, K16).alloc((4, SEQ), BF16)
        tot = [[Tok() for _ in range(4)] for _ in range(4)]
        S.dma("sp", OT[:].rearrange("p a b -> p (a b)"), ot_d, r=[tots], w=[t for row in tot for t in row])
        ring = Ring(Bump(arena, K96, ARENA), 3, 6144)
        proj_accum(None, wout_d, 12, None, lambda j, tt: (txb[tt * 4:tt * 4 + 4] if j < 8 else [tot[j - 8][tt]]), 1.0, ring, ybanks=(0, 4),
                   act_view=lambda j, tt: (xbc[:, j, tsl(tt)] if j < 8 else OT[:, j - 8, tsl(tt)]))

    if plan is None:
        plan = []
        for layer in range(DEPTH):
            plan += [("ffn1", layer), ("mix", layer), ("xa", layer), ("ffn2", layer)]
    for s in range(nseq):
        load_x(s)
        if any(p[0] == "xa" for p in plan):
            prep_mem(s)
        if any(p[0] == "mix" and p[1] % 2 == 0 for p in plan):
            rope_tables(s)
        for kind, layer in plan:
            if kind == "ffn1":
                ffn(dr["ffn1_w_gu"][layer], dr["ffn1_w_down"][layer], G_FFN1 + layer)
            elif kind == "ffn2":
                ffn(dr["ffn2_w_gu"][layer], dr["ffn2_w_down"][layer], G_FFN2 + layer)
            elif kind == "xa":
                xattn(layer)
            elif kind == "mix":
                if layer % 2 == 1:
                    fnet(layer)
                else:
                    even_mixer(layer // 2, layer)
        final(s)
    S.barrier()
    S.emit()
    return nc


def _fm(v):
    return np.ascontiguousarray(np.asarray(v, np.float32).reshape(-1, 128).T)


def host_consts(inp):
    c = {}
    g = np.zeros((128, NGAIN * 8), np.float32)

    def put(idx, v):
        g[:, idx * 8:(idx + 1) * 8] = _fm(v)

    for l in range(4):
        put(G_FFN1 + l, inp["ffn1_norm"][l])
        put(G_MIX + l, inp["mix_norm"][l])
        put(G_XA + l, inp["xa_norm"][l])
        put(G_FFN2 + l, inp["ffn2_norm"][l])
    put(G_FINAL, inp["final_norm"])
    put(G_MEM, inp["mem_norm"])
    for e in range(2):
        put(G_SSD + e, inp["ssd_norm"][e])
    c["gains"] = g
    q = np.zeros((128, 2, 6), np.float32)
    for e in range(2):
        q[:, e, 0:4] = _fm(inp["q_norm"][e])
        q[:, e, 4:6] = _fm(inp["kv_norm"][e])
    c["qkvgains"] = q
    cw = np.asarray(inp["conv_w"], np.float32)
    c["convw"] = np.ascontiguousarray(cw.reshape(2, 5, 12, 128).transpose(3, 0, 2, 1))
    cb = np.asarray(inp["conv_b"], np.float32)
    c["convb"] = np.ascontiguousarray(cb.reshape(2, 12, 128).transpose(2, 0, 1))
    c["dt_bias"] = np.ascontiguousarray(np.asarray(inp["dt_bias"], np.float32).reshape(2, 32))
    c["a_log"] = np.ascontiguousarray(np.asarray(inp["a_log"], np.float32).reshape(2, 32))
    c["ssd_d"] = np.ascontiguousarray(np.asarray(inp["ssd_d"], np.float32))
    c["ident"] = np.eye(128, dtype=np.float32)
    sel = np.zeros((128, 2, 128), np.float32)
    sel[127, 0, :] = 1.0
    sel[0, 1, :] = 1.0
    c["sel"] = sel
    s_ = np.arange(128)
    tri = np.zeros((128, 2, 128), np.float32)
    tri[:, 0, :] = (s_[:, None] <= s_[None, :])
    tri[:, 1, :] = (s_[:, None] >= s_[None, :])
    c["tri"] = tri
    inv = 1.0 / (10000.0 ** (np.arange(0, 32, 2, dtype=np.float32) / 32.0))
    c["invf"] = np.concatenate([inv, inv]).astype(np.float32).reshape(32, 1)
    c["shiftm"] = np.ascontiguousarray(np.roll(np.eye(128, dtype=np.float32), 64, axis=1))
    j = np.arange(256)
    angc = 2 * np.pi * ((j[:, None] * j[None, :]) % 256) / 256.0
    cd = np.concatenate([np.cos(angc), np.sin(angc)], axis=1)
    c["cdft"] = np.ascontiguousarray(cd.reshape(2, 128, 512).transpose(1, 0, 2)).astype(ml_dtypes.bfloat16)
    k = np.arange(SEQ)
    angs = 2 * np.pi * ((k[:, None] * k[None, :]) % SEQ) / float(SEQ)
    sd = np.stack([np.cos(angs), -np.sin(angs)], axis=1)
    c["sdft"] = np.ascontiguousarray(sd).astype(ml_dtypes.bfloat16)
    return c


_CACHE = {}


def kernel(**inputs):
    inp = {k: np.asarray(v) for k, v in inputs.items()}
    if "nc" not in _CACHE:
        _CACHE["nc"] = build_program()
    nc = _CACHE["nc"]
    consts = host_consts(inp)
    shared = {name: np.ascontiguousarray(inp[name], dtype=np.float32) for name, _ in WEIGHT_SPECS}
    shared.update(consts)
    in_maps = []
    for c in range(NCORES):
        m = dict(shared)
        m["x"] = np.ascontiguousarray(inp["x"][c * NSEQ:(c + 1) * NSEQ], dtype=np.float32)
        m["mem"] = np.ascontiguousarray(inp["mem"][c * NSEQ:(c + 1) * NSEQ], dtype=np.float32)
        m["positions"] = np.ascontiguousarray(inp["positions"][c * NSEQ:(c + 1) * NSEQ], dtype=np.int32)
        in_maps.append(m)
    res = run_bass_kernel_spmd(nc, in_maps, core_ids=list(range(NCORES)))
    return np.concatenate([np.asarray(r["out"], np.float32) for r in res.results], axis=0)
```

```python
import math
import numpy as np
import ml_dtypes
import concourse.bass as bass
import concourse.mybir as mybir
from concourse.bass_utils import run_bass_kernel_spmd

F32 = mybir.dt.float32
BF16 = mybir.dt.bfloat16
I32 = mybir.dt.int32
U8 = mybir.dt.uint8
AF = mybir.ActivationFunctionType
ALU = mybir.AluOpType

NCORES = 8
NSEQ = 2
SEQ = 2048
D = 1024
DFF = 2816
NMEM = 256
EPS = 1e-6
DEPTH = 4
DBG = set()
SW2 = 99
EVEN_STOP = 99

ENGS = ("pe", "act", "dve", "pool", "sp")
NDSEM = {"sp": 6, "act": 3, "pool": 6}


class Tok:
    __slots__ = ("w", "rs")

    def __init__(self):
        self.w = None
        self.rs = {}


class Op:
    __slots__ = ("eng", "fn", "idx", "key", "val", "waits", "sig", "sigval", "K", "dma")


class Sched:
    def __init__(self, nc):
        self.nc = nc
        self.ops = {e: [] for e in ENGS}
        self.cnt = {e: 0 for e in ENGS}
        self.K = {e: {} for e in ENGS}
        self.dcnt = {}
        self.drr = {q: 0 for q in NDSEM}
        self.last = {}

    def _add(self, eng, fn, deps, dma):
        op = Op()
        op.eng, op.fn, op.dma, op.sig, op.sigval = eng, fn, dma, False, 0
        self.cnt[eng] += 1
        op.idx = self.cnt[eng]
        if dma:
            slot = self.drr[eng]
            self.drr[eng] = (slot + 1) % NDSEM[eng]
            op.key = (eng, slot)
            self.dcnt[op.key] = self.dcnt.get(op.key, 0) + 1
            op.val = self.dcnt[op.key]
        else:
            op.key = eng
            op.val = op.idx
        K = self.K[eng]
        waits = []
        for d in sorted(deps, key=lambda d: -d.val):
            if eng == "pe" and d.eng == "pe" and not d.dma:
                continue
            if K.get(d.key, 0) >= d.val:
                continue
            waits.append(d)
            d.sig = True
            for k, v in d.K.items():
                if K.get(k, 0) < v:
                    K[k] = v
            if K.get(d.key, 0) < d.val:
                K[d.key] = d.val
        op.waits = waits
        op.K = dict(K)
        self.ops[eng].append(op)
        if fn is not None:
            self.last[op.key] = op
        return op

    def op(self, eng, fn, r=(), w=(), dma=False, nowaw=False):
        deps = set()
        for t in r:
            if t.w is not None:
                deps.add(t.w)
        for t in w:
            if t.w is not None and not (nowaw and t.w.eng == eng and not t.w.dma):
                deps.add(t.w)
            for o in t.rs.values():
                deps.add(o)
        op = self._add(eng, fn, deps, dma)
        for t in r:
            t.rs[op.key] = op
        for t in w:
            t.w = op
            t.rs = {}
        return op

    def dma(self, q, out, in_, r=(), w=()):
        return self.op(q, lambda e: e.dma_start(out=out, in_=in_), r, w, dma=True)

    def barrier(self):
        lasts = set(self.last.values())
        for e in ENGS:
            self._add(e, None, lasts, False)

    def emit(self):
        nc = self.nc
        from contextlib import ExitStack
        with ExitStack() as es:
            esem = {e: es.enter_context(nc.semaphore("s_" + e)) for e in ENGS}
            dsem = {}
            for q, n in NDSEM.items():
                for i in range(n):
                    dsem[(q, i)] = es.enter_context(nc.semaphore("d_%s%d" % (q, i)))
            for e in ENGS:
                c = 0
                for op in self.ops[e]:
                    if op.dma:
                        continue
                    if op.sig and op.fn is not None:
                        c += 1
                    op.sigval = c

            def run(e, eng):
                for op in self.ops[e]:
                    for d in op.waits:
                        if d.dma:
                            eng.wait_ge(dsem[d.key], 16 * d.val)
                        elif d.sigval > 0:
                            eng.wait_ge(esem[d.eng], d.sigval)
                    if op.fn is None:
                        continue
                    ins = op.fn(eng)
                    if op.dma:
                        ins.then_inc(dsem[op.key], 16)
                    elif op.sig:
                        ins.then_inc(esem[e], 1)

            block = es.enter_context(nc.Block())

            @block.tensor
            def _(eng):
                run("pe", eng)

            @block.scalar
            def _(eng):
                run("act", eng)

            @block.vector
            def _(eng):
                run("dve", eng)

            @block.gpsimd
            def _(eng):
                run("pool", eng)

            @block.sync
            def _(eng):
                run("sp", eng)


WEIGHT_SPECS = [
    ("ffn1_w_gu", [4, 1024, 5632]), ("ffn1_w_down", [4, 2816, 1024]),
    ("xa_wq", [4, 1024, 1024]), ("xa_wkv", [4, 1024, 2048]), ("xa_wo", [4, 1024, 1024]),
    ("ffn2_w_gu", [4, 1024, 5632]), ("ffn2_w_down", [4, 2816, 1024]),
    ("w_in", [2, 1024, 3392]), ("w_uq", [2, 512, 768]), ("w_ukv", [2, 256, 1024]),
    ("w_out", [2, 1536, 1024]), ("fnet_w_out", [2, 1024, 1024]),
]
G_FFN1, G_MIX, G_XA, G_FFN2, G_FINAL, G_MEM, G_SSD = 0, 4, 8, 12, 16, 17, 18
NGAIN = 20


class Bump:
    def __init__(self, arena, base, limit):
        self.arena, self.off, self.limit = arena, base, limit

    def alloc(self, free_shape, dtype):
        esz = {F32: 4, BF16: 2, I32: 4, U8: 1}[dtype]
        n = esz
        for s in free_shape:
            n *= s
        off = (self.off + 31) // 32 * 32
        assert off + n <= self.limit, ("arena overflow", off, n, self.limit)
        self.off = off + n
        v = self.arena[:, off:off + n]
        if dtype != U8:
            v = v.bitcast(dtype)
        if len(free_shape) == 2:
            v = v.rearrange("p (a b) -> p a b", b=free_shape[1])
        elif len(free_shape) == 3:
            v = v.rearrange("p (a b c) -> p a b c", b=free_shape[1], c=free_shape[2])
        elif len(free_shape) == 4:
            v = v.rearrange("p (a b c d) -> p a b c d", b=free_shape[1], c=free_shape[2], d=free_shape[3])
        return v


def build_program(plan=None, nseq=NSEQ):
    nc = bass.Bass("TRN2", target_bir_lowering=False)
    dr = {}

    def din(name, shape, dt=F32):
        dr[name] = nc.dram_tensor(name, shape, dt, kind="ExternalInput").ap()
        return dr[name]

    x_d = din("x", [nseq, SEQ, D])
    mem_d = din("mem", [nseq, NMEM, D])
    pos_d = din("positions", [nseq, SEQ], I32)
    for name, shape in WEIGHT_SPECS:
        din(name, shape)
    gains_d = din("gains", [128, NGAIN * 8])
    qkvg_d = din("qkvgains", [128, 2, 6])
    convw_d = din("convw", [128, 2, 12, 5])
    convb_d = din("convb", [128, 2, 12])
    dtb_d = din("dt_bias", [2, 32])
    alog_d = din("a_log", [2, 32])
    ssdd_d = din("ssd_d", [2, 16])
    ident_d = din("ident", [128, 128])
    tri_d = din("tri", [128, 2, 128])
    invf_d = din("invf", [32, 1])
    sel_d = din("sel", [128, 2, 128])
    shift_d = din("shiftm", [128, 128])
    cdft_d = din("cdft", [128, 2, 512], BF16)
    sdft_d = din("sdft", [SEQ, 2, SEQ], BF16)
    out_d = nc.dram_tensor("out", [nseq, SEQ, D], F32, kind="ExternalOutput").ap()
    zs_d = nc.dram_tensor("zs_scr", [SEQ, D], BF16).ap()
    hin_d = nc.dram_tensor("hin_scr", [16, 128, 1024], BF16).ap()
    ot_d = nc.dram_tensor("ot_scr", [128, 4 * SEQ], BF16).ap()

    S = Sched(nc)
    A = nc.alloc_sbuf_tensor

    hT = A("hT", [128, 8, SEQ], F32)
    TH = [[Tok() for _ in range(4)] for _ in range(8)]
    identf = A("identf", [128, 128], F32)
    identb = A("identb", [128, 128], BF16)
    onesb = A("onesb", [128, 128], BF16)
    epst = A("epst", [128, 1], F32)
    gains = A("gains_sb", [128, NGAIN * 8], F32)
    qkvg = A("qkvg_sb", [128, 2, 6], F32)
    convw = A("convw_sb", [128, 2, 12, 5], F32)
    convb = A("convb_sb", [128, 2, 12], F32)
    tri = A("tri_sb", [128, 2, 128], F32)
    invf = A("invf_sb", [128, 1], F32)
    memT = A("memT", [128, 8, NMEM], BF16)
    cos2 = A("cos2", [128, SEQ], BF16)
    sin2 = A("sin2", [128, SEQ], BF16)
    shiftm = A("shiftm_sb", [128, 128], F32)
    TC = Tok()
    TMEM = Tok()
    TROPE = Tok()
    remaining = nc.sbuf_bytes_remaining
    ARENA = (remaining - 256) // 32 * 32
    arena = A("arena", [128, ARENA], U8)
    ps = nc.alloc_psum_tensor("ps", [128, 4096], F32)
    PB = [Tok() for _ in range(8)]

    def bank(i):
        return ps[:, i * 512:(i + 1) * 512]

    def mm(out, lhsT, rhs, start, stop, r, w):
        S.op("pe", lambda e: e.matmul(out, lhsT, rhs, start=start, stop=stop), r=r, w=w)

    def tr(out, in_, idn, r, w):
        S.op("pe", lambda e: e.transpose(out, in_, idn), r=r, w=w)

    def act(out, in_, func, r, w, **kw):
        S.op("act", lambda e: e.activation(out=out, in_=in_, func=func, **kw), r=r, w=w)

    def dve(name, *args, r, w, eng="dve", nowaw=False, **kw):
        S.op(eng, lambda e: getattr(e, name)(*args, **kw), r=r, w=w, nowaw=nowaw)

    def tsl(tt):
        return slice(tt * 512, (tt + 1) * 512)

    def gcol(gi, c):
        return gains[:, gi * 8 + c:gi * 8 + c + 1]

    S.dma("sp", identf[:], ident_d, w=[TC])
    S.dma("sp", gains[:], gains_d, w=[TC])
    S.dma("sp", qkvg[:], qkvg_d, w=[TC])
    S.dma("sp", convw[:], convw_d, w=[TC])
    S.dma("sp", convb[:], convb_d, w=[TC])
    S.dma("sp", tri[:], tri_d, w=[TC])
    S.dma("sp", invf[64:96, :], invf_d, w=[TC])
    S.dma("sp", shiftm[:], shift_d, w=[TC])
    dve("tensor_copy", identb[:], identf[:], r=[TC], w=[TC])
    dve("memset", onesb[:], 1.0, r=[], w=[TC])
    dve("memset", epst[:], EPS, r=[], w=[TC])

    class Ring:
        def __init__(self, b, n, nbytes):
            self.slots = [b.alloc((nbytes,), U8) for _ in range(n)]
            self.toks = [Tok() for _ in range(n)]
            self.i = 0

        def next(self):
            k = self.i % len(self.slots)
            self.i += 1
            return self.slots[k], self.toks[k]

    rr = {"n": 6}

    def nbank(group):
        lo, n = group
        k = rr.get(group, 0)
        rr[group] = (k + 1) % n
        return lo + k

    def rmsnorm_fm(b, src, src_toks, nch, dim, gain_fn, out, out_toks, ntt=4, banks=(6, 2), scr_sq=None):
        sq = [b.alloc((nch, 512), BF16) for _ in range(2)]
        tsq = [Tok() for _ in range(2)]
        rs = [b.alloc((512,), F32) for _ in range(2)]
        trs = [Tok() for _ in range(2)]
        act(sq[0][:], src[:, :, tsl(0)], AF.Square, r=src_toks(0), w=[tsq[0]])
        for tt in range(ntt):
            i = tt % 2
            if tt + 1 < ntt:
                act(sq[1 - i][:], src[:, :, tsl(tt + 1)], AF.Square, r=src_toks(tt + 1), w=[tsq[1 - i]])
            bk = nbank(banks)
            for c in range(nch):
                mm(bank(bk), onesb[:], sq[i][:, c, :], c == 0, c == nch - 1, r=[tsq[i], TC], w=[PB[bk]])
            act(rs[i][:], bank(bk), AF.Ln, r=[PB[bk], TC], w=[trs[i]], scale=1.0 / dim, bias=epst[:])
            act(rs[i][:], rs[i][:], AF.Exp, r=[trs[i]], w=[trs[i]], scale=-0.5)
            for c in range(nch):
                dve("scalar_tensor_tensor", out[:, c, tsl(tt)], src[:, c, tsl(tt)], gain_fn(c), rs[i][:], ALU.mult, ALU.mult,
                    r=src_toks(tt) + [trs[i], TC], w=[out_toks[tt]], nowaw=True)

    def h_toks(tt):
        return [TH[c][tt] for c in range(8)]

    def proj_accum(b, wsrc, nkc, actT, act_toks, scale, ring, ybanks=(4, 2), act_view=None):
        for dcp in range(4):
            slot, tk = ring.next()
            wd = slot[:, 0:nkc * 256 * 2].bitcast(BF16).rearrange("p (j n) -> p j n", j=nkc)
            S.dma("pool", wd, wsrc[:, :, dcp * 256:(dcp + 1) * 256], w=[tk])
            for dd in range(2):
                dc = dcp * 2 + dd
                for tt in range(4):
                    by = nbank(ybanks)
                    for j in range(nkc):
                        av = act_view(j, tt) if act_view is not None else actT[:, j, tsl(tt)]
                        at = act_toks(j, tt)
                        mm(bank(by), wd[:, j, dd * 128:(dd + 1) * 128], av, j == 0, j == nkc - 1,
                           r=[tk] + (at if isinstance(at, list) else [at]), w=[PB[by]])
                    dve("scalar_tensor_tensor", hT[:, dc, tsl(tt)], bank(by), float(scale), hT[:, dc, tsl(tt)], ALU.mult, ALU.add,
                        r=[PB[by], TH[dc][tt]], w=[TH[dc][tt]])

    def load_x(s):
        S.barrier()
        b = Bump(arena, 0, ARENA)
        xt = [b.alloc((D,), F32) for _ in range(2)]
        txt = [Tok() for _ in range(2)]
        for t in range(16):
            i = t % 2
            S.dma("sp", xt[i][:], x_d[s, t * 128:(t + 1) * 128, :], w=[txt[i]])
            for cg in range(2):
                bk = nbank((0, 4))
                for ci in range(4):
                    c = cg * 4 + ci
                    tr(bank(bk)[:, ci * 128:(ci + 1) * 128], xt[i][:, c * 128:(c + 1) * 128], identf[:], r=[txt[i], TC], w=[PB[bk]])
                dst = hT[:, cg * 4:(cg + 1) * 4, t * 128:(t + 1) * 128]
                src = bank(bk).rearrange("p (c n) -> p c n", c=4)
                toks = [TH[c][t // 4] for c in range(cg * 4, cg * 4 + 4)]
                if cg == 0:
                    act(dst, src, AF.Copy, r=[PB[bk]], w=toks)
                else:
                    dve("tensor_copy", dst, src, r=[PB[bk]], w=toks)

    def prep_mem(s):
        S.barrier()
        b = Bump(arena, 0, ARENA)
        for t in range(2):
            mt = b.alloc((D,), F32)
            m2 = b.alloc((D,), F32)
            junk = b.alloc((D,), F32)
            ss = b.alloc((1,), F32)
            tk = Tok()
            S.dma("sp", mt[:], mem_d[s, t * 128:(t + 1) * 128, :], w=[tk])
            act(junk[:], mt[:], AF.Square, r=[tk], w=[tk], accum_out=ss[:])
            act(ss[:], ss[:], AF.Sqrt, r=[tk, TC], w=[tk], scale=1.0 / D, bias=epst[:])
            dve("reciprocal", ss[:], ss[:], r=[tk], w=[tk])
            act(m2[:], mt[:], AF.Copy, r=[tk], w=[tk], scale=ss[:])
            for cg in range(2):
                bk = nbank((0, 4))
                for ci in range(4):
                    c = cg * 4 + ci
                    tr(bank(bk)[:, ci * 128:(ci + 1) * 128], m2[:, c * 128:(c + 1) * 128], identf[:], r=[tk, TC], w=[PB[bk]])
                for ci in range(4):
                    c = cg * 4 + ci
                    act(memT[:, c, t * 128:(t + 1) * 128], bank(bk)[:, ci * 128:(ci + 1) * 128], AF.Copy, r=[PB[bk], TC], w=[TMEM],
                        scale=gcol(G_MEM, c))

    def rope_tables(s):
        S.barrier()
        b = Bump(arena, 0, ARENA)
        pi_ = b.alloc((SEQ,), I32)
        pf = b.alloc((SEQ,), F32)
        tk = Tok()
        S.dma("sp", pi_[64:96, :], pos_d[s:s + 1, :].to_broadcast([32, SEQ]), w=[tk])
        dve("tensor_copy", pf[64:96, :], pi_[64:96, :], r=[tk], w=[tk])
        ang = b.alloc((SEQ,), F32)
        t_ = b.alloc((SEQ,), F32)
        ki = b.alloc((SEQ,), I32)
        kf = b.alloc((SEQ,), F32)
        r_ = b.alloc((SEQ,), F32)
        m_ = b.alloc((SEQ,), F32)
        P32 = slice(64, 96)
        dve("tensor_scalar", ang[P32, :], pf[P32, :], invf[P32, 0:1], 0.0, ALU.mult, ALU.add, r=[tk, TC], w=[tk])
        for dst, off in ((sin2, 0.0), (cos2, math.pi / 2)):
            dve("tensor_scalar", t_[P32, :], ang[P32, :], off, 1.0 / (2 * math.pi), ALU.add, ALU.mult, r=[tk], w=[tk])
            dve("tensor_copy", ki[P32, :], t_[P32, :], r=[tk], w=[tk])
            dve("tensor_copy", kf[P32, :], ki[P32, :], r=[tk], w=[tk])
            dve("tensor_scalar", r_[P32, :], ang[P32, :], off, None, ALU.add, r=[tk], w=[tk])
            dve("scalar_tensor_tensor", r_[P32, :], kf[P32, :], -2 * math.pi, r_[P32, :], ALU.mult, ALU.add, r=[tk], w=[tk])
            dve("tensor_scalar", m_[P32, :], r_[P32, :], math.pi, None, ALU.is_gt, r=[tk], w=[tk])
            dve("scalar_tensor_tensor", r_[P32, :], m_[P32, :], -2 * math.pi, r_[P32, :], ALU.mult, ALU.add, r=[tk], w=[tk])
            dve("tensor_scalar", m_[P32, :], r_[P32, :], -math.pi, None, ALU.is_lt, r=[tk], w=[tk])
            dve("scalar_tensor_tensor", r_[P32, :], m_[P32, :], 2 * math.pi, r_[P32, :], ALU.mult, ALU.add, r=[tk], w=[tk])
            act(dst[P32, :], r_[P32, :], AF.Sin, r=[tk], w=[TROPE])

    def ffn(gu, dn, gi):
        S.barrier()
        b = Bump(arena, 0, ARENA)
        xn = b.alloc((8, SEQ), BF16)
        txn = [Tok() for _ in range(4)]
        rmsnorm_fm(Bump(arena, b.off, ARENA), hT, h_toks, 8, D, lambda c: gcol(gi, c), xn, txn)
        S.barrier()
        actb = b.alloc((11, SEQ), BF16)
        ta = [[Tok() for _ in range(4)] for _ in range(11)]
        sg = [b.alloc((512,), F32) for _ in range(2)]
        tsg = [Tok() for _ in range(2)]
        ring = Ring(b, 6, 6144)
        gsrc = gu.rearrange("(kc p) (two n) -> p two kc n", p=128, two=2)
        dsrc = dn.rearrange("(j p) n -> p j n", p=128)
        k = 0
        for half in range(2):
            for jj in range(11):
                j = half * 11 + jj
                slot, tk = ring.next()
                wv = slot[:, 0:4096].bitcast(BF16).rearrange("p (two kc n) -> p two kc n", two=2, kc=8)
                S.dma("pool", wv, gsrc[:, :, :, j * 128:(j + 1) * 128], w=[tk])
                for tt in range(4):
                    bg = nbank((0, 2))
                    bu = nbank((2, 2))
                    for kc in range(8):
                        mm(bank(bg), wv[:, 0, kc, :], xn[:, kc, tsl(tt)], kc == 0, kc == 7, r=[tk, txn[tt]], w=[PB[bg]])
                    for kc in range(8):
                        mm(bank(bu), wv[:, 1, kc, :], xn[:, kc, tsl(tt)], kc == 0, kc == 7, r=[tk, txn[tt]], w=[PB[bu]])
                    i = k % 2
                    k += 1
                    act(sg[i][:], bank(bg), AF.Silu, r=[PB[bg]], w=[tsg[i]])
                    dve("tensor_tensor", actb[:, jj, tsl(tt)], bank(bu), sg[i][:], ALU.mult, r=[PB[bu], tsg[i]], w=[ta[jj][tt]])
            proj_accum(b, dsrc[:, half * 11:(half + 1) * 11, :], 11, actb, lambda j, tt: ta[j][tt], 0.5, ring)

    def xattn(layer):
        S.barrier()
        b = Bump(arena, 0, ARENA)
        xn = b.alloc((8, SEQ), BF16)
        txn = [Tok() for _ in range(4)]
        rmsnorm_fm(Bump(arena, b.off, ARENA), hT, h_toks, 8, D, lambda c: gcol(G_XA + layer, c), xn, txn)
        S.barrier()
        QT = b.alloc((8, SEQ), BF16)
        tq = [[Tok() for _ in range(4)] for _ in range(8)]
        OT = xn
        tot = [[Tok() for _ in range(4)] for _ in range(8)]
        KT = b.alloc((8, NMEM), BF16)
        tkt = Tok()
        Vt = b.alloc((2, D), BF16)
        tv = Tok()
        PT = [b.alloc((512,), BF16) for _ in range(4)]
        tpt = [Tok() for _ in range(4)]
        rden = [b.alloc((512,), F32) for _ in range(2)]
        trd = [Tok() for _ in range(2)]
        ring = Ring(b, 3, 8192)
        wkv = dr["xa_wkv"][layer].rearrange("(kc p) n -> p kc n", p=128)
        wq = dr["xa_wq"][layer].rearrange("(kc p) n -> p kc n", p=128)
        wo = dr["xa_wo"][layer].rearrange("(kc p) n -> p kc n", p=128)
        for ocp in range(4):
            slot, tk = ring.next()
            wv = slot[:, 0:4096].bitcast(BF16).rearrange("p (kc n) -> p kc n", kc=8)
            S.dma("pool", wv, wkv[:, :, ocp * 256:(ocp + 1) * 256], w=[tk])
            for oo in range(2):
                oc = ocp * 2 + oo
                bk = nbank((6, 2))
                for kc in range(8):
                    mm(bank(bk)[:, 0:NMEM], wv[:, kc, oo * 128:(oo + 1) * 128], memT[:, kc, :], kc == 0, kc == 7, r=[tk, TMEM], w=[PB[bk]])
                act(KT[:, oc, :], bank(bk)[:, 0:NMEM], AF.Copy, r=[PB[bk]], w=[tkt])
        for ct in range(2):
            slot, tk = ring.next()
            wv = slot[:, 0:8192].bitcast(BF16).rearrange("p (kc n) -> p kc n", kc=8)
            S.dma("pool", wv, wkv[:, :, D + ct * 512:D + (ct + 1) * 512], w=[tk])
            for kt in range(2):
                bk = nbank((6, 2))
                for kc in range(8):
                    mm(bank(bk), memT[:, kc, kt * 128:(kt + 1) * 128], wv[:, kc, :], kc == 0, kc == 7, r=[tk, TMEM], w=[PB[bk]])
                dve("tensor_copy", Vt[:, kt, ct * 512:(ct + 1) * 512], bank(bk), r=[PB[bk]], w=[tv])
        for ocp in range(4):
            slot, tk = ring.next()
            wv = slot[:, 0:4096].bitcast(BF16).rearrange("p (kc n) -> p kc n", kc=8)
            S.dma("pool", wv, wq[:, :, ocp * 256:(ocp + 1) * 256], w=[tk])
            for oo in range(2):
                oc = ocp * 2 + oo
                for tt in range(4):
                    bk = nbank((6, 2))
                    for kc in range(8):
                        mm(bank(bk), wv[:, kc, oo * 128:(oo + 1) * 128], xn[:, kc, tsl(tt)], kc == 0, kc == 7, r=[tk, txn[tt]], w=[PB[bk]])
                    act(QT[:, oc, tsl(tt)], bank(bk), AF.Copy, r=[PB[bk]], w=[tq[oc][tt]])
        S.barrier()
        scale = 256.0 ** -0.5
        ip = 0
        ir = 0
        units = [(h, tt) for h in range(4) for tt in range(4)]

        def s_issue(h, tt):
            bks = []
            for kc in range(2):
                bk = nbank((0, 4))
                for dc in range(2):
                    mm(bank(bk), KT[:, h * 2 + dc, kc * 128:(kc + 1) * 128], QT[:, h * 2 + dc, tsl(tt)], dc == 0, dc == 1,
                       r=[tkt, tq[h * 2 + dc][tt]], w=[PB[bk]])
                bks.append(bk)
            return bks

        pend = s_issue(*units[0])
        for n_, (h, tt) in enumerate(units):
            bks = pend
            pts = []
            for kc in range(2):
                i = ip % 4
                ip += 1
                act(PT[i][:], bank(bks[kc]), AF.Exp, r=[PB[bks[kc]]], w=[tpt[i]], scale=scale)
                pts.append(i)
            if n_ + 1 < len(units):
                pend = s_issue(*units[n_ + 1])
            bd = nbank((6, 2))
            for kc in range(2):
                mm(bank(bd), onesb[:], PT[pts[kc]][:], kc == 0, kc == 1, r=[tpt[pts[kc]], TC], w=[PB[bd]])
            j = ir % 2
            ir += 1
            act(rden[j][:], bank(bd), AF.Ln, r=[PB[bd]], w=[trd[j]])
            act(rden[j][:], rden[j][:], AF.Exp, r=[trd[j]], w=[trd[j]], scale=-1.0)
            for dvc in range(2):
                bo = nbank((4, 2))
                for kc in range(2):
                    mm(bank(bo), Vt[:, kc, h * 256 + dvc * 128:h * 256 + (dvc + 1) * 128], PT[pts[kc]][:], kc == 0, kc == 1,
                       r=[tv, tpt[pts[kc]]], w=[PB[bo]])
                dve("tensor_tensor", OT[:, h * 2 + dvc, tsl(tt)], bank(bo), rden[j][:], ALU.mult, r=[PB[bo], trd[j]], w=[tot[h * 2 + dvc][tt]])
        proj_accum(b, wo, 8, OT, lambda j, tt: tot[j][tt], 1.0, ring, ybanks=(6, 2))

    def fnet(layer):
        S.barrier()
        b = Bump(arena, 0, ARENA)
        xn = b.alloc((8, SEQ), BF16)
        txn = [Tok() for _ in range(4)]
        rmsnorm_fm(Bump(arena, b.off, ARENA), hT, h_toks, 8, D, lambda c: gcol(G_MIX + layer, c), xn, txn)
        S.barrier()
        Y = b.alloc((16, 4, 512), BF16)
        ty = [Tok() for _ in range(16)]
        cd = b.alloc((2, 512), BF16)
        tcd = Tok()
        ring = Ring(b, 6, 2048)
        ring2 = Ring(b, 2, 4096)
        S.dma("sp", cd[:], cdft_d, w=[tcd])
        k = 0
        for sc in range(16):
            for g in range(4):
                bk = nbank((0, 4))
                for cc in range(2):
                    mm(bank(bk), xn[:, g * 2 + cc, sc * 128:(sc + 1) * 128], cd[:, cc, :], cc == 0, cc == 1, r=[txn[sc // 4], tcd], w=[PB[bk]])
                if k % 2 == 0:
                    act(Y[:, sc, g, :], bank(bk), AF.Copy, r=[PB[bk]], w=[ty[sc]])
                else:
                    dve("tensor_copy", Y[:, sc, g, :], bank(bk), r=[PB[bk]], w=[ty[sc]])
                k += 1
        nrm = 1.0 / math.sqrt(SEQ * 256.0)
        for kt in range(4):
            for sc in range(16):
                slot, tk = ring.next()
                dv = slot[:, 0:2048].bitcast(BF16).rearrange("p (a n) -> p a n", a=2)
                S.dma("sp", dv, sdft_d[sc * 128:(sc + 1) * 128, :, kt * 512:(kt + 1) * 512], w=[tk])
                for o in range(8):
                    g, jc = o // 2, o % 2
                    mm(bank(o), Y[:, sc, g, jc * 128:(jc + 1) * 128], dv[:, 0, :], sc == 0, False, r=[ty[sc], tk], w=[PB[o]])
                    mm(bank(o), Y[:, sc, g, 256 + jc * 128:256 + (jc + 1) * 128], dv[:, 1, :], False, sc == 15, r=[ty[sc], tk], w=[PB[o]])
            for o in range(8):
                if o % 2 == 0:
                    act(xn[:, o, tsl(kt)], bank(o), AF.Copy, r=[PB[o]], w=[txn[kt]], scale=nrm)
                else:
                    dve("tensor_scalar", xn[:, o, tsl(kt)], bank(o), nrm, None, ALU.mult, r=[PB[o]], w=[txn[kt]])
        wsrc = dr["fnet_w_out"][layer // 2].rearrange("(kc p) n -> p kc n", p=128)
        proj_accum(b, wsrc, 8, xn, lambda j, tt: txn[tt], 1.0, ring2, ybanks=(0, 4))

    def final(s):
        S.barrier()
        b = Bump(arena, 0, ARENA)
        sq = [b.alloc((8, 512), BF16) for _ in range(2)]
        tsq = [Tok() for _ in range(2)]
        rs = [b.alloc((512,), F32) for _ in range(2)]
        trs = [Tok() for _ in range(2)]
        xf = [b.alloc((8, 512), F32) for _ in range(2)]
        txf = [Tok() for _ in range(2)]
        yt = [b.alloc((D,), F32) for _ in range(2)]
        tyt = [Tok() for _ in range(2)]
        k = 0
        for tt in range(4):
            i = tt % 2
            act(sq[i][:], hT[:, :, tsl(tt)], AF.Square, r=h_toks(tt), w=[tsq[i]])
            bk = nbank((6, 2))
            for c in range(8):
                mm(bank(bk), onesb[:], sq[i][:, c, :], c == 0, c == 7, r=[tsq[i], TC], w=[PB[bk]])
            act(rs[i][:], bank(bk), AF.Sqrt, r=[PB[bk], TC], w=[trs[i]], scale=1.0 / D, bias=epst[:])
            dve("reciprocal", rs[i][:], rs[i][:], r=[trs[i]], w=[trs[i]])
            for c in range(8):
                dve("scalar_tensor_tensor", xf[i][:, c, :], hT[:, c, tsl(tt)], gcol(G_FINAL, c), rs[i][:], ALU.mult, ALU.mult,
                    r=h_toks(tt) + [trs[i], TC], w=[txf[i]])
            for t4 in range(4):
                t = tt * 4 + t4
                j = k % 2
                k += 1
                for cg in range(2):
                    bk2 = nbank((0, 4))
                    for ci in range(4):
                        c = cg * 4 + ci
                        tr(bank(bk2)[:, ci * 128:(ci + 1) * 128], xf[i][:, c, t4 * 128:(t4 + 1) * 128], identf[:], r=[txf[i], TC], w=[PB[bk2]])
                    if cg == 0:
                        act(yt[j][:, 0:512], bank(bk2), AF.Copy, r=[PB[bk2]], w=[tyt[j]])
                    else:
                        dve("tensor_copy", yt[j][:, 512:1024], bank(bk2), r=[PB[bk2]], w=[tyt[j]])
                S.dma("sp", out_d[s, t * 128:(t + 1) * 128, :], yt[j][:], r=[tyt[j]])

    def norm_tile(nt, src_tile, src_toks, nch, dim, gain_fn, out_view, out_toks, banks=(6, 2)):
        sq, tsq, rs, trs = nt
        i = rr.get("nt", 0)
        rr["nt"] = (i + 1) % 2
        act(sq[i][:, 0:nch, :], src_tile, AF.Square, r=src_toks, w=[tsq[i]])
        bk = nbank(banks)
        for c in range(nch):
            mm(bank(bk), onesb[:], sq[i][:, c, :], c == 0, c == nch - 1, r=[tsq[i], TC], w=[PB[bk]])
        act(rs[i][:], bank(bk), AF.Ln, r=[PB[bk], TC], w=[trs[i]], scale=1.0 / dim, bias=epst[:])
        act(rs[i][:], rs[i][:], AF.Exp, r=[trs[i]], w=[trs[i]], scale=-0.5)
        for c in range(nch):
            dve("scalar_tensor_tensor", out_view[:, c, :], src_tile[:, c, :], gain_fn(c), rs[i][:], ALU.mult, ALU.mult,
                r=src_toks + [trs[i], TC], w=out_toks, nowaw=True)

    def even_mixer(e, layer):
        K16, K48, K76, K96, K104 = 16384, 49152, 77824, 98304, 98304 + 16384
        win = dr["w_in"][e].rearrange("(kc p) n -> p kc n", p=128)
        wuq_d = dr["w_uq"][e].rearrange("(kc p) n -> p kc n", p=128)
        wukv_d = dr["w_ukv"][e].rearrange("(kc p) n -> p kc n", p=128)
        wout_d = dr["w_out"][e].rearrange("(kc p) n -> p kc n", p=128)
        S.barrier()
        b0 = Bump(arena, 0, K16)
        OT = b0.alloc((4, SEQ), BF16)
        tot = [[Tok() for _ in range(4)] for _ in range(4)]
        bx = Bump(arena, K16, K48)
        xn = bx.alloc((8, SEQ), BF16)
        txn = [Tok() for _ in range(4)]
        rmsnorm_fm(Bump(arena, K48, ARENA), hT, h_toks, 8, D, lambda c: gcol(G_MIX + layer, c), xn, txn)
        S.barrier()
        ba = Bump(arena, K48, K76)
        cqn = ba.alloc((4, SEQ), BF16)
        tcq = [Tok() for _ in range(4)]
        ckvn = ba.alloc((2, SEQ), BF16)
        tckv = [Tok() for _ in range(4)]
        kpe = ba.alloc((SEQ,), BF16)
        tkpe = [Tok() for _ in range(4)]
        bt = Bump(arena, K76, ARENA)
        wcq = bt.alloc((8, 512), BF16)
        wckv = bt.alloc((8, 256), BF16)
        wkr = bt.alloc((8, 32), BF16)
        wkrr = bt.alloc((8, 32), BF16)
        tw = Tok()
        S.dma("pool", wcq, win[:, :, 2592:3104], w=[tw])
        S.dma("pool", wckv, win[:, :, 3104:3360], w=[tw])
        S.dma("pool", wkr, win[:, :, 3360:3392], w=[tw])
        dve("tensor_scalar", wkrr[:, :, 0:16], wkr[:, :, 16:32], -1.0, None, ALU.mult, r=[tw], w=[tw])
        dve("tensor_copy", wkrr[:, :, 16:32], wkr[:, :, 0:16], r=[tw], w=[tw])
        nt = ([bt.alloc((4, 512), BF16) for _ in range(2)], [Tok() for _ in range(2)],
              [bt.alloc((512,), F32) for _ in range(2)], [Tok() for _ in range(2)])
        cqr = [bt.alloc((4, 512), BF16) for _ in range(2)]
        tcqr = [Tok() for _ in range(2)]
        ckr = [bt.alloc((2, 512), BF16) for _ in range(2)]
        tckr = [Tok() for _ in range(2)]
        t1 = [bt.alloc((512,), F32) for _ in range(2)]
        t2 = [bt.alloc((512,), F32) for _ in range(2)]
        tt12 = [Tok() for _ in range(2)]
        for tt in range(4):
            i = tt % 2
            for oc in range(4):
                bk = nbank((0, 4))
                for kc in range(8):
                    mm(bank(bk), wcq[:, kc, oc * 128:(oc + 1) * 128], xn[:, kc, tsl(tt)], kc == 0, kc == 7, r=[tw, txn[tt]], w=[PB[bk]])
                act(cqr[i][:, oc, :], bank(bk), AF.Copy, r=[PB[bk]], w=[tcqr[i]])
            norm_tile(nt, cqr[i][:], [tcqr[i]], 4, 512, lambda c: qkvg[:, e, c:c + 1], cqn[:, :, tsl(tt)], [tcq[tt]])
            for oc in range(2):
                bk = nbank((0, 4))
                for kc in range(8):
                    mm(bank(bk), wckv[:, kc, oc * 128:(oc + 1) * 128], xn[:, kc, tsl(tt)], kc == 0, kc == 7, r=[tw, txn[tt]], w=[PB[bk]])
                act(ckr[i][:, oc, :], bank(bk), AF.Copy, r=[PB[bk]], w=[tckr[i]])
            norm_tile(nt, ckr[i][:], [tckr[i]], 2, 256, lambda c: qkvg[:, e, 4 + c:5 + c], ckvn[:, :, tsl(tt)], [tckv[tt]])
            ba_, bb_ = nbank((0, 4)), nbank((0, 4))
            for kc in range(8):
                mm(bank(ba_)[64:96, :], wkr[:, kc, :], xn[:, kc, tsl(tt)], kc == 0, kc == 7, r=[tw, txn[tt]], w=[PB[ba_]])
            for kc in range(8):
                mm(bank(bb_)[64:96, :], wkrr[:, kc, :], xn[:, kc, tsl(tt)], kc == 0, kc == 7, r=[tw, txn[tt]], w=[PB[bb_]])
            dve("tensor_tensor", t1[i][64:96, :], bank(ba_)[64:96, :], cos2[64:96, tsl(tt)], ALU.mult, r=[PB[ba_], TROPE], w=[tt12[i]])
            dve("tensor_tensor", t2[i][64:96, :], bank(bb_)[64:96, :], sin2[64:96, tsl(tt)], ALU.mult, r=[PB[bb_], TROPE], w=[tt12[i]])
            dve("tensor_tensor", kpe[64:96, tsl(tt)], t1[i][64:96, :], t2[i][64:96, :], ALU.add, r=[tt12[i]], w=[tkpe[tt]])
        S.barrier()
        if EVEN_STOP < 1:
            return
        bb = Bump(arena, K16, K48)
        V = bb.alloc((16, 8, 64), BF16)
        tv = Tok()
        wukv = bb.alloc((2, 1024), BF16)
        wuq = bb.alloc((4, 768), BF16)
        wuqr = bb.alloc((4, 8, 32), BF16)
        twb = Tok()
        S.dma("pool", wukv, wukv_d, w=[twb])
        S.dma("pool", wuq, wuq_d, w=[twb])
        for h in range(8):
            dve("tensor_scalar", wuqr[:, :, h, 0:16], wuq[:, :, h * 96 + 80:h * 96 + 96], -1.0, None, ALU.mult, r=[twb], w=[twb])
            dve("tensor_copy", wuqr[:, :, h, 16:32], wuq[:, :, h * 96 + 64:h * 96 + 80], r=[twb], w=[twb])
        bh = Bump(arena, K76, ARENA)
        Qh = [bh.alloc((SEQ,), BF16) for _ in range(2)]
        Kh = [bh.alloc((SEQ,), BF16) for _ in range(2)]
        tqk = [[Tok() for _ in range(4)] for _ in range(2)]
        Va = [bh.alloc((16, 128), BF16) for _ in range(2)]
        tva = [Tok() for _ in range(2)]
        PT = [bh.alloc((512,), BF16) for _ in range(4)]
        tpt = [Tok() for _ in range(4)]
        rden = [bh.alloc((512,), F32) for _ in range(2)]
        trd = [Tok() for _ in range(2)]
        rsh = [bh.alloc((512,), F32) for _ in range(2)]
        trs_ = [Tok() for _ in range(2)]
        u1 = [bh.alloc((512,), F32) for _ in range(2)]
        u2 = [bh.alloc((512,), F32) for _ in range(2)]
        tu = [Tok() for _ in range(2)]
        for sc in range(16):
            for ct in range(2):
                bk = nbank((6, 2))
                for kc in range(2):
                    mm(bank(bk), ckvn[:, kc, sc * 128:(sc + 1) * 128], wukv[:, kc, ct * 512:(ct + 1) * 512], kc == 0, kc == 1,
                       r=[twb, tckv[sc // 4]], w=[PB[bk]])
                src = bank(bk).rearrange("p (h c) -> p h c", h=4)[:, :, 64:128]
                if ct == 0:
                    act(V[:, sc, 0:4, :], src, AF.Copy, r=[PB[bk]], w=[tv])
                else:
                    dve("tensor_copy", V[:, sc, 4:8, :], src, r=[PB[bk]], w=[tv])
        dve("memset", Va[0][:, :, 64:128], 1.0, r=[], w=[tva[0]])
        dve("memset", Va[1][:, :, 0:64], 1.0, r=[], w=[tva[1]])
        for hb in range(2):
            dve("tensor_copy", Kh[hb][64:96, :], kpe[64:96, :], r=tkpe, w=tqk[hb])
        scale = 96.0 ** -0.5
        cnt = {"ip": 0, "ir": 0, "iu": 0}

        def head_proj(h):
            hb = h % 2
            po = hb * 64
            dve("tensor_copy", Va[hb][:, :, po:po + 64], V[:, :, h, :], r=[tv], w=[tva[hb]])
            for tt in range(4):
                bk = nbank((6, 2))
                for kc in range(4):
                    mm(bank(bk)[0:64, :], wuq[:, kc, h * 96:h * 96 + 64], cqn[:, kc, tsl(tt)], kc == 0, kc == 3, r=[twb, tcq[tt]], w=[PB[bk]])
                dve("tensor_copy", Qh[hb][0:64, tsl(tt)], bank(bk)[0:64, :], r=[PB[bk]], w=[tqk[hb][tt]])
                bk = nbank((6, 2))
                for kc in range(2):
                    mm(bank(bk)[0:64, :], wukv[:, kc, h * 128:h * 128 + 64], ckvn[:, kc, tsl(tt)], kc == 0, kc == 1, r=[twb, tckv[tt]], w=[PB[bk]])
                dve("tensor_copy", Kh[hb][0:64, tsl(tt)], bank(bk)[0:64, :], r=[PB[bk]], w=[tqk[hb][tt]])
                ba_, bb_ = nbank((6, 2)), nbank((6, 2))
                for kc in range(4):
                    mm(bank(ba_)[64:96, :], wuq[:, kc, h * 96 + 64:h * 96 + 96], cqn[:, kc, tsl(tt)], kc == 0, kc == 3, r=[twb, tcq[tt]], w=[PB[ba_]])
                for kc in range(4):
                    mm(bank(bb_)[64:96, :], wuqr[:, kc, h, :], cqn[:, kc, tsl(tt)], kc == 0, kc == 3, r=[twb, tcq[tt]], w=[PB[bb_]])
                i = cnt["iu"] % 2
                cnt["iu"] += 1
                dve("tensor_tensor", u1[i][64:96, :], bank(ba_)[64:96, :], cos2[64:96, tsl(tt)], ALU.mult, r=[PB[ba_], TROPE], w=[tu[i]])
                dve("tensor_tensor", u2[i][64:96, :], bank(bb_)[64:96, :], sin2[64:96, tsl(tt)], ALU.mult, r=[PB[bb_], TROPE], w=[tu[i]])
                dve("tensor_tensor", Qh[hb][64:96, tsl(tt)], u1[i][64:96, :], u2[i][64:96, :], ALU.add, r=[tu[i]], w=[tqk[hb][tt]])
                yield tt

        def s_issue(h, qt, kc):
            hb = h % 2
            bs = nbank((0, 4))
            mm(bank(bs), Kh[hb][0:96, kc * 128:(kc + 1) * 128], Qh[hb][0:96, tsl(qt)], True, True,
               r=[tqk[hb][kc // 4], tqk[hb][qt]], w=[PB[bs]])
            return bs

        for _ in head_proj(0):
            pass
        pending = []
        for h in range(8):
            hb = h % 2
            po = hb * 64
            dpo = 64 - po
            gen = head_proj(h + 1) if h < 7 else iter(())
            units = [(qt, kc) for qt in range(4) for kc in range(16)]
            sb = {}
            LA = 3
            for n_ in range(LA):
                sb[n_] = s_issue(h, *units[n_])
            bo = None
            for n_, (qt, kc) in enumerate(units):
                if kc == 0:
                    bo = nbank((4, 2))
                bs = sb.pop(n_)
                i = cnt["ip"] % 4
                cnt["ip"] += 1
                act(PT[i][:], bank(bs), AF.Exp, r=[PB[bs]], w=[tpt[i]], scale=scale)
                mm(bank(bo), Va[hb][:, kc, :], PT[i][:], kc == 0, kc == 15, r=[tva[hb], tpt[i]], w=[PB[bo]])
                if n_ + LA < len(units):
                    sb[n_ + LA] = s_issue(h, *units[n_ + LA])
                if kc == 15:
                    j = cnt["ir"] % 2
                    cnt["ir"] += 1
                    dve("reciprocal", rden[j][dpo:dpo + 64, :], bank(bo)[dpo:dpo + 64, :], r=[PB[bo]], w=[trd[j]])

                    def epi(bo=bo, j=j, po=po, dpo=dpo, h=h, qt=qt):
                        br = nbank((6, 2))
                        mm(bank(br), shiftm[dpo:dpo + 64, :], rden[j][dpo:dpo + 64, :], True, True, r=[TC, trd[j]], w=[PB[br]])
                        dve("tensor_copy", rsh[j][po:po + 64, :], bank(br)[po:po + 64, :], r=[PB[br]], w=[trs_[j]])
                        dve("tensor_tensor", OT[po:po + 64, h // 2, tsl(qt)], bank(bo)[po:po + 64, :], rsh[j][po:po + 64, :], ALU.mult,
                            r=[PB[bo], trs_[j]], w=[tot[h // 2][qt]])
                    pending.append(epi)
                if kc == 8 and pending:
                    pending.pop(0)()
                if kc == 3:
                    next(gen, None)
        while pending:
            pending.pop(0)()
        tots = Tok()
        S.dma("sp", ot_d, OT[:].rearrange("p a b -> p (a b)"), r=[t for row in tot for t in row], w=[tots])
        S.barrier()
        if EVEN_STOP < 2:
            return
        rmsnorm_fm(Bump(arena, K48, ARENA), hT, h_toks, 8, D, lambda c: gcol(G_MIX + layer, c), xn, txn)
        S.barrier()
        bc = Bump(arena, K48, K96)
        xbc = bc.alloc((12, SEQ), BF16)
        txb = [Tok() for _ in range(16)]
        bt = Bump(arena, K96, ARENA)
        raw = [bt.alloc((SEQ + 4,), BF16) for _ in range(2)]
        traw = [Tok() for _ in range(2)]
        dg = bt.alloc((12, 5, 128), BF16)
        tdg = Tok()
        ring = Ring(bt, 3, 2048)
        for oc in range(12):
            dve("tensor_tensor", dg[:, oc], identb[:].unsqueeze(1).to_broadcast([128, 5, 128]),
                convw[:, e, oc, :].unsqueeze(2).to_broadcast([128, 5, 128]), ALU.mult, r=[TC], w=[tdg], nowaw=True)
        for i in range(2):
            dve("memset", raw[i][:, 0:2], 0.0, r=[], w=[traw[i]])
            dve("memset", raw[i][:, SEQ + 2:SEQ + 4], 0.0, r=[], w=[traw[i]])
        for oc in range(12):
            i = oc % 2
            slot, tk = ring.next()
            wv = slot[:, 0:2048].bitcast(BF16).rearrange("p (kc n) -> p kc n", kc=8)
            S.dma("pool", wv, win[:, :, 1024 + oc * 128:1024 + (oc + 1) * 128], w=[tk])
            for tt in range(4):
                bk = nbank((0, 4))
                for kc in range(8):
                    mm(bank(bk), wv[:, kc, :], xn[:, kc, tsl(tt)], kc == 0, kc == 7, r=[tk, txn[tt]], w=[PB[bk]])
                dve("tensor_copy", raw[i][:, 2 + tt * 512:2 + (tt + 1) * 512], bank(bk), r=[PB[bk]], w=[traw[i]], nowaw=True)
            for tt in range(4):
                bk = nbank((4, 4))
                for t in range(5):
                    mm(bank(bk), dg[:, oc, t, :], raw[i][:, t + tt * 512:t + (tt + 1) * 512], t == 0, t == 4, r=[tdg, traw[i]], w=[PB[bk]])
                act(xbc[:, oc, tsl(tt)], bank(bk), AF.Silu, r=[PB[bk], TC], w=txb[tt * 4:tt * 4 + 4], bias=convb[:, e, oc:oc + 1])
        S.barrier()
        if EVEN_STOP < 3:
            return
        bs_ = Bump(arena, K96, K104)
        dt = bs_.alloc((16, 32), F32)
        la = bs_.alloc((16, 32), F32)
        cum = bs_.alloc((16, 32), F32)
        ecum = bs_.alloc((16, 32), F32)
        dtb = bs_.alloc((32,), F32)
        eal = bs_.alloc((32,), F32)
        dsk = bs_.alloc((16,), F32)
        tsm = Tok()
        bt = Bump(arena, K104, ARENA)
        wz = Bump(arena, 0, K16).alloc((8, 1024), BF16)
        wdt = bt.alloc((8, 32), BF16)
        twz = Tok()
        zt = [bt.alloc((1024,), BF16) for _ in range(2)]
        tzt = [Tok() for _ in range(2)]
        S.dma("pool", wz, win[:, :, 0:1024], w=[twz])
        S.dma("pool", wdt, win[:, :, 2560:2592], w=[twz])
        S.dma("sp", dtb[:], dtb_d[e:e + 1, :].to_broadcast([128, 32]), w=[tsm])
        S.dma("sp", eal[:], alog_d[e:e + 1, :].to_broadcast([128, 32]), w=[tsm])
        S.dma("sp", dsk[:], ssdd_d[e:e + 1, :].to_broadcast([128, 16]), w=[tsm])
        tzs = [Tok() for _ in range(16)]
        for sc in range(16):
            i = sc % 2
            for ct in range(2):
                bk = nbank((0, 4))
                for kc in range(8):
                    mm(bank(bk), xn[:, kc, sc * 128:(sc + 1) * 128], wz[:, kc, ct * 512:(ct + 1) * 512], kc == 0, kc == 7,
                       r=[twz, txn[sc // 4]], w=[PB[bk]])
                act(zt[i][:, ct * 512:(ct + 1) * 512], bank(bk), AF.Silu, r=[PB[bk]], w=[tzt[i]])
            S.dma("sp", zs_d[sc * 128:(sc + 1) * 128, :], zt[i][:], r=[tzt[i]], w=[tzs[sc]])
            bk = nbank((4, 2))
            for kc in range(8):
                mm(bank(bk)[:, 0:32], xn[:, kc, sc * 128:(sc + 1) * 128], wdt[:, kc, :], kc == 0, kc == 7, r=[twz, txn[sc // 4]], w=[PB[bk]])
            dve("tensor_tensor", dt[:, sc, :], bank(bk)[:, 0:32], dtb[:], ALU.add, r=[PB[bk], tsm], w=[tsm])
        act(dt[:], dt[:], AF.Exp, r=[tsm], w=[tsm])
        act(dt[:], dt[:], AF.Ln, r=[tsm], w=[tsm], bias=1.0)
        act(eal[:], eal[:], AF.Exp, r=[tsm], w=[tsm])
        dve("scalar_tensor_tensor", la[:], dt[:], -1.0, eal[:].unsqueeze(1).to_broadcast([128, 16, 32]), ALU.mult, ALU.mult, r=[tsm], w=[tsm])
        for d_ in range(2):
            bk = nbank((4, 2))
            mm(bank(bk)[:, 0:256].rearrange("p (a b) -> p a b", b=16), tri[:, d_, :], la[:, :, d_ * 16:(d_ + 1) * 16], True, True, r=[tsm, TC], w=[PB[bk]])
            dve("tensor_copy", cum[:, :, d_ * 16:(d_ + 1) * 16], bank(bk)[:, 0:256].rearrange("p (a b) -> p a b", b=16), r=[PB[bk]], w=[tsm])
        act(ecum[:], cum[:], AF.Exp, r=[tsm], w=[tsm])
        negc = la
        totr = bs_.alloc((16, 32), F32)
        dtw = bs_.alloc((16, 32), F32)
        etot = bs_.alloc((16, 32), F32)
        selt = bs_.alloc((2, 128), F32)
        S.dma("sp", selt, sel_d, w=[tsm])
        for d_ in range(2):
            bk = nbank((4, 2))
            mm(bank(bk)[:, 0:256].rearrange("p (a b) -> p a b", b=16), selt[:, d_, :], cum[:, :, d_ * 16:(d_ + 1) * 16], True, True, r=[tsm], w=[PB[bk]])
            dve("tensor_copy", totr[:, :, d_ * 16:(d_ + 1) * 16], bank(bk)[:, 0:256].rearrange("p (a b) -> p a b", b=16), r=[PB[bk]], w=[tsm])
        dve("tensor_tensor", dtw[:], totr[:], cum[:], ALU.subtract, r=[tsm], w=[tsm])
        act(dtw[:], dtw[:], AF.Exp, r=[tsm], w=[tsm])
        dve("tensor_tensor", dtw[:], dtw[:], dt[:], ALU.mult, r=[tsm], w=[tsm])
        act(etot[:], totr[:], AF.Exp, r=[tsm], w=[tsm])
        dve("tensor_scalar", negc[:], cum[:], -1.0, None, ALU.mult, r=[tsm], w=[tsm])
        S.barrier()
        if EVEN_STOP < 4:
            return
        bt1 = Bump(arena, 0, K48)
        bt = Bump(arena, K104, ARENA)
        Dm = bt1.alloc((16, 128), BF16)
        tdm = Tok()
        dve("tensor_tensor", Dm[:], identb[:].unsqueeze(1).to_broadcast([128, 16, 128]), dsk[:].unsqueeze(2).to_broadcast([128, 16, 128]), ALU.mult,
            r=[TC, tsm], w=[tdm])
        XT1 = bt1.alloc((1024,), BF16)
        txt1 = Tok()
        BT1 = bt1.alloc((256,), BF16)
        tbt1 = Tok()
        xdtF = bt1.alloc((1024,), BF16)
        xdtB = bt1.alloc((1024,), BF16)
        xw = bt1.alloc((1024,), BF16)
        txdF, txdB, txw = Tok(), Tok(), Tok()
        CBm = bt1.alloc((2, 2, 128), F32)
        tcb = [Tok(), Tok()]
        Eb = [bt1.alloc((4, 128), F32) for _ in range(2)]
        teb = [Tok() for _ in range(2)]
        MT = [bt1.alloc((16, 128), BF16) for _ in range(2)]
        tmt = [[Tok() for _ in range(4)] for _ in range(2)]
        Hf = bt1.alloc((1024,), F32)
        Hb = bt1.alloc((1024,), BF16)
        thf, thb = Tok(), Tok()
        hinl = bt1.alloc((1024,), BF16)
        thin = Tok()
        t1 = bt1.alloc((1024,), F32)
        yv = bt1.alloc((1024,), F32)
        tt1, tyv = Tok(), Tok()
        zl = bt.alloc((1024,), BF16)
        tzl = Tok()
        ynb = bt.alloc((1024,), BF16)
        tynb = Tok()
        junk = xw
        ssq = bt.alloc((1,), F32)
        tss = Tok()
        thd = [Tok() for _ in range(16)]
        pb7 = bank(7).bitcast(BF16)
        pb6 = bank(6).bitcast(BF16)
        h3 = lambda v: v.rearrange("p (h c) -> p h c", c=64)

        def make_xt(sc):
            csl = slice(sc * 128, (sc + 1) * 128)
            for c in range(8):
                tr(pb7[:, c * 128:(c + 1) * 128], xbc[:, c, csl], identb[:], r=[txb[sc], TC], w=[PB[7]])
            dve("tensor_copy", XT1[:], pb7[:, :], r=[PB[7]], w=[txt1])
            for g in range(2):
                tr(pb6[:, g * 128:(g + 1) * 128], xbc[:, 8 + g, csl], identb[:], r=[txb[sc], TC], w=[PB[6]])
            dve("tensor_copy", BT1[:], pb6[:, 0:256], r=[PB[6]], w=[tbt1])

        def state_update(sc, d_, first, split=False):
            S.op("pool", lambda e: e.tensor_tensor(h3(xw[:]), h3(XT1[:]), dtw[:, sc, d_ * 16:(d_ + 1) * 16].unsqueeze(2).to_broadcast([128, 16, 64]), ALU.mult),
                 r=[txt1, tsm], w=[txw])
            bks = []
            for g in range(2):
                bk = (2 + g) if split else nbank((4, 2))
                bks.append(bk)
                mm(bank(bk), BT1[:, g * 128:(g + 1) * 128], xw[:, g * 512:(g + 1) * 512], True, True, r=[tbt1, txw], w=[PB[bk]])
            if not first:
                S.op("pool", lambda e: e.tensor_tensor(h3(Hf[:]), h3(Hf[:]), etot[:, sc, d_ * 16:(d_ + 1) * 16].unsqueeze(2).to_broadcast([128, 16, 64]), ALU.mult),
                     r=[thf, tsm], w=[thf])
            def fin():
                for g in range(2):
                    gs = slice(g * 512, (g + 1) * 512)
                    if first:
                        dve("tensor_copy", Hf[:, gs], bank(bks[g]), r=[PB[bks[g]]], w=[thf])
                    else:
                        dve("tensor_tensor", Hf[:, gs], Hf[:, gs], bank(bks[g]), ALU.add, r=[thf, PB[bks[g]]], w=[thf])
                act(Hb[:], Hf[:], AF.Copy, r=[thf], w=[thb])
            if split:
                return fin
            fin()

        for sc in range(15):
            make_xt(sc)
            state_update(sc, 0, sc == 0)
            S.dma("sp", hin_d[sc + 1], Hb[:], r=[thb], w=[thd[sc + 1]])
        if EVEN_STOP == 41:
            return
        t1s = [t1, bt.alloc((1024,), F32)]
        t1Bs = [bt1.alloc((1024,), F32), bt.alloc((1024,), F32)]
        zls = [zl, bt.alloc((1024,), BF16)]
        tt1s, tt1bs, tzls = [Tok(), Tok()], [Tok(), Tok()], [Tok(), Tok()]
        lnt = bt.alloc((1,), F32)

        def front_a(sc):
            csl = slice(sc * 128, (sc + 1) * 128)
            hasF, hasB = sc >= 1, sc <= 14
            par = sc % 2
            t1, t1B, zl, tt1, tt1b, tzl = t1s[par], t1Bs[par], zls[par], tt1s[par], tt1bs[par], tzls[par]
            if hasF:
                S.dma("sp", hinl[:], hin_d[sc], r=[thd[sc]], w=[thin])
            S.dma("sp", zl[:], zs_d[csl, :], r=[tzs[sc]], w=[tzl])
            bkc = nbank((4, 2))
            for g in range(2):
                mm(bank(bkc)[:, g * 128:(g + 1) * 128], xbc[:, 8 + g, csl], xbc[:, 10 + g, csl], True, True, r=[txb[sc]], w=[PB[bkc]])
            for d_ in range(2):
                dve("tensor_tensor", CBm[:, d_], bank(bkc)[:, 0:256].rearrange("p (g l) -> p g l", g=2),
                    tri[:, d_, :].unsqueeze(1).to_broadcast([128, 2, 128]), ALU.mult, r=[PB[bkc], TC], w=[tcb[d_]])
            ie = [0]

            def seg_group(d_, hb):
                g = hb // 2
                bk = nbank((4, 2))
                for i in range(4):
                    h = hb * 4 + i
                    tr(bank(bk)[:, i * 128:(i + 1) * 128], cum[:, sc, d_ * 16 + h:d_ * 16 + h + 1].to_broadcast([128, 128]), identf[:],
                       r=[tsm, TC], w=[PB[bk]])
                k = ie[0] % 2
                ie[0] += 1
                for i in range(4):
                    h = hb * 4 + i
                    S.op("act", lambda e, k=k, i=i, bk=bk, h=h: e.activation(out=Eb[k][:, i, :], in_=bank(bk)[:, i * 128:(i + 1) * 128], func=AF.Exp,
                                                                     bias=negc[:, sc, d_ * 16 + h:d_ * 16 + h + 1]),
                         r=[PB[bk], tsm], w=[teb[k]], nowaw=True)
                dve("scalar_tensor_tensor", MT[d_][:, hb * 4:(hb + 1) * 4, :], Eb[k][:], 1.0, CBm[:, d_, g, :].unsqueeze(1).to_broadcast([128, 4, 128]),
                    ALU.min, ALU.mult, r=[teb[k], tcb[d_]], w=[tmt[d_][hb]])

            groups = [(d_, hb) for d_ in range(2) for hb in range(4)]
            seg_group(*groups[0])
            seg_group(*groups[1])
            make_xt(sc)
            S.op("pool", lambda e: e.tensor_tensor(h3(xdtF[:]), h3(XT1[:]), dt[:, sc, 0:16].unsqueeze(2).to_broadcast([128, 16, 64]), ALU.mult),
                 r=[txt1, tsm], w=[txdF])
            S.op("pool", lambda e: e.tensor_tensor(h3(xdtB[:]), h3(XT1[:]), dt[:, sc, 16:32].unsqueeze(2).to_broadcast([128, 16, 64]), ALU.mult),
                 r=[txt1, tsm], w=[txdB])
            seg_group(*groups[2])
            seg_group(*groups[3])
            for g in range(2):
                gs = slice(g * 512, (g + 1) * 512)
                if hasF:
                    mm(bank(2), xbc[:, 10 + g, csl], hinl[:, gs], True, True, r=[txb[sc], thin], w=[PB[2]])
                    dve("tensor_tensor", h3(t1[:, gs]), h3(bank(2)), ecum[:, sc, g * 8:(g + 1) * 8].unsqueeze(2).to_broadcast([128, 8, 64]), ALU.mult,
                        r=[PB[2], tsm], w=[tt1], nowaw=True)
                if hasB:
                    mm(bank(3), xbc[:, 10 + g, csl], Hb[:, gs], True, True, r=[txb[sc], thb], w=[PB[3]])
                    dve("tensor_tensor", h3(t1B[:, gs]), h3(bank(3)), ecum[:, sc, 16 + g * 8:16 + (g + 1) * 8].unsqueeze(2).to_broadcast([128, 8, 64]), ALU.mult,
                        r=[PB[3], tsm], w=[tt1b], nowaw=True)
            seg_group(*groups[4])
            fin = None
            if sc >= 1:
                fin = state_update(sc, 1, sc == 15, split=True)
            seg_group(*groups[5])
            seg_group(*groups[6])
            seg_group(*groups[7])
            if fin is not None:
                fin()

        def front_b(sc):
            for h in range(16):
                hs = slice(h * 64, (h + 1) * 64)
                mm(ps[:, 0:1024][:, hs], MT[0][:, h, :], xdtF[:, hs], True, False, r=[tmt[0][h // 4], txdF], w=[PB[h // 8]])
                mm(ps[:, 0:1024][:, hs], MT[1][:, h, :], xdtB[:, hs], False, False, r=[tmt[1][h // 4], txdB], w=[PB[h // 8]])
                mm(ps[:, 0:1024][:, hs], Dm[:, h, :], XT1[:, hs], False, True, r=[tdm, txt1], w=[PB[h // 8]])

        def back(sc):
            csl = slice(sc * 128, (sc + 1) * 128)
            hasF, hasB = sc >= 1, sc <= 14
            par = sc % 2
            t1, t1B, zl, tt1, tt1b, tzl = t1s[par], t1Bs[par], zls[par], tt1s[par], tt1bs[par], tzls[par]
            for g in range(2):
                gs = slice(g * 512, (g + 1) * 512)
                if hasF:
                    dve("tensor_tensor", yv[:, gs], t1[:, gs], bank(g), ALU.add, r=[tt1, PB[g]], w=[tyv], nowaw=True)
                else:
                    dve("tensor_copy", yv[:, gs], bank(g), r=[PB[g]], w=[tyv], nowaw=True)
            if hasB:
                S.op("pool", lambda e: e.tensor_tensor(yv[:], yv[:], t1B[:], ALU.add), r=[tt1b, tyv], w=[tyv])
            S.op("pool", lambda e: e.tensor_tensor(yv[:], yv[:], zl[:], ALU.mult), r=[tyv, tzl], w=[tyv])
            act(junk[:], yv[:], AF.Square, r=[tyv], w=[tss, txw], accum_out=ssq[:])
            act(lnt[:], ssq[:], AF.Ln, r=[tss, TC], w=[tss], scale=1.0 / 1024, bias=epst[:])
            act(ssq[:], lnt[:], AF.Exp, r=[tss], w=[tss], scale=-0.5)
            dve("tensor_scalar", ynb[:], yv[:], ssq[:, 0:1], 0.0, ALU.mult, ALU.add, r=[tyv, tss], w=[tynb])
            for c in range(8):
                tr(pb7[:, c * 128:(c + 1) * 128], ynb[:, c * 128:(c + 1) * 128], identb[:], r=[tynb, TC], w=[PB[7]])
            dve("tensor_tensor", xbc[:, 0:8, csl], pb7[:, :].rearrange("p (c l) -> p c l", c=8),
                gains[:, (G_SSD + e) * 8:(G_SSD + e) * 8 + 8].unsqueeze(2).to_broadcast([128, 8, 128]), ALU.mult,
                r=[PB[7], TC], w=[txb[sc]])

        front_a(15)
        front_b(15)
        for sc in range(14, -1, -1):
            front_a(sc)
            back(sc + 1)
            front_b(sc)
        back(0)
        S.barrier()
        if EVEN_STOP < 5:
            return
        OT = Bump(arena, 0, K16).alloc((4, SEQ), BF16)
        tot = [[Tok() for _ in range(4)] for _ in range(4)]
        S.dma("sp", OT[:].rearrange("p a b -> p (a b)"), ot_d, r=[tots], w=[t for row in tot for t in row])
        ring = Ring(Bump(arena, K96, ARENA), 3, 6144)
        proj_accum(None, wout_d, 12, None, lambda j, tt: (txb[tt * 4:tt * 4 + 4] if j < 8 else [tot[j - 8][tt]]), 1.0, ring, ybanks=(0, 4),
                   act_view=lambda j, tt: (xbc[:, j, tsl(tt)] if j < 8 else OT[:, j - 8, tsl(tt)]))

    if plan is None:
        plan = []
        for layer in range(DEPTH):
            plan += [("ffn1", layer), ("mix", layer), ("xa", layer), ("ffn2", layer)]
    for s in range(nseq):
        load_x(s)
        if any(p[0] == "xa" for p in plan):
            prep_mem(s)
        if any(p[0] == "mix" and p[1] % 2 == 0 for p in plan):
            rope_tables(s)
        for kind, layer in plan:
            if kind == "ffn1":
                ffn(dr["ffn1_w_gu"][layer], dr["ffn1_w_down"][layer], G_FFN1 + layer)
            elif kind == "ffn2":
                ffn(dr["ffn2_w_gu"][layer], dr["ffn2_w_down"][layer], G_FFN2 + layer)
            elif kind == "xa":
                xattn(layer)
            elif kind == "mix":
                if layer % 2 == 1:
                    fnet(layer)
                else:
                    even_mixer(layer // 2, layer)
        final(s)
    S.barrier()
    S.emit()
    return nc


def _fm(v):
    return np.ascontiguousarray(np.asarray(v, np.float32).reshape(-1, 128).T)


def host_consts(inp):
    c = {}
    g = np.zeros((128, NGAIN * 8), np.float32)

    def put(idx, v):
        g[:, idx * 8:(idx + 1) * 8] = _fm(v)

    for l in range(4):
        put(G_FFN1 + l, inp["ffn1_norm"][l])
        put(G_MIX + l, inp["mix_norm"][l])
        put(G_XA + l, inp["xa_norm"][l])
        put(G_FFN2 + l, inp["ffn2_norm"][l])
    put(G_FINAL, inp["final_norm"])
    put(G_MEM, inp["mem_norm"])
    for e in range(2):
        put(G_SSD + e, inp["ssd_norm"][e])
    c["gains"] = g
    q = np.zeros((128, 2, 6), np.float32)
    for e in range(2):
        q[:, e, 0:4] = _fm(inp["q_norm"][e])
        q[:, e, 4:6] = _fm(inp["kv_norm"][e])
    c["qkvgains"] = q
    cw = np.asarray(inp["conv_w"], np.float32)
    c["convw"] = np.ascontiguousarray(cw.reshape(2, 5, 12, 128).transpose(3, 0, 2, 1))
    cb = np.asarray(inp["conv_b"], np.float32)
    c["convb"] = np.ascontiguousarray(cb.reshape(2, 12, 128).transpose(2, 0, 1))
    c["dt_bias"] = np.ascontiguousarray(np.asarray(inp["dt_bias"], np.float32).reshape(2, 32))
    c["a_log"] = np.ascontiguousarray(np.asarray(inp["a_log"], np.float32).reshape(2, 32))
    c["ssd_d"] = np.ascontiguousarray(np.asarray(inp["ssd_d"], np.float32))
    c["ident"] = np.eye(128, dtype=np.float32)
    sel = np.zeros((128, 2, 128), np.float32)
    sel[127, 0, :] = 1.0
    sel[0, 1, :] = 1.0
    c["sel"] = sel
    s_ = np.arange(128)
    tri = np.zeros((128, 2, 128), np.float32)
    tri[:, 0, :] = (s_[:, None] <= s_[None, :])
    tri[:, 1, :] = (s_[:, None] >= s_[None, :])
    c["tri"] = tri
    inv = 1.0 / (10000.0 ** (np.arange(0, 32, 2, dtype=np.float32) / 32.0))
    c["invf"] = np.concatenate([inv, inv]).astype(np.float32).reshape(32, 1)
    c["shiftm"] = np.ascontiguousarray(np.roll(np.eye(128, dtype=np.float32), 64, axis=1))
    j = np.arange(256)
    angc = 2 * np.pi * ((j[:, None] * j[None, :]) % 256) / 256.0
    cd = np.concatenate([np.cos(angc), np.sin(angc)], axis=1)
    c["cdft"] = np.ascontiguousarray(cd.reshape(2, 128, 512).transpose(1, 0, 2)).astype(ml_dtypes.bfloat16)
    k = np.arange(SEQ)
    angs = 2 * np.pi * ((k[:, None] * k[None, :]) % SEQ) / float(SEQ)
    sd = np.stack([np.cos(angs), -np.sin(angs)], axis=1)
    c["sdft"] = np.ascontiguousarray(sd).astype(ml_dtypes.bfloat16)
    return c


_CACHE = {}


def kernel(**inputs):
    inp = {k: np.asarray(v) for k, v in inputs.items()}
    if "nc" not in _CACHE:
        _CACHE["nc"] = build_program()
    nc = _CACHE["nc"]
    consts = host_consts(inp)
    shared = {name: np.ascontiguousarray(inp[name], dtype=np.float32) for name, _ in WEIGHT_SPECS}
    shared.update(consts)
    in_maps = []
    for c in range(NCORES):
        m = dict(shared)
        m["x"] = np.ascontiguousarray(inp["x"][c * NSEQ:(c + 1) * NSEQ], dtype=np.float32)
        m["mem"] = np.ascontiguousarray(inp["mem"][c * NSEQ:(c + 1) * NSEQ], dtype=np.float32)
        m["positions"] = np.ascontiguousarray(inp["positions"][c * NSEQ:(c + 1) * NSEQ], dtype=np.int32)
        in_maps.append(m)
    res = run_bass_kernel_spmd(nc, in_maps, core_ids=list(range(NCORES)))
    return np.concatenate([np.asarray(r["out"], np.float32) for r in res.results], axis=0)
```

```python
import math
import numpy as np
import ml_dtypes
import concourse.bass as bass
import concourse.mybir as mybir
from concourse.bass_utils import run_bass_kernel_spmd

F32 = mybir.dt.float32
BF16 = mybir.dt.bfloat16
I32 = mybir.dt.int32
U8 = mybir.dt.uint8
AF = mybir.ActivationFunctionType
ALU = mybir.AluOpType

NCORES = 8
NSEQ = 2
SEQ = 2048
D = 1024
DFF = 2816
NMEM = 256
EPS = 1e-6
DEPTH = 4
DBG = set()
SW2 = 99
EVEN_STOP = 99

ENGS = ("pe", "act", "dve", "pool", "sp")
NDSEM = {"sp": 6, "act": 3, "pool": 6}


class Tok:
    __slots__ = ("w", "rs")

    def __init__(self):
        self.w = None
        self.rs = {}


class Op:
    __slots__ = ("eng", "fn", "idx", "key", "val", "waits", "sig", "sigval", "K", "dma")


class Sched:
    def __init__(self, nc):
        self.nc = nc
        self.ops = {e: [] for e in ENGS}
        self.cnt = {e: 0 for e in ENGS}
        self.K = {e: {} for e in ENGS}
        self.dcnt = {}
        self.drr = {q: 0 for q in NDSEM}
        self.last = {}

    def _add(self, eng, fn, deps, dma):
        op = Op()
        op.eng, op.fn, op.dma, op.sig, op.sigval = eng, fn, dma, False, 0
        self.cnt[eng] += 1
        op.idx = self.cnt[eng]
        if dma:
            slot = self.drr[eng]
            self.drr[eng] = (slot + 1) % NDSEM[eng]
            op.key = (eng, slot)
            self.dcnt[op.key] = self.dcnt.get(op.key, 0) + 1
            op.val = self.dcnt[op.key]
        else:
            op.key = eng
            op.val = op.idx
        K = self.K[eng]
        waits = []
        for d in sorted(deps, key=lambda d: -d.val):
            if eng == "pe" and d.eng == "pe" and not d.dma:
                continue
            if K.get(d.key, 0) >= d.val:
                continue
            waits.append(d)
            d.sig = True
            for k, v in d.K.items():
                if K.get(k, 0) < v:
                    K[k] = v
            if K.get(d.key, 0) < d.val:
                K[d.key] = d.val
        op.waits = waits
        op.K = dict(K)
        self.ops[eng].append(op)
        if fn is not None:
            self.last[op.key] = op
        return op

    def op(self, eng, fn, r=(), w=(), dma=False, nowaw=False):
        deps = set()
        for t in r:
            if t.w is not None:
                deps.add(t.w)
        for t in w:
            if t.w is not None and not (nowaw and t.w.eng == eng and not t.w.dma):
                deps.add(t.w)
            for o in t.rs.values():
                deps.add(o)
        op = self._add(eng, fn, deps, dma)
        for t in r:
            t.rs[op.key] = op
        for t in w:
            t.w = op
            t.rs = {}
        return op

    def dma(self, q, out, in_, r=(), w=()):
        return self.op(q, lambda e: e.dma_start(out=out, in_=in_), r, w, dma=True)

    def barrier(self):
        lasts = set(self.last.values())
        for e in ENGS:
            self._add(e, None, lasts, False)

    def emit(self):
        nc = self.nc
        from contextlib import ExitStack
        with ExitStack() as es:
            esem = {e: es.enter_context(nc.semaphore("s_" + e)) for e in ENGS}
            dsem = {}
            for q, n in NDSEM.items():
                for i in range(n):
                    dsem[(q, i)] = es.enter_context(nc.semaphore("d_%s%d" % (q, i)))
            for e in ENGS:
                c = 0
                for op in self.ops[e]:
                    if op.dma:
                        continue
                    if op.sig and op.fn is not None:
                        c += 1
                    op.sigval = c

            def run(e, eng):
                for op in self.ops[e]:
                    for d in op.waits:
                        if d.dma:
                            eng.wait_ge(dsem[d.key], 16 * d.val)
                        elif d.sigval > 0:
                            eng.wait_ge(esem[d.eng], d.sigval)
                    if op.fn is None:
                        continue
                    ins = op.fn(eng)
                    if op.dma:
                        ins.then_inc(dsem[op.key], 16)
                    elif op.sig:
                        ins.then_inc(esem[e], 1)

            block = es.enter_context(nc.Block())

            @block.tensor
            def _(eng):
                run("pe", eng)

            @block.scalar
            def _(eng):
                run("act", eng)

            @block.vector
            def _(eng):
                run("dve", eng)

            @block.gpsimd
            def _(eng):
                run("pool", eng)

            @block.sync
            def _(eng):
                run("sp", eng)


WEIGHT_SPECS = [
    ("ffn1_w_gu", [4, 1024, 5632]), ("ffn1_w_down", [4, 2816, 1024]),
    ("xa_wq", [4, 1024, 1024]), ("xa_wkv", [4, 1024, 2048]), ("xa_wo", [4, 1024, 1024]),
    ("ffn2_w_gu", [4, 1024, 5632]), ("ffn2_w_down", [4, 2816, 1024]),
    ("w_in", [2, 1024, 3392]), ("w_uq", [2, 512, 768]), ("w_ukv", [2, 256, 1024]),
    ("w_out", [2, 1536, 1024]), ("fnet_w_out", [2, 1024, 1024]),
]
G_FFN1, G_MIX, G_XA, G_FFN2, G_FINAL, G_MEM, G_SSD = 0, 4, 8, 12, 16, 17, 18
NGAIN = 20


class Bump:
    def __init__(self, arena, base, limit):
        self.arena, self.off, self.limit = arena, base, limit

    def alloc(self, free_shape, dtype):
        esz = {F32: 4, BF16: 2, I32: 4, U8: 1}[dtype]
        n = esz
        for s in free_shape:
            n *= s
        off = (self.off + 31) // 32 * 32
        assert off + n <= self.limit, ("arena overflow", off, n, self.limit)
        self.off = off + n
        v = self.arena[:, off:off + n]
        if dtype != U8:
            v = v.bitcast(dtype)
        if len(free_shape) == 2:
            v = v.rearrange("p (a b) -> p a b", b=free_shape[1])
        elif len(free_shape) == 3:
            v = v.rearrange("p (a b c) -> p a b c", b=free_shape[1], c=free_shape[2])
        elif len(free_shape) == 4:
            v = v.rearrange("p (a b c d) -> p a b c d", b=free_shape[1], c=free_shape[2], d=free_shape[3])
        return v


def build_program(plan=None, nseq=NSEQ):
    nc = bass.Bass("TRN2", target_bir_lowering=False)
    dr = {}

    def din(name, shape, dt=F32):
        dr[name] = nc.dram_tensor(name, shape, dt, kind="ExternalInput").ap()
        return dr[name]

    x_d = din("x", [nseq, SEQ, D])
    mem_d = din("mem", [nseq, NMEM, D])
    pos_d = din("positions", [nseq, SEQ], I32)
    for name, shape in WEIGHT_SPECS:
        din(name, shape)
    gains_d = din("gains", [128, NGAIN * 8])
    qkvg_d = din("qkvgains", [128, 2, 6])
    convw_d = din("convw", [128, 2, 12, 5])
    convb_d = din("convb", [128, 2, 12])
    dtb_d = din("dt_bias", [2, 32])
    alog_d = din("a_log", [2, 32])
    ssdd_d = din("ssd_d", [2, 16])
    ident_d = din("ident", [128, 128])
    tri_d = din("tri", [128, 2, 128])
    invf_d = din("invf", [32, 1])
    sel_d = din("sel", [128, 2, 128])
    shift_d = din("shiftm", [128, 128])
    cdft_d = din("cdft", [128, 2, 512], BF16)
    sdft_d = din("sdft", [SEQ, 2, SEQ], BF16)
    out_d = nc.dram_tensor("out", [nseq, SEQ, D], F32, kind="ExternalOutput").ap()
    zs_d = nc.dram_tensor("zs_scr", [SEQ, D], BF16).ap()
    hin_d = nc.dram_tensor("hin_scr", [16, 128, 1024], BF16).ap()
    ot_d = nc.dram_tensor("ot_scr", [128, 4 * SEQ], BF16).ap()

    S = Sched(nc)
    A = nc.alloc_sbuf_tensor

    hT = A("hT", [128, 8, SEQ], F32)
    TH = [[Tok() for _ in range(4)] for _ in range(8)]
    identf = A("identf", [128, 128], F32)
    identb = A("identb", [128, 128], BF16)
    onesb = A("onesb", [128, 128], BF16)
    epst = A("epst", [128, 1], F32)
    gains = A("gains_sb", [128, NGAIN * 8], F32)
    qkvg = A("qkvg_sb", [128, 2, 6], F32)
    convw = A("convw_sb", [128, 2, 12, 5], F32)
    convb = A("convb_sb", [128, 2, 12], F32)
    tri = A("tri_sb", [128, 2, 128], F32)
    invf = A("invf_sb", [128, 1], F32)
    memT = A("memT", [128, 8, NMEM], BF16)
    cos2 = A("cos2", [128, SEQ], BF16)
    sin2 = A("sin2", [128, SEQ], BF16)
    shiftm = A("shiftm_sb", [128, 128], F32)
    TC = Tok()
    TMEM = Tok()
    TROPE = Tok()
    remaining = nc.sbuf_bytes_remaining
    ARENA = (remaining - 256) // 32 * 32
    arena = A("arena", [128, ARENA], U8)
    ps = nc.alloc_psum_tensor("ps", [128, 4096], F32)
    PB = [Tok() for _ in range(8)]

    def bank(i):
        return ps[:, i * 512:(i + 1) * 512]

    def mm(out, lhsT, rhs, start, stop, r, w):
        S.op("pe", lambda e: e.matmul(out, lhsT, rhs, start=start, stop=stop), r=r, w=w)

    def tr(out, in_, idn, r, w):
        S.op("pe", lambda e: e.transpose(out, in_, idn), r=r, w=w)

    def act(out, in_, func, r, w, **kw):
        S.op("act", lambda e: e.activation(out=out, in_=in_, func=func, **kw), r=r, w=w)

    def dve(name, *args, r, w, eng="dve", nowaw=False, **kw):
        S.op(eng, lambda e: getattr(e, name)(*args, **kw), r=r, w=w, nowaw=nowaw)

    def tsl(tt):
        return slice(tt * 512, (tt + 1) * 512)

    def gcol(gi, c):
        return gains[:, gi * 8 + c:gi * 8 + c + 1]

    S.dma("sp", identf[:], ident_d, w=[TC])
    S.dma("sp", gains[:], gains_d, w=[TC])
    S.dma("sp", qkvg[:], qkvg_d, w=[TC])
    S.dma("sp", convw[:], convw_d, w=[TC])
    S.dma("sp", convb[:], convb_d, w=[TC])
    S.dma("sp", tri[:], tri_d, w=[TC])
    S.dma("sp", invf[64:96, :], invf_d, w=[TC])
    S.dma("sp", shiftm[:], shift_d, w=[TC])
    dve("tensor_copy", identb[:], identf[:], r=[TC], w=[TC])
    dve("memset", onesb[:], 1.0, r=[], w=[TC])
    dve("memset", epst[:], EPS, r=[], w=[TC])

    class Ring:
        def __init__(self, b, n, nbytes):
            self.slots = [b.alloc((nbytes,), U8) for _ in range(n)]
            self.toks = [Tok() for _ in range(n)]
            self.i = 0

        def next(self):
            k = self.i % len(self.slots)
            self.i += 1
            return self.slots[k], self.toks[k]

    rr = {"n": 6}

    def nbank(group):
        lo, n = group
        k = rr.get(group, 0)
        rr[group] = (k + 1) % n
        return lo + k

    def rmsnorm_fm(b, src, src_toks, nch, dim, gain_fn, out, out_toks, ntt=4, banks=(6, 2), scr_sq=None):
        sq = [b.alloc((nch, 512), BF16) for _ in range(2)]
        tsq = [Tok() for _ in range(2)]
        rs = [b.alloc((512,), F32) for _ in range(2)]
        trs = [Tok() for _ in range(2)]
        act(sq[0][:], src[:, :, tsl(0)], AF.Square, r=src_toks(0), w=[tsq[0]])
        for tt in range(ntt):
            i = tt % 2
            if tt + 1 < ntt:
                act(sq[1 - i][:], src[:, :, tsl(tt + 1)], AF.Square, r=src_toks(tt + 1), w=[tsq[1 - i]])
            bk = nbank(banks)
            for c in range(nch):
                mm(bank(bk), onesb[:], sq[i][:, c, :], c == 0, c == nch - 1, r=[tsq[i], TC], w=[PB[bk]])
            act(rs[i][:], bank(bk), AF.Ln, r=[PB[bk], TC], w=[trs[i]], scale=1.0 / dim, bias=epst[:])
            act(rs[i][:], rs[i][:], AF.Exp, r=[trs[i]], w=[trs[i]], scale=-0.5)
            for c in range(nch):
                dve("scalar_tensor_tensor", out[:, c, tsl(tt)], src[:, c, tsl(tt)], gain_fn(c), rs[i][:], ALU.mult, ALU.mult,
                    r=src_toks(tt) + [trs[i], TC], w=[out_toks[tt]], nowaw=True)

    def h_toks(tt):
        return [TH[c][tt] for c in range(8)]

    def proj_accum(b, wsrc, nkc, actT, act_toks, scale, ring, ybanks=(4, 2), act_view=None):
        for dcp in range(4):
            slot, tk = ring.next()
            wd = slot[:, 0:nkc * 256 * 2].bitcast(BF16).rearrange("p (j n) -> p j n", j=nkc)
            S.dma("pool", wd, wsrc[:, :, dcp * 256:(dcp + 1) * 256], w=[tk])
            for dd in range(2):
                dc = dcp * 2 + dd
                for tt in range(4):
                    by = nbank(ybanks)
                    for j in range(nkc):
                        av = act_view(j, tt) if act_view is not None else actT[:, j, tsl(tt)]
                        at = act_toks(j, tt)
                        mm(bank(by), wd[:, j, dd * 128:(dd + 1) * 128], av, j == 0, j == nkc - 1,
                           r=[tk] + (at if isinstance(at, list) else [at]), w=[PB[by]])
                    dve("scalar_tensor_tensor", hT[:, dc, tsl(tt)], bank(by), float(scale), hT[:, dc, tsl(tt)], ALU.mult, ALU.add,
                        r=[PB[by], TH[dc][tt]], w=[TH[dc][tt]])

    def load_x(s):
        S.barrier()
        b = Bump(arena, 0, ARENA)
        xt = [b.alloc((D,), F32) for _ in range(2)]
        txt = [Tok() for _ in range(2)]
        for t in range(16):
            i = t % 2
            S.dma("sp", xt[i][:], x_d[s, t * 128:(t + 1) * 128, :], w=[txt[i]])
            for cg in range(2):
                bk = nbank((0, 4))
                for ci in range(4):
                    c = cg * 4 + ci
                    tr(bank(bk)[:, ci * 128:(ci + 1) * 128], xt[i][:, c * 128:(c + 1) * 128], identf[:], r=[txt[i], TC], w=[PB[bk]])
                dst = hT[:, cg * 4:(cg + 1) * 4, t * 128:(t + 1) * 128]
                src = bank(bk).rearrange("p (c n) -> p c n", c=4)
                toks = [TH[c][t // 4] for c in range(cg * 4, cg * 4 + 4)]
                if cg == 0:
                    act(dst, src, AF.Copy, r=[PB[bk]], w=toks)
                else:
                    dve("tensor_copy", dst, src, r=[PB[bk]], w=toks)

    def prep_mem(s):
        S.barrier()
        b = Bump(arena, 0, ARENA)
        for t in range(2):
            mt = b.alloc((D,), F32)
            m2 = b.alloc((D,), F32)
            junk = b.alloc((D,), F32)
            ss = b.alloc((1,), F32)
            tk = Tok()
            S.dma("sp", mt[:], mem_d[s, t * 128:(t + 1) * 128, :], w=[tk])
            act(junk[:], mt[:], AF.Square, r=[tk], w=[tk], accum_out=ss[:])
            act(ss[:], ss[:], AF.Sqrt, r=[tk, TC], w=[tk], scale=1.0 / D, bias=epst[:])
            dve("reciprocal", ss[:], ss[:], r=[tk], w=[tk])
            act(m2[:], mt[:], AF.Copy, r=[tk], w=[tk], scale=ss[:])
            for cg in range(2):
                bk = nbank((0, 4))
                for ci in range(4):
                    c = cg * 4 + ci
                    tr(bank(bk)[:, ci * 128:(ci + 1) * 128], m2[:, c * 128:(c + 1) * 128], identf[:], r=[tk, TC], w=[PB[bk]])
                for ci in range(4):
                    c = cg * 4 + ci
                    act(memT[:, c, t * 128:(t + 1) * 128], bank(bk)[:, ci * 128:(ci + 1) * 128], AF.Copy, r=[PB[bk], TC], w=[TMEM],
                        scale=gcol(G_MEM, c))

    def rope_tables(s):
        S.barrier()
        b = Bump(arena, 0, ARENA)
        pi_ = b.alloc((SEQ,), I32)
        pf = b.alloc((SEQ,), F32)
        tk = Tok()
        S.dma("sp", pi_[64:96, :], pos_d[s:s + 1, :].to_broadcast([32, SEQ]), w=[tk])
        dve("tensor_copy", pf[64:96, :], pi_[64:96, :], r=[tk], w=[tk])
        ang = b.alloc((SEQ,), F32)
        t_ = b.alloc((SEQ,), F32)
        ki = b.alloc((SEQ,), I32)
        kf = b.alloc((SEQ,), F32)
        r_ = b.alloc((SEQ,), F32)
        m_ = b.alloc((SEQ,), F32)
        P32 = slice(64, 96)
        dve("tensor_scalar", ang[P32, :], pf[P32, :], invf[P32, 0:1], 0.0, ALU.mult, ALU.add, r=[tk, TC], w=[tk])
        for dst, off in ((sin2, 0.0), (cos2, math.pi / 2)):
            dve("tensor_scalar", t_[P32, :], ang[P32, :], off, 1.0 / (2 * math.pi), ALU.add, ALU.mult, r=[tk], w=[tk])
            dve("tensor_copy", ki[P32, :], t_[P32, :], r=[tk], w=[tk])
            dve("tensor_copy", kf[P32, :], ki[P32, :], r=[tk], w=[tk])
            dve("tensor_scalar", r_[P32, :], ang[P32, :], off, None, ALU.add, r=[tk], w=[tk])
            dve("scalar_tensor_tensor", r_[P32, :], kf[P32, :], -2 * math.pi, r_[P32, :], ALU.mult, ALU.add, r=[tk], w=[tk])
            dve("tensor_scalar", m_[P32, :], r_[P32, :], math.pi, None, ALU.is_gt, r=[tk], w=[tk])
            dve("scalar_tensor_tensor", r_[P32, :], m_[P32, :], -2 * math.pi, r_[P32, :], ALU.mult, ALU.add, r=[tk], w=[tk])
            dve("tensor_scalar", m_[P32, :], r_[P32, :], -math.pi, None, ALU.is_lt, r=[tk], w=[tk])
            dve("scalar_tensor_tensor", r_[P32, :], m_[P32, :], 2 * math.pi, r_[P32, :], ALU.mult, ALU.add, r=[tk], w=[tk])
            act(dst[P32, :], r_[P32, :], AF.Sin, r=[tk], w=[TROPE])

    def ffn(gu, dn, gi):
        S.barrier()
        b = Bump(arena, 0, ARENA)
        xn = b.alloc((8, SEQ), BF16)
        txn = [Tok() for _ in range(4)]
        rmsnorm_fm(b, hT, h_toks, 8, D, lambda c: gcol(gi, c), xn, txn)
        actb = b.alloc((11, SEQ), BF16)
        ta = [[Tok() for _ in range(4)] for _ in range(11)]
        sg = [b.alloc((512,), F32) for _ in range(2)]
        tsg = [Tok() for _ in range(2)]
        ring = Ring(b, 4, 6144)
        gsrc = gu.rearrange("(kc p) (two n) -> p two kc n", p=128, two=2)
        dsrc = dn.rearrange("(j p) n -> p j n", p=128)
        k = 0
        for half in range(2):
            for jj in range(11):
                j = half * 11 + jj
                slot, tk = ring.next()
                wv = slot[:, 0:4096].bitcast(BF16).rearrange("p (two kc n) -> p two kc n", two=2, kc=8)
                S.dma("pool", wv, gsrc[:, :, :, j * 128:(j + 1) * 128], w=[tk])
                for tt in range(4):
                    bg = nbank((0, 2))
                    bu = nbank((2, 2))
                    for kc in range(8):
                        mm(bank(bg), wv[:, 0, kc, :], xn[:, kc, tsl(tt)], kc == 0, kc == 7, r=[tk, txn[tt]], w=[PB[bg]])
                    for kc in range(8):
                        mm(bank(bu), wv[:, 1, kc, :], xn[:, kc, tsl(tt)], kc == 0, kc == 7, r=[tk, txn[tt]], w=[PB[bu]])
                    i = k % 2
                    k += 1
                    act(sg[i][:], bank(bg), AF.Silu, r=[PB[bg]], w=[tsg[i]])
                    dve("tensor_tensor", actb[:, jj, tsl(tt)], bank(bu), sg[i][:], ALU.mult, r=[PB[bu], tsg[i]], w=[ta[jj][tt]])
            proj_accum(b, dsrc[:, half * 11:(half + 1) * 11, :], 11, actb, lambda j, tt: ta[j][tt], 0.5, ring)

    def xattn(layer):
        S.barrier()
        b = Bump(arena, 0, ARENA)
        xn = b.alloc((8, SEQ), BF16)
        txn = [Tok() for _ in range(4)]
        rmsnorm_fm(b, hT, h_toks, 8, D, lambda c: gcol(G_XA + layer, c), xn, txn)
        QT = b.alloc((8, SEQ), BF16)
        tq = [[Tok() for _ in range(4)] for _ in range(8)]
        OT = xn
        tot = [[Tok() for _ in range(4)] for _ in range(8)]
        KT = b.alloc((8, NMEM), BF16)
        tkt = Tok()
        Vt = b.alloc((2, D), BF16)
        tv = Tok()
        PT = [b.alloc((512,), BF16) for _ in range(4)]
        tpt = [Tok() for _ in range(4)]
        rden = [b.alloc((512,), F32) for _ in range(2)]
        trd = [Tok() for _ in range(2)]
        ring = Ring(b, 3, 8192)
        wkv = dr["xa_wkv"][layer].rearrange("(kc p) n -> p kc n", p=128)
        wq = dr["xa_wq"][layer].rearrange("(kc p) n -> p kc n", p=128)
        wo = dr["xa_wo"][layer].rearrange("(kc p) n -> p kc n", p=128)
        for ocp in range(4):
            slot, tk = ring.next()
            wv = slot[:, 0:4096].bitcast(BF16).rearrange("p (kc n) -> p kc n", kc=8)
            S.dma("pool", wv, wkv[:, :, ocp * 256:(ocp + 1) * 256], w=[tk])
            for oo in range(2):
                oc = ocp * 2 + oo
                bk = nbank((6, 2))
                for kc in range(8):
                    mm(bank(bk)[:, 0:NMEM], wv[:, kc, oo * 128:(oo + 1) * 128], memT[:, kc, :], kc == 0, kc == 7, r=[tk, TMEM], w=[PB[bk]])
                act(KT[:, oc, :], bank(bk)[:, 0:NMEM], AF.Copy, r=[PB[bk]], w=[tkt])
        for ct in range(2):
            slot, tk = ring.next()
            wv = slot[:, 0:8192].bitcast(BF16).rearrange("p (kc n) -> p kc n", kc=8)
            S.dma("pool", wv, wkv[:, :, D + ct * 512:D + (ct + 1) * 512], w=[tk])
            for kt in range(2):
                bk = nbank((6, 2))
                for kc in range(8):
                    mm(bank(bk), memT[:, kc, kt * 128:(kt + 1) * 128], wv[:, kc, :], kc == 0, kc == 7, r=[tk, TMEM], w=[PB[bk]])
                dve("tensor_copy", Vt[:, kt, ct * 512:(ct + 1) * 512], bank(bk), r=[PB[bk]], w=[tv])
        for ocp in range(4):
            slot, tk = ring.next()
            wv = slot[:, 0:4096].bitcast(BF16).rearrange("p (kc n) -> p kc n", kc=8)
            S.dma("pool", wv, wq[:, :, ocp * 256:(ocp + 1) * 256], w=[tk])
            for oo in range(2):
                oc = ocp * 2 + oo
                for tt in range(4):
                    bk = nbank((6, 2))
                    for kc in range(8):
                        mm(bank(bk), wv[:, kc, oo * 128:(oo + 1) * 128], xn[:, kc, tsl(tt)], kc == 0, kc == 7, r=[tk, txn[tt]], w=[PB[bk]])
                    act(QT[:, oc, tsl(tt)], bank(bk), AF.Copy, r=[PB[bk]], w=[tq[oc][tt]])
        S.barrier()
        scale = 256.0 ** -0.5
        ip = 0
        ir = 0
        units = [(h, tt) for h in range(4) for tt in range(4)]

        def s_issue(h, tt):
            bks = []
            for kc in range(2):
                bk = nbank((0, 4))
                for dc in range(2):
                    mm(bank(bk), KT[:, h * 2 + dc, kc * 128:(kc + 1) * 128], QT[:, h * 2 + dc, tsl(tt)], dc == 0, dc == 1,
                       r=[tkt, tq[h * 2 + dc][tt]], w=[PB[bk]])
                bks.append(bk)
            return bks

        pend = s_issue(*units[0])
        for n_, (h, tt) in enumerate(units):
            bks = pend
            pts = []
            for kc in range(2):
                i = ip % 4
                ip += 1
                act(PT[i][:], bank(bks[kc]), AF.Exp, r=[PB[bks[kc]]], w=[tpt[i]], scale=scale)
                pts.append(i)
            if n_ + 1 < len(units):
                pend = s_issue(*units[n_ + 1])
            bd = nbank((6, 2))
            for kc in range(2):
                mm(bank(bd), onesb[:], PT[pts[kc]][:], kc == 0, kc == 1, r=[tpt[pts[kc]], TC], w=[PB[bd]])
            j = ir % 2
            ir += 1
            act(rden[j][:], bank(bd), AF.Ln, r=[PB[bd]], w=[trd[j]])
            act(rden[j][:], rden[j][:], AF.Exp, r=[trd[j]], w=[trd[j]], scale=-1.0)
            for dvc in range(2):
                bo = nbank((4, 2))
                for kc in range(2):
                    mm(bank(bo), Vt[:, kc, h * 256 + dvc * 128:h * 256 + (dvc + 1) * 128], PT[pts[kc]][:], kc == 0, kc == 1,
                       r=[tv, tpt[pts[kc]]], w=[PB[bo]])
                dve("tensor_tensor", OT[:, h * 2 + dvc, tsl(tt)], bank(bo), rden[j][:], ALU.mult, r=[PB[bo], trd[j]], w=[tot[h * 2 + dvc][tt]])
        proj_accum(b, wo, 8, OT, lambda j, tt: tot[j][tt], 1.0, ring, ybanks=(6, 2))

    def fnet(layer):
        S.barrier()
        b = Bump(arena, 0, ARENA)
        xn = b.alloc((8, SEQ), BF16)
        txn = [Tok() for _ in range(4)]
        rmsnorm_fm(Bump(arena, b.off, ARENA), hT, h_toks, 8, D, lambda c: gcol(G_MIX + layer, c), xn, txn)
        S.barrier()
        Y = b.alloc((16, 4, 512), BF16)
        ty = [Tok() for _ in range(16)]
        cd = b.alloc((2, 512), BF16)
        tcd = Tok()
        ring = Ring(b, 6, 2048)
        ring2 = Ring(b, 2, 4096)
        S.dma("sp", cd[:], cdft_d, w=[tcd])
        k = 0
        for sc in range(16):
            for g in range(4):
                bk = nbank((0, 4))
                for cc in range(2):
                    mm(bank(bk), xn[:, g * 2 + cc, sc * 128:(sc + 1) * 128], cd[:, cc, :], cc == 0, cc == 1, r=[txn[sc // 4], tcd], w=[PB[bk]])
                if k % 2 == 0:
                    act(Y[:, sc, g, :], bank(bk), AF.Copy, r=[PB[bk]], w=[ty[sc]])
                else:
                    dve("tensor_copy", Y[:, sc, g, :], bank(bk), r=[PB[bk]], w=[ty[sc]])
                k += 1
        nrm = 1.0 / math.sqrt(SEQ * 256.0)
        for kt in range(4):
            for sc in range(16):
                slot, tk = ring.next()
                dv = slot[:, 0:2048].bitcast(BF16).rearrange("p (a n) -> p a n", a=2)
                S.dma("sp", dv, sdft_d[sc * 128:(sc + 1) * 128, :, kt * 512:(kt + 1) * 512], w=[tk])
                for o in range(8):
                    g, jc = o // 2, o % 2
                    mm(bank(o), Y[:, sc, g, jc * 128:(jc + 1) * 128], dv[:, 0, :], sc == 0, False, r=[ty[sc], tk], w=[PB[o]])
                    mm(bank(o), Y[:, sc, g, 256 + jc * 128:256 + (jc + 1) * 128], dv[:, 1, :], False, sc == 15, r=[ty[sc], tk], w=[PB[o]])
            for o in range(8):
                if o % 2 == 0:
                    act(xn[:, o, tsl(kt)], bank(o), AF.Copy, r=[PB[o]], w=[txn[kt]], scale=nrm)
                else:
                    dve("tensor_scalar", xn[:, o, tsl(kt)], bank(o), nrm, None, ALU.mult, r=[PB[o]], w=[txn[kt]])
        wsrc = dr["fnet_w_out"][layer // 2].rearrange("(kc p) n -> p kc n", p=128)
        proj_accum(b, wsrc, 8, xn, lambda j, tt: txn[tt], 1.0, ring2, ybanks=(0, 4))

    def final(s):
        S.barrier()
        b = Bump(arena, 0, ARENA)
        sq = [b.alloc((8, 512), BF16) for _ in range(2)]
        tsq = [Tok() for _ in range(2)]
        rs = [b.alloc((512,), F32) for _ in range(2)]
        trs = [Tok() for _ in range(2)]
        xf = [b.alloc((8, 512), F32) for _ in range(2)]
        txf = [Tok() for _ in range(2)]
        yt = [b.alloc((D,), F32) for _ in range(2)]
        tyt = [Tok() for _ in range(2)]
        k = 0
        for tt in range(4):
            i = tt % 2
            act(sq[i][:], hT[:, :, tsl(tt)], AF.Square, r=h_toks(tt), w=[tsq[i]])
            bk = nbank((6, 2))
            for c in range(8):
                mm(bank(bk), onesb[:], sq[i][:, c, :], c == 0, c == 7, r=[tsq[i], TC], w=[PB[bk]])
            act(rs[i][:], bank(bk), AF.Sqrt, r=[PB[bk], TC], w=[trs[i]], scale=1.0 / D, bias=epst[:])
            dve("reciprocal", rs[i][:], rs[i][:], r=[trs[i]], w=[trs[i]])
            for c in range(8):
                dve("scalar_tensor_tensor", xf[i][:, c, :], hT[:, c, tsl(tt)], gcol(G_FINAL, c), rs[i][:], ALU.mult, ALU.mult,
                    r=h_toks(tt) + [trs[i], TC], w=[txf[i]])
            for t4 in range(4):
                t = tt * 4 + t4
                j = k % 2
                k += 1
                for cg in range(2):
                    bk2 = nbank((0, 4))
                    for ci in range(4):
                        c = cg * 4 + ci
                        tr(bank(bk2)[:, ci * 128:(ci + 1) * 128], xf[i][:, c, t4 * 128:(t4 + 1) * 128], identf[:], r=[txf[i], TC], w=[PB[bk2]])
                    if cg == 0:
                        act(yt[j][:, 0:512], bank(bk2), AF.Copy, r=[PB[bk2]], w=[tyt[j]])
                    else:
                        dve("tensor_copy", yt[j][:, 512:1024], bank(bk2), r=[PB[bk2]], w=[tyt[j]])
                S.dma("sp", out_d[s, t * 128:(t + 1) * 128, :], yt[j][:], r=[tyt[j]])

    def norm_tile(nt, src_tile, src_toks, nch, dim, gain_fn, out_view, out_toks, banks=(6, 2)):
        sq, tsq, rs, trs = nt
        i = rr.get("nt", 0)
        rr["nt"] = (i + 1) % 2
        act(sq[i][:, 0:nch, :], src_tile, AF.Square, r=src_toks, w=[tsq[i]])
        bk = nbank(banks)
        for c in range(nch):
            mm(bank(bk), onesb[:], sq[i][:, c, :], c == 0, c == nch - 1, r=[tsq[i], TC], w=[PB[bk]])
        act(rs[i][:], bank(bk), AF.Ln, r=[PB[bk], TC], w=[trs[i]], scale=1.0 / dim, bias=epst[:])
        act(rs[i][:], rs[i][:], AF.Exp, r=[trs[i]], w=[trs[i]], scale=-0.5)
        for c in range(nch):
            dve("scalar_tensor_tensor", out_view[:, c, :], src_tile[:, c, :], gain_fn(c), rs[i][:], ALU.mult, ALU.mult,
                r=src_toks + [trs[i], TC], w=out_toks, nowaw=True)

    def even_mixer(e, layer):
        K16, K48, K76, K96, K104 = 16384, 49152, 77824, 98304, 98304 + 16384
        win = dr["w_in"][e].rearrange("(kc p) n -> p kc n", p=128)
        wuq_d = dr["w_uq"][e].rearrange("(kc p) n -> p kc n", p=128)
        wukv_d = dr["w_ukv"][e].rearrange("(kc p) n -> p kc n", p=128)
        wout_d = dr["w_out"][e].rearrange("(kc p) n -> p kc n", p=128)
        S.barrier()
        b0 = Bump(arena, 0, K16)
        OT = b0.alloc((4, SEQ), BF16)
        tot = [[Tok() for _ in range(4)] for _ in range(4)]
        bx = Bump(arena, K16, K48)
        xn = bx.alloc((8, SEQ), BF16)
        txn = [Tok() for _ in range(4)]
        rmsnorm_fm(Bump(arena, K48, ARENA), hT, h_toks, 8, D, lambda c: gcol(G_MIX + layer, c), xn, txn)
        S.barrier()
        ba = Bump(arena, K48, K76)
        cqn = ba.alloc((4, SEQ), BF16)
        tcq = [Tok() for _ in range(4)]
        ckvn = ba.alloc((2, SEQ), BF16)
        tckv = [Tok() for _ in range(4)]
        kpe = ba.alloc((SEQ,), BF16)
        tkpe = [Tok() for _ in range(4)]
        bt = Bump(arena, K76, ARENA)
        wcq = bt.alloc((8, 512), BF16)
        wckv = bt.alloc((8, 256), BF16)
        wkr = bt.alloc((8, 32), BF16)
        wkrr = bt.alloc((8, 32), BF16)
        tw = Tok()
        S.dma("pool", wcq, win[:, :, 2592:3104], w=[tw])
        S.dma("pool", wckv, win[:, :, 3104:3360], w=[tw])
        S.dma("pool", wkr, win[:, :, 3360:3392], w=[tw])
        dve("tensor_scalar", wkrr[:, :, 0:16], wkr[:, :, 16:32], -1.0, None, ALU.mult, r=[tw], w=[tw])
        dve("tensor_copy", wkrr[:, :, 16:32], wkr[:, :, 0:16], r=[tw], w=[tw])
        nt = ([bt.alloc((4, 512), BF16) for _ in range(2)], [Tok() for _ in range(2)],
              [bt.alloc((512,), F32) for _ in range(2)], [Tok() for _ in range(2)])
        cqr = [bt.alloc((4, 512), BF16) for _ in range(2)]
        tcqr = [Tok() for _ in range(2)]
        ckr = [bt.alloc((2, 512), BF16) for _ in range(2)]
        tckr = [Tok() for _ in range(2)]
        t1 = [bt.alloc((512,), F32) for _ in range(2)]
        t2 = [bt.alloc((512,), F32) for _ in range(2)]
        tt12 = [Tok() for _ in range(2)]
        for tt in range(4):
            i = tt % 2
            for oc in range(4):
                bk = nbank((0, 4))
                for kc in range(8):
                    mm(bank(bk), wcq[:, kc, oc * 128:(oc + 1) * 128], xn[:, kc, tsl(tt)], kc == 0, kc == 7, r=[tw, txn[tt]], w=[PB[bk]])
                act(cqr[i][:, oc, :], bank(bk), AF.Copy, r=[PB[bk]], w=[tcqr[i]])
            norm_tile(nt, cqr[i][:], [tcqr[i]], 4, 512, lambda c: qkvg[:, e, c:c + 1], cqn[:, :, tsl(tt)], [tcq[tt]])
            for oc in range(2):
                bk = nbank((0, 4))
                for kc in range(8):
                    mm(bank(bk), wckv[:, kc, oc * 128:(oc + 1) * 128], xn[:, kc, tsl(tt)], kc == 0, kc == 7, r=[tw, txn[tt]], w=[PB[bk]])
                act(ckr[i][:, oc, :], bank(bk), AF.Copy, r=[PB[bk]], w=[tckr[i]])
            norm_tile(nt, ckr[i][:], [tckr[i]], 2, 256, lambda c: qkvg[:, e, 4 + c:5 + c], ckvn[:, :, tsl(tt)], [tckv[tt]])
            ba_, bb_ = nbank((0, 4)), nbank((0, 4))
            for kc in range(8):
                mm(bank(ba_)[64:96, :], wkr[:, kc, :], xn[:, kc, tsl(tt)], kc == 0, kc == 7, r=[tw, txn[tt]], w=[PB[ba_]])
            for kc in range(8):
                mm(bank(bb_)[64:96, :], wkrr[:, kc, :], xn[:, kc, tsl(tt)], kc == 0, kc == 7, r=[tw, txn[tt]], w=[PB[bb_]])
            dve("tensor_tensor", t1[i][64:96, :], bank(ba_)[64:96, :], cos2[64:96, tsl(tt)], ALU.mult, r=[PB[ba_], TROPE], w=[tt12[i]])
            dve("tensor_tensor", t2[i][64:96, :], bank(bb_)[64:96, :], sin2[64:96, tsl(tt)], ALU.mult, r=[PB[bb_], TROPE], w=[tt12[i]])
            dve("tensor_tensor", kpe[64:96, tsl(tt)], t1[i][64:96, :], t2[i][64:96, :], ALU.add, r=[tt12[i]], w=[tkpe[tt]])
        S.barrier()
        if EVEN_STOP < 1:
            return
        bb = Bump(arena, K16, K48)
        V = bb.alloc((16, 8, 64), BF16)
        tv = Tok()
        wukv = bb.alloc((2, 1024), BF16)
        wuq = bb.alloc((4, 768), BF16)
        wuqr = bb.alloc((4, 8, 32), BF16)
        twb = Tok()
        S.dma("pool", wukv, wukv_d, w=[twb])
        S.dma("pool", wuq, wuq_d, w=[twb])
        for h in range(8):
            dve("tensor_scalar", wuqr[:, :, h, 0:16], wuq[:, :, h * 96 + 80:h * 96 + 96], -1.0, None, ALU.mult, r=[twb], w=[twb])
            dve("tensor_copy", wuqr[:, :, h, 16:32], wuq[:, :, h * 96 + 64:h * 96 + 80], r=[twb], w=[twb])
        bh = Bump(arena, K76, ARENA)
        Qh = [bh.alloc((SEQ,), BF16) for _ in range(2)]
        Kh = [bh.alloc((SEQ,), BF16) for _ in range(2)]
        tqk = [[Tok() for _ in range(4)] for _ in range(2)]
        Va = [bh.alloc((16, 128), BF16) for _ in range(2)]
        tva = [Tok() for _ in range(2)]
        PT = [bh.alloc((512,), BF16) for _ in range(4)]
        tpt = [Tok() for _ in range(4)]
        rden = [bh.alloc((512,), F32) for _ in range(2)]
        trd = [Tok() for _ in range(2)]
        rsh = [bh.alloc((512,), F32) for _ in range(2)]
        trs_ = [Tok() for _ in range(2)]
        u1 = [bh.alloc((512,), F32) for _ in range(2)]
        u2 = [bh.alloc((512,), F32) for _ in range(2)]
        tu = [Tok() for _ in range(2)]
        for sc in range(16):
            for ct in range(2):
                bk = nbank((6, 2))
                for kc in range(2):
                    mm(bank(bk), ckvn[:, kc, sc * 128:(sc + 1) * 128], wukv[:, kc, ct * 512:(ct + 1) * 512], kc == 0, kc == 1,
                       r=[twb, tckv[sc // 4]], w=[PB[bk]])
                src = bank(bk).rearrange("p (h c) -> p h c", h=4)[:, :, 64:128]
                if ct == 0:
                    act(V[:, sc, 0:4, :], src, AF.Copy, r=[PB[bk]], w=[tv])
                else:
                    dve("tensor_copy", V[:, sc, 4:8, :], src, r=[PB[bk]], w=[tv])
        dve("memset", Va[0][:, :, 64:128], 1.0, r=[], w=[tva[0]])
        dve("memset", Va[1][:, :, 0:64], 1.0, r=[], w=[tva[1]])
        for hb in range(2):
            dve("tensor_copy", Kh[hb][64:96, :], kpe[64:96, :], r=tkpe, w=tqk[hb])
        scale = 96.0 ** -0.5
        cnt = {"ip": 0, "ir": 0, "iu": 0}

        def head_proj(h):
            hb = h % 2
            po = hb * 64
            dve("tensor_copy", Va[hb][:, :, po:po + 64], V[:, :, h, :], r=[tv], w=[tva[hb]])
            for tt in range(4):
                bk = nbank((6, 2))
                for kc in range(4):
                    mm(bank(bk)[0:64, :], wuq[:, kc, h * 96:h * 96 + 64], cqn[:, kc, tsl(tt)], kc == 0, kc == 3, r=[twb, tcq[tt]], w=[PB[bk]])
                dve("tensor_copy", Qh[hb][0:64, tsl(tt)], bank(bk)[0:64, :], r=[PB[bk]], w=[tqk[hb][tt]])
                bk = nbank((6, 2))
                for kc in range(2):
                    mm(bank(bk)[0:64, :], wukv[:, kc, h * 128:h * 128 + 64], ckvn[:, kc, tsl(tt)], kc == 0, kc == 1, r=[twb, tckv[tt]], w=[PB[bk]])
                dve("tensor_copy", Kh[hb][0:64, tsl(tt)], bank(bk)[0:64, :], r=[PB[bk]], w=[tqk[hb][tt]])
                ba_, bb_ = nbank((6, 2)), nbank((6, 2))
                for kc in range(4):
                    mm(bank(ba_)[64:96, :], wuq[:, kc, h * 96 + 64:h * 96 + 96], cqn[:, kc, tsl(tt)], kc == 0, kc == 3, r=[twb, tcq[tt]], w=[PB[ba_]])
                for kc in range(4):
                    mm(bank(bb_)[64:96, :], wuqr[:, kc, h, :], cqn[:, kc, tsl(tt)], kc == 0, kc == 3, r=[twb, tcq[tt]], w=[PB[bb_]])
                i = cnt["iu"] % 2
                cnt["iu"] += 1
                dve("tensor_tensor", u1[i][64:96, :], bank(ba_)[64:96, :], cos2[64:96, tsl(tt)], ALU.mult, r=[PB[ba_], TROPE], w=[tu[i]])
                dve("tensor_tensor", u2[i][64:96, :], bank(bb_)[64:96, :], sin2[64:96, tsl(tt)], ALU.mult, r=[PB[bb_], TROPE], w=[tu[i]])
                dve("tensor_tensor", Qh[hb][64:96, tsl(tt)], u1[i][64:96, :], u2[i][64:96, :], ALU.add, r=[tu[i]], w=[tqk[hb][tt]])
                yield tt

        def s_issue(h, qt, kc):
            hb = h % 2
            bs = nbank((0, 4))
            mm(bank(bs), Kh[hb][0:96, kc * 128:(kc + 1) * 128], Qh[hb][0:96, tsl(qt)], True, True,
               r=[tqk[hb][kc // 4], tqk[hb][qt]], w=[PB[bs]])
            return bs

        for _ in head_proj(0):
            pass
        pending = []
        for h in range(8):
            hb = h % 2
            po = hb * 64
            dpo = 64 - po
            gen = head_proj(h + 1) if h < 7 else iter(())
            units = [(qt, kc) for qt in range(4) for kc in range(16)]
            sb = {}
            LA = 3
            for n_ in range(LA):
                sb[n_] = s_issue(h, *units[n_])
            bo = None
            for n_, (qt, kc) in enumerate(units):
                if kc == 0:
                    bo = nbank((4, 2))
                bs = sb.pop(n_)
                i = cnt["ip"] % 4
                cnt["ip"] += 1
                act(PT[i][:], bank(bs), AF.Exp, r=[PB[bs]], w=[tpt[i]], scale=scale)
                mm(bank(bo), Va[hb][:, kc, :], PT[i][:], kc == 0, kc == 15, r=[tva[hb], tpt[i]], w=[PB[bo]])
                if n_ + LA < len(units):
                    sb[n_ + LA] = s_issue(h, *units[n_ + LA])
                if kc == 15:
                    j = cnt["ir"] % 2
                    cnt["ir"] += 1
                    dve("reciprocal", rden[j][dpo:dpo + 64, :], bank(bo)[dpo:dpo + 64, :], r=[PB[bo]], w=[trd[j]])

                    def epi(bo=bo, j=j, po=po, dpo=dpo, h=h, qt=qt):
                        br = nbank((6, 2))
                        mm(bank(br), shiftm[dpo:dpo + 64, :], rden[j][dpo:dpo + 64, :], True, True, r=[TC, trd[j]], w=[PB[br]])
                        dve("tensor_copy", rsh[j][po:po + 64, :], bank(br)[po:po + 64, :], r=[PB[br]], w=[trs_[j]])
                        dve("tensor_tensor", OT[po:po + 64, h // 2, tsl(qt)], bank(bo)[po:po + 64, :], rsh[j][po:po + 64, :], ALU.mult,
                            r=[PB[bo], trs_[j]], w=[tot[h // 2][qt]])
                    pending.append(epi)
                if kc == 8 and pending:
                    pending.pop(0)()
                if kc == 3:
                    next(gen, None)
        while pending:
            pending.pop(0)()
        tots = Tok()
        S.dma("sp", ot_d, OT[:].rearrange("p a b -> p (a b)"), r=[t for row in tot for t in row], w=[tots])
        S.barrier()
        if EVEN_STOP < 2:
            return
        rmsnorm_fm(Bump(arena, K48, ARENA), hT, h_toks, 8, D, lambda c: gcol(G_MIX + layer, c), xn, txn)
        S.barrier()
        bc = Bump(arena, K48, K96)
        xbc = bc.alloc((12, SEQ), BF16)
        txb = [Tok() for _ in range(16)]
        bt = Bump(arena, K96, ARENA)
        raw = [bt.alloc((SEQ + 4,), BF16) for _ in range(2)]
        traw = [Tok() for _ in range(2)]
        dg = bt.alloc((12, 5, 128), BF16)
        tdg = Tok()
        ring = Ring(bt, 3, 2048)
        for oc in range(12):
            dve("tensor_tensor", dg[:, oc], identb[:].unsqueeze(1).to_broadcast([128, 5, 128]),
                convw[:, e, oc, :].unsqueeze(2).to_broadcast([128, 5, 128]), ALU.mult, r=[TC], w=[tdg], nowaw=True)
        for i in range(2):
            dve("memset", raw[i][:, 0:2], 0.0, r=[], w=[traw[i]])
            dve("memset", raw[i][:, SEQ + 2:SEQ + 4], 0.0, r=[], w=[traw[i]])
        for oc in range(12):
            i = oc % 2
            slot, tk = ring.next()
            wv = slot[:, 0:2048].bitcast(BF16).rearrange("p (kc n) -> p kc n", kc=8)
            S.dma("pool", wv, win[:, :, 1024 + oc * 128:1024 + (oc + 1) * 128], w=[tk])
            for tt in range(4):
                bk = nbank((0, 4))
                for kc in range(8):
                    mm(bank(bk), wv[:, kc, :], xn[:, kc, tsl(tt)], kc == 0, kc == 7, r=[tk, txn[tt]], w=[PB[bk]])
                dve("tensor_copy", raw[i][:, 2 + tt * 512:2 + (tt + 1) * 512], bank(bk), r=[PB[bk]], w=[traw[i]], nowaw=True)
            for tt in range(4):
                bk = nbank((4, 4))
                for t in range(5):
                    mm(bank(bk), dg[:, oc, t, :], raw[i][:, t + tt * 512:t + (tt + 1) * 512], t == 0, t == 4, r=[tdg, traw[i]], w=[PB[bk]])
                act(xbc[:, oc, tsl(tt)], bank(bk), AF.Silu, r=[PB[bk], TC], w=txb[tt * 4:tt * 4 + 4], bias=convb[:, e, oc:oc + 1])
        S.barrier()
        if EVEN_STOP < 3:
            return
        bs_ = Bump(arena, K96, K104)
        dt = bs_.alloc((16, 32), F32)
        la = bs_.alloc((16, 32), F32)
        cum = bs_.alloc((16, 32), F32)
        ecum = bs_.alloc((16, 32), F32)
        dtb = bs_.alloc((32,), F32)
        eal = bs_.alloc((32,), F32)
        dsk = bs_.alloc((16,), F32)
        tsm = Tok()
        bt = Bump(arena, K104, ARENA)
        wz = Bump(arena, 0, K16).alloc((8, 1024), BF16)
        wdt = bt.alloc((8, 32), BF16)
        twz = Tok()
        zt = [bt.alloc((1024,), BF16) for _ in range(2)]
        tzt = [Tok() for _ in range(2)]
        S.dma("pool", wz, win[:, :, 0:1024], w=[twz])
        S.dma("pool", wdt, win[:, :, 2560:2592], w=[twz])
        S.dma("sp", dtb[:], dtb_d[e:e + 1, :].to_broadcast([128, 32]), w=[tsm])
        S.dma("sp", eal[:], alog_d[e:e + 1, :].to_broadcast([128, 32]), w=[tsm])
        S.dma("sp", dsk[:], ssdd_d[e:e + 1, :].to_broadcast([128, 16]), w=[tsm])
        tzs = [Tok() for _ in range(16)]
        for sc in range(16):
            i = sc % 2
            for ct in range(2):
                bk = nbank((0, 4))
                for kc in range(8):
                    mm(bank(bk), xn[:, kc, sc * 128:(sc + 1) * 128], wz[:, kc, ct * 512:(ct + 1) * 512], kc == 0, kc == 7,
                       r=[twz, txn[sc // 4]], w=[PB[bk]])
                act(zt[i][:, ct * 512:(ct + 1) * 512], bank(bk), AF.Silu, r=[PB[bk]], w=[tzt[i]])
            S.dma("sp", zs_d[sc * 128:(sc + 1) * 128, :], zt[i][:], r=[tzt[i]], w=[tzs[sc]])
            bk = nbank((4, 2))
            for kc in range(8):
                mm(bank(bk)[:, 0:32], xn[:, kc, sc * 128:(sc + 1) * 128], wdt[:, kc, :], kc == 0, kc == 7, r=[twz, txn[sc // 4]], w=[PB[bk]])
            dve("tensor_tensor", dt[:, sc, :], bank(bk)[:, 0:32], dtb[:], ALU.add, r=[PB[bk], tsm], w=[tsm])
        act(dt[:], dt[:], AF.Exp, r=[tsm], w=[tsm])
        act(dt[:], dt[:], AF.Ln, r=[tsm], w=[tsm], bias=1.0)
        act(eal[:], eal[:], AF.Exp, r=[tsm], w=[tsm])
        dve("scalar_tensor_tensor", la[:], dt[:], -1.0, eal[:].unsqueeze(1).to_broadcast([128, 16, 32]), ALU.mult, ALU.mult, r=[tsm], w=[tsm])
        for d_ in range(2):
            bk = nbank((4, 2))
            mm(bank(bk)[:, 0:256].rearrange("p (a b) -> p a b", b=16), tri[:, d_, :], la[:, :, d_ * 16:(d_ + 1) * 16], True, True, r=[tsm, TC], w=[PB[bk]])
            dve("tensor_copy", cum[:, :, d_ * 16:(d_ + 1) * 16], bank(bk)[:, 0:256].rearrange("p (a b) -> p a b", b=16), r=[PB[bk]], w=[tsm])
        act(ecum[:], cum[:], AF.Exp, r=[tsm], w=[tsm])
        negc = la
        totr = bs_.alloc((16, 32), F32)
        dtw = bs_.alloc((16, 32), F32)
        etot = bs_.alloc((16, 32), F32)
        selt = bs_.alloc((2, 128), F32)
        S.dma("sp", selt, sel_d, w=[tsm])
        for d_ in range(2):
            bk = nbank((4, 2))
            mm(bank(bk)[:, 0:256].rearrange("p (a b) -> p a b", b=16), selt[:, d_, :], cum[:, :, d_ * 16:(d_ + 1) * 16], True, True, r=[tsm], w=[PB[bk]])
            dve("tensor_copy", totr[:, :, d_ * 16:(d_ + 1) * 16], bank(bk)[:, 0:256].rearrange("p (a b) -> p a b", b=16), r=[PB[bk]], w=[tsm])
        dve("tensor_tensor", dtw[:], totr[:], cum[:], ALU.subtract, r=[tsm], w=[tsm])
        act(dtw[:], dtw[:], AF.Exp, r=[tsm], w=[tsm])
        dve("tensor_tensor", dtw[:], dtw[:], dt[:], ALU.mult, r=[tsm], w=[tsm])
        act(etot[:], totr[:], AF.Exp, r=[tsm], w=[tsm])
        dve("tensor_scalar", negc[:], cum[:], -1.0, None, ALU.mult, r=[tsm], w=[tsm])
        S.barrier()
        if EVEN_STOP < 4:
            return
        bt1 = Bump(arena, 0, K48)
        bt = Bump(arena, K104, ARENA)
        Dm = bt1.alloc((16, 128), BF16)
        tdm = Tok()
        dve("tensor_tensor", Dm[:], identb[:].unsqueeze(1).to_broadcast([128, 16, 128]), dsk[:].unsqueeze(2).to_broadcast([128, 16, 128]), ALU.mult,
            r=[TC, tsm], w=[tdm])
        XT1 = bt1.alloc((1024,), BF16)
        txt1 = Tok()
        BT1 = bt1.alloc((256,), BF16)
        tbt1 = Tok()
        xdtF = bt1.alloc((1024,), BF16)
        xdtB = bt1.alloc((1024,), BF16)
        xw = bt1.alloc((1024,), BF16)
        txdF, txdB, txw = Tok(), Tok(), Tok()
        CBm = bt1.alloc((2, 2, 128), F32)
        tcb = [Tok(), Tok()]
        Eb = [bt1.alloc((4, 128), F32) for _ in range(2)]
        teb = [Tok() for _ in range(2)]
        MT = [bt1.alloc((16, 128), BF16) for _ in range(2)]
        tmt = [[Tok() for _ in range(4)] for _ in range(2)]
        Hf = bt1.alloc((1024,), F32)
        Hb = bt1.alloc((1024,), BF16)
        thf, thb = Tok(), Tok()
        hinl = bt1.alloc((1024,), BF16)
        thin = Tok()
        t1 = bt1.alloc((1024,), F32)
        yv = bt1.alloc((1024,), F32)
        tt1, tyv = Tok(), Tok()
        zl = bt.alloc((1024,), BF16)
        tzl = Tok()
        ynb = bt.alloc((1024,), BF16)
        tynb = Tok()
        junk = xw
        ssq = bt.alloc((1,), F32)
        tss = Tok()
        thd = [Tok() for _ in range(16)]
        pb7 = bank(7).bitcast(BF16)
        pb6 = bank(6).bitcast(BF16)
        h3 = lambda v: v.rearrange("p (h c) -> p h c", c=64)

        def make_xt(sc):
            csl = slice(sc * 128, (sc + 1) * 128)
            for c in range(8):
                tr(pb7[:, c * 128:(c + 1) * 128], xbc[:, c, csl], identb[:], r=[txb[sc], TC], w=[PB[7]])
            dve("tensor_copy", XT1[:], pb7[:, :], r=[PB[7]], w=[txt1])
            for g in range(2):
                tr(pb6[:, g * 128:(g + 1) * 128], xbc[:, 8 + g, csl], identb[:], r=[txb[sc], TC], w=[PB[6]])
            dve("tensor_copy", BT1[:], pb6[:, 0:256], r=[PB[6]], w=[tbt1])

        def state_update(sc, d_, first, split=False):
            S.op("pool", lambda e: e.tensor_tensor(h3(xw[:]), h3(XT1[:]), dtw[:, sc, d_ * 16:(d_ + 1) * 16].unsqueeze(2).to_broadcast([128, 16, 64]), ALU.mult),
                 r=[txt1, tsm], w=[txw])
            bks = []
            for g in range(2):
                bk = (2 + g) if split else nbank((4, 2))
                bks.append(bk)
                mm(bank(bk), BT1[:, g * 128:(g + 1) * 128], xw[:, g * 512:(g + 1) * 512], True, True, r=[tbt1, txw], w=[PB[bk]])
            if not first:
                S.op("pool", lambda e: e.tensor_tensor(h3(Hf[:]), h3(Hf[:]), etot[:, sc, d_ * 16:(d_ + 1) * 16].unsqueeze(2).to_broadcast([128, 16, 64]), ALU.mult),
                     r=[thf, tsm], w=[thf])
            def fin():
                for g in range(2):
                    gs = slice(g * 512, (g + 1) * 512)
                    if first:
                        dve("tensor_copy", Hf[:, gs], bank(bks[g]), r=[PB[bks[g]]], w=[thf])
                    else:
                        dve("tensor_tensor", Hf[:, gs], Hf[:, gs], bank(bks[g]), ALU.add, r=[thf, PB[bks[g]]], w=[thf])
                act(Hb[:], Hf[:], AF.Copy, r=[thf], w=[thb])
            if split:
                return fin
            fin()

        for sc in range(15):
            make_xt(sc)
            state_update(sc, 0, sc == 0)
            S.dma("sp", hin_d[sc + 1], Hb[:], r=[thb], w=[thd[sc + 1]])
        if EVEN_STOP == 41:
            return
        t1s = [t1, bt.alloc((1024,), F32)]
        t1Bs = [bt1.alloc((1024,), F32), bt.alloc((1024,), F32)]
        zls = [zl, bt.alloc((1024,), BF16)]
        tt1s, tt1bs, tzls = [Tok(), Tok()], [Tok(), Tok()], [Tok(), Tok()]
        lnt = bt.alloc((1,), F32)

        def front_a(sc):
            csl = slice(sc * 128, (sc + 1) * 128)
            hasF, hasB = sc >= 1, sc <= 14
            par = sc % 2
            t1, t1B, zl, tt1, tt1b, tzl = t1s[par], t1Bs[par], zls[par], tt1s[par], tt1bs[par], tzls[par]
            if hasF:
                S.dma("sp", hinl[:], hin_d[sc], r=[thd[sc]], w=[thin])
            S.dma("sp", zl[:], zs_d[csl, :], r=[tzs[sc]], w=[tzl])
            bkc = nbank((4, 2))
            for g in range(2):
                mm(bank(bkc)[:, g * 128:(g + 1) * 128], xbc[:, 8 + g, csl], xbc[:, 10 + g, csl], True, True, r=[txb[sc]], w=[PB[bkc]])
            for d_ in range(2):
                dve("tensor_tensor", CBm[:, d_], bank(bkc)[:, 0:256].rearrange("p (g l) -> p g l", g=2),
                    tri[:, d_, :].unsqueeze(1).to_broadcast([128, 2, 128]), ALU.mult, r=[PB[bkc], TC], w=[tcb[d_]])
            ie = [0]

            def seg_group(d_, hb):
                g = hb // 2
                bk = nbank((4, 2))
                for i in range(4):
                    h = hb * 4 + i
                    tr(bank(bk)[:, i * 128:(i + 1) * 128], cum[:, sc, d_ * 16 + h:d_ * 16 + h + 1].to_broadcast([128, 128]), identf[:],
                       r=[tsm, TC], w=[PB[bk]])
                k = ie[0] % 2
                ie[0] += 1
                for i in range(4):
                    h = hb * 4 + i
                    S.op("act", lambda e, k=k, i=i, bk=bk, h=h: e.activation(out=Eb[k][:, i, :], in_=bank(bk)[:, i * 128:(i + 1) * 128], func=AF.Exp,
                                                                     bias=negc[:, sc, d_ * 16 + h:d_ * 16 + h + 1]),
                         r=[PB[bk], tsm], w=[teb[k]], nowaw=True)
                dve("scalar_tensor_tensor", MT[d_][:, hb * 4:(hb + 1) * 4, :], Eb[k][:], 1.0, CBm[:, d_, g, :].unsqueeze(1).to_broadcast([128, 4, 128]),
                    ALU.min, ALU.mult, r=[teb[k], tcb[d_]], w=[tmt[d_][hb]])

            groups = [(d_, hb) for d_ in range(2) for hb in range(4)]
            seg_group(*groups[0])
            seg_group(*groups[1])
            make_xt(sc)
            S.op("pool", lambda e: e.tensor_tensor(h3(xdtF[:]), h3(XT1[:]), dt[:, sc, 0:16].unsqueeze(2).to_broadcast([128, 16, 64]), ALU.mult),
                 r=[txt1, tsm], w=[txdF])
            S.op("pool", lambda e: e.tensor_tensor(h3(xdtB[:]), h3(XT1[:]), dt[:, sc, 16:32].unsqueeze(2).to_broadcast([128, 16, 64]), ALU.mult),
                 r=[txt1, tsm], w=[txdB])
            seg_group(*groups[2])
            seg_group(*groups[3])
            for g in range(2):
                gs = slice(g * 512, (g + 1) * 512)
                if hasF:
                    mm(bank(2), xbc[:, 10 + g, csl], hinl[:, gs], True, True, r=[txb[sc], thin], w=[PB[2]])
                    dve("tensor_tensor", h3(t1[:, gs]), h3(bank(2)), ecum[:, sc, g * 8:(g + 1) * 8].unsqueeze(2).to_broadcast([128, 8, 64]), ALU.mult,
                        r=[PB[2], tsm], w=[tt1], nowaw=True)
                if hasB:
                    mm(bank(3), xbc[:, 10 + g, csl], Hb[:, gs], True, True, r=[txb[sc], thb], w=[PB[3]])
                    dve("tensor_tensor", h3(t1B[:, gs]), h3(bank(3)), ecum[:, sc, 16 + g * 8:16 + (g + 1) * 8].unsqueeze(2).to_broadcast([128, 8, 64]), ALU.mult,
                        r=[PB[3], tsm], w=[tt1b], nowaw=True)
            seg_group(*groups[4])
            fin = None
            if sc >= 1:
                fin = state_update(sc, 1, sc == 15, split=True)
            seg_group(*groups[5])
            seg_group(*groups[6])
            seg_group(*groups[7])
            if fin is not None:
                fin()

        def front_b(sc):
            for h in range(16):
                hs = slice(h * 64, (h + 1) * 64)
                mm(ps[:, 0:1024][:, hs], MT[0][:, h, :], xdtF[:, hs], True, False, r=[tmt[0][h // 4], txdF], w=[PB[h // 8]])
                mm(ps[:, 0:1024][:, hs], MT[1][:, h, :], xdtB[:, hs], False, False, r=[tmt[1][h // 4], txdB], w=[PB[h // 8]])
                mm(ps[:, 0:1024][:, hs], Dm[:, h, :], XT1[:, hs], False, True, r=[tdm, txt1], w=[PB[h // 8]])

        def back(sc):
            csl = slice(sc * 128, (sc + 1) * 128)
            hasF, hasB = sc >= 1, sc <= 14
            par = sc % 2
            t1, t1B, zl, tt1, tt1b, tzl = t1s[par], t1Bs[par], zls[par], tt1s[par], tt1bs[par], tzls[par]
            for g in range(2):
                gs = slice(g * 512, (g + 1) * 512)
                if hasF:
                    dve("tensor_tensor", yv[:, gs], t1[:, gs], bank(g), ALU.add, r=[tt1, PB[g]], w=[tyv], nowaw=True)
                else:
                    dve("tensor_copy", yv[:, gs], bank(g), r=[PB[g]], w=[tyv], nowaw=True)
            if hasB:
                S.op("pool", lambda e: e.tensor_tensor(yv[:], yv[:], t1B[:], ALU.add), r=[tt1b, tyv], w=[tyv])
            S.op("pool", lambda e: e.tensor_tensor(yv[:], yv[:], zl[:], ALU.mult), r=[tyv, tzl], w=[tyv])
            act(junk[:], yv[:], AF.Square, r=[tyv], w=[tss, txw], accum_out=ssq[:])
            act(lnt[:], ssq[:], AF.Ln, r=[tss, TC], w=[tss], scale=1.0 / 1024, bias=epst[:])
            act(ssq[:], lnt[:], AF.Exp, r=[tss], w=[tss], scale=-0.5)
            dve("tensor_scalar", ynb[:], yv[:], ssq[:, 0:1], 0.0, ALU.mult, ALU.add, r=[tyv, tss], w=[tynb])
            for c in range(8):
                tr(pb7[:, c * 128:(c + 1) * 128], ynb[:, c * 128:(c + 1) * 128], identb[:], r=[tynb, TC], w=[PB[7]])
            dve("tensor_tensor", xbc[:, 0:8, csl], pb7[:, :].rearrange("p (c l) -> p c l", c=8),
                gains[:, (G_SSD + e) * 8:(G_SSD + e) * 8 + 8].unsqueeze(2).to_broadcast([128, 8, 128]), ALU.mult,
                r=[PB[7], TC], w=[txb[sc]])

        front_a(15)
        front_b(15)
        for sc in range(14, -1, -1):
            front_a(sc)
            back(sc + 1)
            front_b(sc)
        back(0)
        S.barrier()
        if EVEN_STOP < 5:
            return
        OT = Bump(arena, 0, K16).alloc((4, SEQ), BF16)
        tot = [[Tok() for _ in range(4)] for _ in range(4)]
        S.dma("sp", OT[:].rearrange("p a b -> p (a b)"), ot_d, r=[tots], w=[t for row in tot for t in row])
        ring = Ring(Bump(arena, K96, ARENA), 3, 6144)
        proj_accum(None, wout_d, 12, None, lambda j, tt: (txb[tt * 4:tt * 4 + 4] if j < 8 else [tot[j - 8][tt]]), 1.0, ring, ybanks=(0, 4),
                   act_view=lambda j, tt: (xbc[:, j, tsl(tt)] if j < 8 else OT[:, j - 8, tsl(tt)]))

    if plan is None:
        plan = []
        for layer in range(DEPTH):
            plan += [("ffn1", layer), ("mix", layer), ("xa", layer), ("ffn2", layer)]
    for s in range(nseq):
        load_x(s)
        if any(p[0] == "xa" for p in plan):
            prep_mem(s)
        if any(p[0] == "mix" and p[1] % 2 == 0 for p in plan):
            rope_tables(s)
        for kind, layer in plan:
            if kind == "ffn1":
                ffn(dr["ffn1_w_gu"][layer], dr["ffn1_w_down"][layer], G_FFN1 + layer)
            elif kind == "ffn2":
                ffn(dr["ffn2_w_gu"][layer], dr["ffn2_w_down"][layer], G_FFN2 + layer)
            elif kind == "xa":
                xattn(layer)
            elif kind == "mix":
                if layer % 2 == 1:
                    fnet(layer)
                else:
                    even_mixer(layer // 2, layer)
        final(s)
    S.barrier()
    S.emit()
    return nc


def _fm(v):
    return np.ascontiguousarray(np.asarray(v, np.float32).reshape(-1, 128).T)


def host_consts(inp):
    c = {}
    g = np.zeros((128, NGAIN * 8), np.float32)

    def put(idx, v):
        g[:, idx * 8:(idx + 1) * 8] = _fm(v)

    for l in range(4):
        put(G_FFN1 + l, inp["ffn1_norm"][l])
        put(G_MIX + l, inp["mix_norm"][l])
        put(G_XA + l, inp["xa_norm"][l])
        put(G_FFN2 + l, inp["ffn2_norm"][l])
    put(G_FINAL, inp["final_norm"])
    put(G_MEM, inp["mem_norm"])
    for e in range(2):
        put(G_SSD + e, inp["ssd_norm"][e])
    c["gains"] = g
    q = np.zeros((128, 2, 6), np.float32)
    for e in range(2):
        q[:, e, 0:4] = _fm(inp["q_norm"][e])
        q[:, e, 4:6] = _fm(inp["kv_norm"][e])
    c["qkvgains"] = q
    cw = np.asarray(inp["conv_w"], np.float32)
    c["convw"] = np.ascontiguousarray(cw.reshape(2, 5, 12, 128).transpose(3, 0, 2, 1))
    cb = np.asarray(inp["conv_b"], np.float32)
    c["convb"] = np.ascontiguousarray(cb.reshape(2, 12, 128).transpose(2, 0, 1))
    c["dt_bias"] = np.ascontiguousarray(np.asarray(inp["dt_bias"], np.float32).reshape(2, 32))
    c["a_log"] = np.ascontiguousarray(np.asarray(inp["a_log"], np.float32).reshape(2, 32))
    c["ssd_d"] = np.ascontiguousarray(np.asarray(inp["ssd_d"], np.float32))
    c["ident"] = np.eye(128, dtype=np.float32)
    sel = np.zeros((128, 2, 128), np.float32)
    sel[127, 0, :] = 1.0
    sel[0, 1, :] = 1.0
    c["sel"] = sel
    s_ = np.arange(128)
    tri = np.zeros((128, 2, 128), np.float32)
    tri[:, 0, :] = (s_[:, None] <= s_[None, :])
    tri[:, 1, :] = (s_[:, None] >= s_[None, :])
    c["tri"] = tri
    inv = 1.0 / (10000.0 ** (np.arange(0, 32, 2, dtype=np.float32) / 32.0))
    c["invf"] = np.concatenate([inv, inv]).astype(np.float32).reshape(32, 1)
    c["shiftm"] = np.ascontiguousarray(np.roll(np.eye(128, dtype=np.float32), 64, axis=1))
    j = np.arange(256)
    angc = 2 * np.pi * ((j[:, None] * j[None, :]) % 256) / 256.0
    cd = np.concatenate([np.cos(angc), np.sin(angc)], axis=1)
    c["cdft"] = np.ascontiguousarray(cd.reshape(2, 128, 512).transpose(1, 0, 2)).astype(ml_dtypes.bfloat16)
    k = np.arange(SEQ)
    angs = 2 * np.pi * ((k[:, None] * k[None, :]) % SEQ) / float(SEQ)
    sd = np.stack([np.cos(angs), -np.sin(angs)], axis=1)
    c["sdft"] = np.ascontiguousarray(sd).astype(ml_dtypes.bfloat16)
    return c


_CACHE = {}


def kernel(**inputs):
    inp = {k: np.asarray(v) for k, v in inputs.items()}
    if "nc" not in _CACHE:
        _CACHE["nc"] = build_program()
    nc = _CACHE["nc"]
    consts = host_consts(inp)
    shared = {name: np.ascontiguousarray(inp[name], dtype=np.float32) for name, _ in WEIGHT_SPECS}
    shared.update(consts)
    in_maps = []
    for c in range(NCORES):
        m = dict(shared)
        m["x"] = np.ascontiguousarray(inp["x"][c * NSEQ:(c + 1) * NSEQ], dtype=np.float32)
        m["mem"] = np.ascontiguousarray(inp["mem"][c * NSEQ:(c + 1) * NSEQ], dtype=np.float32)
        m["positions"] = np.ascontiguousarray(inp["positions"][c * NSEQ:(c + 1) * NSEQ], dtype=np.int32)
        in_maps.append(m)
    res = run_bass_kernel_spmd(nc, in_maps, core_ids=list(range(NCORES)))
    return np.concatenate([np.asarray(r["out"], np.float32) for r in res.results], axis=0)
```
